# Optimizing a Trainium2 kernel written in Bass

```python
import jax, jax.numpy as jnp
from jax import lax
import numpy as np

D_MODEL = 1024
BATCH = 8
SEQ = 2048
DEPTH = 2

GRID_W = 64
MIX_WIDTH = D_MODEL
HG_WIDTH = MIX_WIDTH // 2
HG_HEAD_DIM = 128
HG_HEADS = HG_WIDTH // HG_HEAD_DIM
HG_CHUNK = 64
ATT_WIDTH = MIX_WIDTH - HG_WIDTH
ATT_HEAD_DIM = 64
ATT_HEADS = ATT_WIDTH // ATT_HEAD_DIM
ATT_KV_HEADS = 2
ATT_GROUP = ATT_HEADS // ATT_KV_HEADS
KV_WIDTH = ATT_KV_HEADS * ATT_HEAD_DIM
Q_BLOCK = 128
ROPE_THETA = 10000.0
D_FF = 4 * D_MODEL
N_MOD = 6
EPS = 1e-6
IN_SIZES = [HG_WIDTH] * 5 + [ATT_WIDTH, KV_WIDTH, KV_WIDTH]
IN_COLS = sum(IN_SIZES)
IN_SPLITS = [int(s) for s in np.cumsum(IN_SIZES)[:-1]]

kernel_name = 'hybrid_hgrn2_axial_gqa_encoder'

F32 = jnp.float32


def rms_norm(x, gain):
    xf = x.astype(F32)
    y = xf * lax.rsqrt(jnp.mean(xf * xf, axis=-1, keepdims=True) + EPS)
    return (y * gain.astype(F32)).astype(x.dtype)


def hgrn_lower_bounds(lb_logits):
    p = jnp.cumsum(jax.nn.softmax(lb_logits.astype(F32), axis=0), axis=0)
    return p - p[0:1]


def chunked_gated_recurrence(q, k, v, log_f):
    N, S, H, dk = q.shape
    dv = v.shape[-1]
    nc = S // HG_CHUNK

    def to_chunks(a):
        return a.reshape(N, nc, HG_CHUNK, H, a.shape[-1]).transpose(1, 0, 3, 2, 4)

    xs = (to_chunks(q), to_chunks(k), to_chunks(v), to_chunks(log_f))
    lower = jnp.tril(jnp.ones((HG_CHUNK, HG_CHUNK), dtype=bool))[:, :, None]

    def step(state, inp):
        qi, ki, vi, gi = inp
        b = jnp.cumsum(gi, axis=-2)
        b_last = b[..., -1:, :]
        o_inter = jnp.einsum('nhtd,nhde->nhte', qi * jnp.exp(b), state)
        diff = b[..., :, None, :] - b[..., None, :, :]
        decay = jnp.exp(jnp.where(lower, diff, -jnp.inf))
        scores = jnp.einsum('nhtd,nhsd,nhtsd->nhts', qi, ki, decay)
        o_intra = jnp.einsum('nhts,nhse->nhte', scores, vi)
        k_dec = ki * jnp.exp(b_last - b)
        new_state = state * jnp.exp(b_last)[..., 0, :, None] + jnp.einsum('nhsd,nhse->nhde', k_dec, vi)
        return new_state, o_inter + o_intra

    state0 = jnp.zeros((N, H, dk, dv), F32)
    _, o = lax.scan(step, state0, xs)
    return o.transpose(1, 0, 3, 2, 4).reshape(N, S, H, dv)


def hgrn2_group(q, f_fwd, f_bwd, i, g, lb, out_gain):
    B, S, _ = q.shape

    def heads(a):
        return a.astype(F32).reshape(B, S, HG_HEADS, HG_HEAD_DIM)

    qh = jax.nn.silu(heads(q))
    ih = heads(i)
    lbh = lb.reshape(HG_HEADS, HG_HEAD_DIM)

    def gates(fz):
        z = heads(fz)
        log_f = jnp.logaddexp(jnp.log(lbh), jnp.log1p(-lbh) + jax.nn.log_sigmoid(z))
        return log_f, (1.0 - lbh) * jax.nn.sigmoid(-z)

    lf_fw, k_fw = gates(f_fwd)
    lf_bw, k_bw = gates(f_bwd)
    flip = lambda a: a[:, ::-1]
    both = lambda a, b: jnp.concatenate([a, flip(b)], axis=0)
    o = chunked_gated_recurrence(both(qh, qh), both(k_fw, k_bw), both(ih, ih), both(lf_fw, lf_bw))
    o = o[:B] + flip(o[B:])
    o = rms_norm(o, out_gain) * jax.nn.silu(heads(g))
    return o.reshape(B, S, HG_WIDTH).astype(q.dtype)


def axial_angles(seq_len):
    rows = seq_len // GRID_W
    row = jnp.repeat(jnp.arange(rows, dtype=F32), GRID_W)
    col = jnp.tile(jnp.arange(GRID_W, dtype=F32), rows)
    half = ATT_HEAD_DIM // 2
    inv_freq = 1.0 / (ROPE_THETA ** (jnp.arange(0, half, 2, dtype=F32) / half))
    return row[:, None, None] * inv_freq, col[:, None, None] * inv_freq


def rotate(x, ang):
    x1, x2 = jnp.split(x, 2, axis=-1)
    cos, sin = jnp.cos(ang), jnp.sin(ang)
    return jnp.concatenate([x1 * cos - x2 * sin, x2 * cos + x1 * sin], axis=-1)


def apply_axial_rope(x, ang_row, ang_col):
    xr, xc = jnp.split(x.astype(F32), 2, axis=-1)
    return jnp.concatenate([rotate(xr, ang_row), rotate(xc, ang_col)], axis=-1).astype(x.dtype)


def gqa_axial_group(q, k, v, q_gain, k_gain, out_gain):
    B, S, _ = q.shape
    qh = rms_norm(q.reshape(B, S, ATT_HEADS, ATT_HEAD_DIM), q_gain)
    kh = rms_norm(k.reshape(B, S, ATT_KV_HEADS, ATT_HEAD_DIM), k_gain)
    vh = v.reshape(B, S, ATT_KV_HEADS, ATT_HEAD_DIM)
    ang_r, ang_c = axial_angles(S)
    qh = apply_axial_rope(qh, ang_r, ang_c)
    kh = apply_axial_rope(kh, ang_r, ang_c)
    scale = ATT_HEAD_DIM ** -0.5
    nb = S // Q_BLOCK
    qb = qh.reshape(B, nb, Q_BLOCK, ATT_KV_HEADS, ATT_GROUP, ATT_HEAD_DIM).transpose(1, 0, 3, 4, 2, 5)
    kt = kh.transpose(0, 2, 1, 3)
    vt = vh.transpose(0, 2, 1, 3)

    def block(qi):
        s = jnp.einsum('bkgqd,bksd->bkgqs', qi, kt).astype(F32) * scale
        p = jax.nn.softmax(s, axis=-1).astype(vt.dtype)
        return jnp.einsum('bkgqs,bksd->bkgqd', p, vt)

    o = lax.map(block, qb)
    o = o.transpose(1, 0, 4, 2, 3, 5).reshape(B, S, ATT_HEADS, ATT_HEAD_DIM)
    o = rms_norm(o, out_gain)
    return o.reshape(B, S, ATT_WIDTH)


def hybrid_layer(x, mod, lb, pre_mix, post_mix, w_in, w_out, hg_out_gain, q_gain, k_gain,
                 att_out_gain, pre_ff, post_ff, w_ff1, w_ff2):
    shift1, scale1, gate1, shift2, scale2, gate2 = [m[:, None, :] for m in jnp.split(mod, N_MOD, axis=-1)]
    h = rms_norm(x, pre_mix) * (1.0 + scale1) + shift1
    z = h @ w_in
    hq, hf_fw, hf_bw, hi, hg, aq, ak, av = jnp.split(z, IN_SPLITS, axis=-1)
    y_hgrn = hgrn2_group(hq, hf_fw, hf_bw, hi, hg, lb, hg_out_gain)
    y_att = gqa_axial_group(aq, ak, av, q_gain, k_gain, att_out_gain)
    y = jnp.concatenate([y_hgrn, y_att.astype(y_hgrn.dtype)], axis=-1) @ w_out
    x = x + gate1 * rms_norm(y, post_mix)
    h = rms_norm(x, pre_ff) * (1.0 + scale2) + shift2
    y = jnp.square(jax.nn.relu(h @ w_ff1)) @ w_ff2
    return x + gate2 * rms_norm(y, post_ff)


def setup_inputs(seed: int = 0) -> dict:
    key = jax.random.key(seed)
    ks = jax.random.split(key, 20)
    nrm = lambda k, shape, s: jax.random.normal(k, shape, F32) * s
    gain = lambda k, shape: 1.0 + 0.05 * jax.random.normal(k, shape, F32)
    return {
        'x': nrm(ks[0], (BATCH, SEQ, D_MODEL), 1.0),
        'c': nrm(ks[1], (BATCH, D_MODEL), 1.0),
        'w_ada': nrm(ks[2], (D_MODEL, N_MOD * D_MODEL), D_MODEL ** -0.5),
        'ada_layer_bias': nrm(ks[3], (DEPTH, N_MOD * D_MODEL), 0.1),
        'hg_lb_logits': nrm(ks[4], (DEPTH, HG_WIDTH), 1.0),
        'pre_mix': gain(ks[5], (DEPTH, D_MODEL)),
        'post_mix': gain(ks[6], (DEPTH, D_MODEL)),
        'w_in': nrm(ks[7], (DEPTH, D_MODEL, IN_COLS), D_MODEL ** -0.5),
        'w_out': nrm(ks[8], (DEPTH, MIX_WIDTH, D_MODEL), MIX_WIDTH ** -0.5),
        'hg_out_gain': gain(ks[9], (DEPTH, HG_HEAD_DIM)),
        'q_gain': gain(ks[10], (DEPTH, ATT_HEAD_DIM)),
        'k_gain': gain(ks[11], (DEPTH, ATT_HEAD_DIM)),
        'att_out_gain': gain(ks[12], (DEPTH, ATT_HEAD_DIM)),
        'pre_ff': gain(ks[13], (DEPTH, D_MODEL)),
        'post_ff': gain(ks[14], (DEPTH, D_MODEL)),
        'w_ff1': nrm(ks[15], (DEPTH, D_MODEL, D_FF), D_MODEL ** -0.5),
        'w_ff2': nrm(ks[16], (DEPTH, D_FF, D_MODEL), D_FF ** -0.5),
    }


def reference(x, c, w_ada, ada_layer_bias, hg_lb_logits, pre_mix, post_mix, w_in, w_out,
              hg_out_gain, q_gain, k_gain, att_out_gain, pre_ff, post_ff, w_ff1, w_ff2):
    lb_table = hgrn_lower_bounds(hg_lb_logits)
    cond = jax.nn.silu(c) @ w_ada
    for l in range(DEPTH):
        mod = cond + ada_layer_bias[l]
        x = hybrid_layer(x, mod, lb_table[l], pre_mix[l], post_mix[l], w_in[l], w_out[l],
                         hg_out_gain[l], q_gain[l], k_gain[l], att_out_gain[l],
                         pre_ff[l], post_ff[l], w_ff1[l], w_ff2[l])
    return x
```

```python
import contextlib
import numpy as np
import concourse.bass as bass
import concourse.mybir as mybir
from concourse.bass_utils import run_bass_kernel_spmd

F32 = mybir.dt.float32
BF16 = mybir.dt.bfloat16
AF = mybir.ActivationFunctionType
ALU = mybir.AluOpType

D = 1024
S = 2048
NL = 2
NT = 4
EPS = 1e-6
NCORES = 8
ENGS = ("pe", "act", "dve", "pool", "sp")


class Op:
    __slots__ = ("eng", "fn", "deps", "signal", "sigval", "dma_sem", "dma_val", "idx")

    def __init__(self, eng, fn):
        self.eng = eng
        self.fn = fn
        self.deps = []
        self.signal = False
        self.sigval = 0
        self.dma_sem = None
        self.dma_val = 0


class Prog:
    def __init__(self):
        self.ops = []
        self.recs = {"sb": [], "dr": [], "ps": [[b * 2048, (b + 1) * 2048, None, []] for b in range(8)]}
        self.dma_counts = {}

    @staticmethod
    def _bank(res):
        sp, lo, hi = res
        if sp != "ps":
            return res
        return (sp, lo // 2048 * 2048, (hi + 2047) // 2048 * 2048)

    def _deps_for(self, op, reads, writes):
        deps = set()
        for res in reads:
            sp, lo, hi = self._bank(res)
            for rec in self.recs[sp]:
                if rec[0] < hi and lo < rec[1]:
                    if rec[2] is not None:
                        deps.add(rec[2])
                    if sp == "ps":
                        for rd in rec[3]:
                            if rd.eng != op.eng:
                                deps.add(rd)
                    rec[3].append(op)
        for res in writes:
            sp, lo, hi = self._bank(res)
            new = []
            for rec in self.recs[sp]:
                if rec[0] < hi and lo < rec[1]:
                    if rec[2] is not None:
                        deps.add(rec[2])
                    deps.update(rec[3])
                    if rec[0] < lo:
                        new.append([rec[0], lo, rec[2], list(rec[3])])
                    if hi < rec[1]:
                        new.append([hi, rec[1], rec[2], list(rec[3])])
                else:
                    new.append(rec)
            new.append([lo, hi, op, []])
            self.recs[sp] = new
        deps.discard(op)
        return deps

    def _add(self, op, reads, writes):
        latest = {}
        for d in self._deps_for(op, reads, writes):
            if d.dma_sem is None:
                if d.eng == op.eng and op.dma_sem is None and d.eng == "pe":
                    continue
                cur = latest.get(d.eng)
                if cur is None or d.idx > cur.idx:
                    latest[d.eng] = d
            else:
                op.deps.append(d)
        for d in latest.values():
            d.signal = True
            op.deps.append(d)
        op.idx = len(self.ops)
        self.ops.append(op)
        return op

    def op(self, eng, fn, reads=(), writes=()):
        return self._add(Op(eng, fn), reads, writes)

    def dma_group(self, eng, key, fns, reads=(), writes=()):
        ops = []
        for i, fn in enumerate(fns):
            o = Op(eng, fn)
            o.dma_sem = key
            self.dma_counts[key] = self.dma_counts.get(key, 0) + 1
            self._add(o, reads, writes if i == 0 else ())
            ops.append(o)
        final = 16 * self.dma_counts[key]
        for o in ops:
            o.dma_val = final
        if len(ops) > 1:
            pass
        return ops

    def emit(self, nc, final_wait_keys=()):
        counts = {e: 0 for e in ENGS}
        for o in self.ops:
            if o.dma_sem is None and o.signal:
                counts[o.eng] += 1
                o.sigval = counts[o.eng]
        with contextlib.ExitStack() as st:
            esem = {e: st.enter_context(nc.semaphore("s_" + e)) for e in ENGS}
            dsem = {k: st.enter_context(nc.semaphore("d_%d" % i))
                    for i, k in enumerate(self.dma_counts)}
            block = st.enter_context(nc.Block())
            per = {e: [o for o in self.ops if o.eng == e] for e in ENGS}

            def run(e, h, final=False):
                seen = {}
                for o in per[e]:
                    for d in o.deps:
                        if d.dma_sem is not None:
                            s, v = dsem[d.dma_sem], d.dma_val
                        else:
                            s, v = esem[d.eng], d.sigval
                        key = id(s)
                        if seen.get(key, 0) < v:
                            h.wait_ge(s, v)
                            seen[key] = v
                    ins = o.fn(h)
                    if o.dma_sem is not None:
                        ins.then_inc(dsem[o.dma_sem], 16)
                    elif o.signal:
                        ins.then_inc(esem[e], 1)
                if final:
                    for k in final_wait_keys:
                        h.wait_ge(dsem[k], 16 * self.dma_counts[k])

            @block.tensor
            def _(h):
                run("pe", h)

            @block.scalar
            def _(h):
                run("act", h)

            @block.vector
            def _(h):
                run("dve", h)

            @block.gpsimd
            def _(h):
                run("pool", h)

            @block.sync
            def _(h):
                run("sp", h, final=True)
        return counts


class T:
    def __init__(self, ap, space, lo, esz, nelem):
        self.ap = ap
        self.space = space
        self.lo = lo
        self.esz = esz
        self.n = nelem

    def r(self, off=0, n=None):
        if n is None:
            n = self.n - off
        return (self.space, self.lo + off * self.esz, self.lo + (off + n) * self.esz)

    def __getitem__(self, idx):
        return self.ap[idx]


def MM(out, lhsT, rhs, start=True, stop=True):
    return lambda h: h.matmul(out, lhsT=lhsT, rhs=rhs, start=start, stop=stop)


def TR(out, in_, ident):
    return lambda h: h.transpose(out=out, in_=in_, identity=ident)


def ACT(out, in_, func, **kw):
    return lambda h: h.activation(out=out, in_=in_, func=func, **kw)


def TT(out, a, b, op):
    return lambda h: h.tensor_tensor(out=out, in0=a, in1=b, op=op)


def TS(out, a, s1, s2, op0, op1=None):
    if op1 is None:
        return lambda h: h.tensor_scalar(out=out, in0=a, scalar1=s1, scalar2=None, op0=op0)
    return lambda h: h.tensor_scalar(out=out, in0=a, scalar1=s1, scalar2=s2, op0=op0, op1=op1)


def STT(out, in0, scalar, in1, op0, op1):
    return lambda h: h.scalar_tensor_tensor(out=out, in0=in0, scalar=scalar, in1=in1, op0=op0, op1=op1)


def CP(out, in_):
    return lambda h: h.tensor_copy(out=out, in_=in_)


def MS(out, val):
    return lambda h: h.memset(out, val)


def DMA(out, in_, **kw):
    return lambda h: h.dma_start(out=out, in_=in_, **kw)


C_ADAB = 0
C_PREMIX = 48
C_POSTMIX = 56
C_PREFF = 64
C_POSTFF = 72
C_LBLOG = 80
C_HGG = 84
C_QG = 85
C_KG = 86
C_AOG = 87
C_PER_LAYER = 88
NCOLS = NL * C_PER_LAYER
CB_IDENT = 0
CB_PROT = 128
CB_MASKF = 256
CB_MASKB = 320
CB_SCAN = 384
NCB = 384 + 1024

ARENA_BYTES = 212480


class _Stop(Exception):
    pass


def build(nc, stop=None, dbg=False):
    P = Prog()
    st = contextlib.ExitStack()

    def chk(name):
        if stop == name:
            raise _Stop()

    def dram(name, shape, kind="ExternalInput"):
        return nc.dram_tensor(name, list(shape), F32, kind=kind).ap()

    xT_d = dram("xT", [128, 8, S])
    cT_d = dram("cT", [128, 8])
    cols_d = dram("cols", [128, NCOLS])
    cbf_d = dram("cbf", [128, NCB])
    cos_d = dram("cosT", [128, S])
    sin_d = dram("sinT", [128, S])
    wada_d = dram("wada", [48, 128, 1024])
    win_d = dram("win", [NL, 27, 128, 1024])
    wout_d = dram("wout", [NL, 8, 128, 1024])
    w1_d = dram("w1", [NL, 8, 128, 4096])
    w2_d = dram("w2", [NL, 8, 128, 4096])
    out_d = dram("outT", [128, 8, S], kind="ExternalOutput")
    w1s = nc.dram_tensor("w1s", [NL, 8, 128, 4096], BF16).ap()
    w2s = nc.dram_tensor("w2s", [NL, 8, 128, 4096], BF16).ap()

    def dres(which, l, jg, n=1):
        base = ((which * NL + l) * 8 + jg) * 4096
        return ("dr", base, base + n * 4096)

    def convert_ffn(l, which, q):
        src, dst = (w1_d, w1s) if which == 0 else (w2_d, w2s)
        fns = [DMA(dst[l, jg].rearrange("p (a b) -> p a b", b=1024),
                   src[l, jg].rearrange("p (a b) -> p a b", b=1024)) for jg in range(q * 2, q * 2 + 2)]
        P.dma_group("pool", ("cv", l, which, q), fns, writes=[dres(which, l, q * 2, 2)])
    if dbg:
        dbg_h = dram("dbg_hT", [128, 8, S], kind="ExternalOutput")
        dbg_y = dram("dbg_ycat", [128, 4, S], kind="ExternalOutput")
        dbg_c = dram("dbg_dcol", [128, 512], kind="ExternalOutput")

    arena = st.enter_context(nc.sbuf_tensor("arena", [128, ARENA_BYTES // 4], F32))
    psum = st.enter_context(nc.psum_tensor("psum", [128, 8, 512], F32))
    psflat = psum[:].rearrange("p a b -> p (a b)")

    def sb(lo, shape, dt=F32):
        esz = 4 if dt == F32 else 2
        n = int(np.prod(shape))
        assert lo % 4 == 0 and lo + n * esz <= ARENA_BYTES, (lo, shape)
        ap = arena[:, lo // 4:(lo + n * esz + 3) // 4]
        if dt != F32:
            ap = ap.bitcast(dt)[:, 0:n]
        if len(shape) == 2:
            ap = ap.rearrange("p (a b) -> p a b", b=shape[1])
        elif len(shape) == 3:
            ap = ap.rearrange("p (a b c) -> p a b c", b=shape[1], c=shape[2])
        return T(ap, "sb", lo, esz, n)

    def ps(off, n, dt=F32):
        ap = psflat[:, off:off + n]
        if dt != F32:
            ap = ap.bitcast(dt)
            return T(ap, "ps", off * 4, 2, n * 2)
        return T(ap, "ps", off * 4, 4, n)

    xT = sb(0, (8, S))
    hT = sb(65536, (8, S), BF16)
    CB = 98304
    colsT = sb(CB, (NCOLS,))
    dcol = sb(CB + 1024, (512,))
    cbf = sb(CB + 3072, (NCB,), BF16)
    mats = sb(CB + 6144, (5, 128), BF16)
    condc = sb(CB + 7680, (48,))
    sc_c = sb(CB + 7936, (8,))
    YH = CB + 12288
    ycat_h = sb(YH, (4, S), BF16)
    PH = YH + 16384

    ident = cbf[:, CB_IDENT:CB_IDENT + 128]
    prot = cbf[:, CB_PROT:CB_PROT + 128]
    maskF = cbf[:, CB_MASKF:CB_MASKF + 64]
    maskB = cbf[:, CB_MASKB:CB_MASKB + 64]
    scanm = cbf[:, CB_SCAN:CB_SCAN + 1024]
    bd64 = mats[:, 0, :]
    ones128 = mats[:, 1, :]
    ones1024 = mats[:, 2, :]
    wvE = mats[:, 3, :]
    wvO = mats[:, 4, :]

    DC_EPS = 0
    DC_G1 = [8 + l * 64 for l in range(NL)]
    DC_SH1 = [16 + l * 64 for l in range(NL)]
    DC_GG1 = [24 + l * 64 for l in range(NL)]
    DC_G2 = [32 + l * 64 for l in range(NL)]
    DC_SH2 = [40 + l * 64 for l in range(NL)]
    DC_GG2 = [48 + l * 64 for l in range(NL)]
    DC_LB = [56 + l * 64 for l in range(NL)]
    DC_OML = [60 + l * 64 for l in range(NL)]
    DC_NOML = [64 + l * 64 for l in range(NL)]
    DC_QG8 = [68 + l * 64 for l in range(NL)]
    DC_TMP = 200

    def dc(i, n=1):
        return dcol[:, i:i + n]

    def cl(l, i, n=1):
        return colsT[:, l * C_PER_LAYER + i:l * C_PER_LAYER + i + n]

    P.dma_group("sp", "cols", [DMA(colsT[:], cols_d)], writes=[colsT.r()])
    P.dma_group("pool", "cbf", [DMA(cbf[:], cbf_d)], writes=[cbf.r()])
    P.op("dve", MS(dcol[:], 0.0), writes=[dcol.r()])
    P.op("dve", MS(dc(DC_EPS), EPS), writes=[dcol.r()])
    P.op("dve", MS(mats[:, 0, :], 0.0), writes=[mats.r()])
    P.op("dve", MS(mats[0:64, 0, 0:64], 1.0 / 64), writes=[mats.r()])
    P.op("dve", MS(mats[64:128, 0, 64:128], 1.0 / 64), writes=[mats.r()])
    P.op("dve", MS(mats[:, 1, :], 1.0 / 128), writes=[mats.r()])
    P.op("dve", MS(mats[:, 2, :], 1.0 / 1024), writes=[mats.r()])
    P.op("dve", MS(mats[0:64, 3, :], 1.0 / 64), writes=[mats.r()])
    P.op("dve", MS(mats[64:128, 3, :], EPS / 64), writes=[mats.r()])
    P.op("dve", MS(mats[0:64, 4, :], EPS / 64), writes=[mats.r()])
    P.op("dve", MS(mats[64:128, 4, :], 1.0 / 64), writes=[mats.r()])

    for kc in range(8):
        P.dma_group("sp", ("x", kc), [DMA(xT[:, kc, :], xT_d[:, kc, :])], writes=[xT.r(kc * S, S)])

    cst = sb(PH, (8,))
    P.dma_group("sp", "cT", [DMA(cst[:], cT_d)], writes=[cst.r()])
    P.op("act", ACT(sc_c[:], cst[:], AF.Silu), reads=[cst.r()], writes=[sc_c.r()])
    wst = [sb(PH + 1024 + i * 16384, (4, 1024), BF16) for i in range(2)]
    sc_b = sb(PH + 512, (8,), BF16)
    P.op("dve", CP(sc_b[:], sc_c[:]), reads=[sc_c.r()], writes=[sc_b.r()])
    pcond = ps(0, 48)
    for g in range(4):
        w = wst[g % 2]
        P.dma_group("pool", ("wada", g % 2), [DMA(w[:], wada_d[g * 4:(g + 1) * 4].rearrange("c p n -> p c n"))],
                    writes=[w.r()])
        for cc in range(4):
            j = g * 4 + cc
            for kc in range(8):
                P.op("pe", MM(pcond[:, j:j + 1], w[:, cc, kc * 128:(kc + 1) * 128], sc_b[:, kc:kc + 1],
                              start=(kc == 0), stop=(kc == 7)),
                     reads=[w.r(), sc_b.r()], writes=[pcond.r()])
    P.op("dve", CP(condc[:, 0:16], pcond[:, 0:16]), reads=[pcond.r()], writes=[condc.r()])
    DC_MODE = 256
    DC_MOD = [288 + l * 48 for l in range(NL)]
    mode = dcol[:, DC_MODE:DC_MODE + 16]
    P.op("dve", TT(mode, condc[:, 0:16], cl(0, C_ADAB, 16), ALU.add),
         reads=[condc.r(), colsT.r()], writes=[dcol.r()])
    P.op("dve", STT(dc(DC_G1[0], 8), mode[:, 8:16], 1.0, cl(0, C_PREMIX, 8), ALU.add, ALU.mult),
         reads=[dcol.r(), colsT.r()], writes=[dcol.r()])
    P.op("dve", CP(dc(DC_SH1[0], 8), mode[:, 0:8]), reads=[dcol.r()], writes=[dcol.r()])
    for l in range(NL):
        P.op("dve", TS(dc(DC_QG8[l]), cl(l, C_QG), 0.125, None, ALU.mult), reads=[colsT.r()], writes=[dcol.r()])

    st2 = [sb(YH + 8192 + i * 2048, (1024,), BF16) for i in range(4)]
    sc_b2 = sb(CB + 7936 + 64, (8,), BF16)
    P.op("dve", CP(sc_b2[:], sc_c[:]), reads=[sc_c.r()], writes=[sc_b2.r()])
    pcond2 = ps(4 * 512, 32)

    def ada_chunk(j):
        w = st2[j % 4]
        P.dma_group("pool", ("wada2", j % 4), [DMA(w[:], wada_d[j])], writes=[w.r()])
        for kc in range(8):
            P.op("pe", MM(pcond2[:, j - 16:j - 15], w[:, kc * 128:(kc + 1) * 128], sc_b2[:, kc:kc + 1],
                          start=(kc == 0), stop=(kc == 7)),
                 reads=[w.r(), sc_b2.r()], writes=[pcond2.r()])

    def ada_late():
        P.op("dve", CP(condc[:, 16:48], pcond2[:]), reads=[pcond2.r()], writes=[condc.r()])
        for l in range(NL):
            modc = dcol[:, DC_MOD[l]:DC_MOD[l] + 48]
            rw = dict(reads=[condc.r(), colsT.r(), dcol.r()], writes=[dcol.r()])
            P.op("dve", TT(modc, condc[:], cl(l, C_ADAB, 48), ALU.add), **rw)
            for (dst, sci, gi) in ((DC_G1[l], 8, C_PREMIX), (DC_G2[l], 32, C_PREFF)):
                if l == 0 and sci == 8:
                    continue
                P.op("dve", STT(dc(dst, 8), modc[:, sci:sci + 8], 1.0, cl(l, gi, 8), ALU.add, ALU.mult), **rw)
            for (dst, si) in ((DC_SH1[l], 0), (DC_SH2[l], 24)):
                if l == 0 and si == 0:
                    continue
                P.op("dve", CP(dc(dst, 8), modc[:, si:si + 8]), **rw)
            for (dst, gi, pi) in ((DC_GG1[l], 16, C_POSTMIX), (DC_GG2[l], 40, C_POSTFF)):
                P.op("dve", TT(dc(dst, 8), modc[:, gi:gi + 8], cl(l, pi, 8), ALU.mult), **rw)

    l0 = cl(0, C_LBLOG, 4)
    l1 = cl(1, C_LBLOG, 4)
    t = lambda i: dcol[:, DC_TMP + 4 * i:DC_TMP + 4 * i + 4]
    rd = [colsT.r(), dcol.r()]
    P.op("dve", TT(t(0), l0, l1, ALU.max), reads=rd, writes=[dcol.r()])
    P.op("dve", TT(t(1), l0, t(0), ALU.subtract), reads=rd, writes=[dcol.r()])
    P.op("dve", TT(t(2), l1, t(0), ALU.subtract), reads=rd, writes=[dcol.r()])
    P.op("act", ACT(t(1), t(1), AF.Exp), reads=rd, writes=[dcol.r()])
    P.op("act", ACT(t(2), t(2), AF.Exp), reads=rd, writes=[dcol.r()])
    P.op("dve", TT(t(3), t(1), t(2), ALU.add), reads=rd, writes=[dcol.r()])
    P.op("dve", lambda h: h.reciprocal(out=t(3), in_=t(3)), reads=rd, writes=[dcol.r()])
    P.op("dve", TT(t(1), t(1), t(3), ALU.mult), reads=rd, writes=[dcol.r()])
    P.op("dve", TT(t(2), t(2), t(3), ALU.mult), reads=rd, writes=[dcol.r()])
    P.op("dve", TT(dc(DC_LB[0], 4), t(1), t(1), ALU.subtract), reads=rd, writes=[dcol.r()])
    P.op("dve", TT(t(0), t(1), t(2), ALU.add), reads=rd, writes=[dcol.r()])
    P.op("dve", TT(dc(DC_LB[1], 4), t(0), t(1), ALU.subtract), reads=rd, writes=[dcol.r()])
    for l in range(NL):
        P.op("dve", TS(dc(DC_OML[l], 4), dc(DC_LB[l], 4), -1.0, 1.0, ALU.mult, ALU.add), reads=rd, writes=[dcol.r()])
        P.op("dve", TS(dc(DC_NOML[l], 4), dc(DC_LB[l], 4), -1.0, None, ALU.add), reads=rd, writes=[dcol.r()])

    eps_c = dc(DC_EPS)

    def rstd_from(den, out_t, use_eps=True):
        if use_eps:
            P.op("act", ACT(out_t[:], den[:], AF.Ln, bias=eps_c), reads=[den.r(), dcol.r()], writes=[out_t.r()])
        else:
            P.op("act", ACT(out_t[:], den[:], AF.Ln), reads=[den.r()], writes=[out_t.r()])
        P.op("act", ACT(out_t[:], out_t[:], AF.Exp, scale=-0.5), reads=[out_t.r()], writes=[out_t.r()])

    def prenorm(tile, gcol0, shcol0, base, den_bank):
        sq = [sb(base + i * 1024, (512,), BF16) for i in range(2)]
        rs = sb(base + 2048, (512,))
        tm = [sb(base + 4096 + i * 2048, (512,)) for i in range(2)]
        den = ps(den_bank * 512, 512)
        for kc in range(8):
            s_ = sq[kc % 2]
            xr = xT.r(kc * S + tile * 512, 512)
            P.op("act", ACT(s_[:], xT[:, kc, tile * 512:(tile + 1) * 512], AF.Square), reads=[xr], writes=[s_.r()])
            P.op("pe", MM(den[:], ones1024, s_[:], start=(kc == 0), stop=(kc == 7)),
                 reads=[s_.r(), mats.r()], writes=[den.r()])
        rstd_from(den, rs)
        for kc in range(8):
            t_ = tm[kc % 2]
            xr = xT.r(kc * S + tile * 512, 512)
            hr = hT.r(kc * S + tile * 512, 512)
            P.op("dve", TT(t_[:], xT[:, kc, tile * 512:(tile + 1) * 512], rs[:], ALU.mult),
                 reads=[xr, rs.r()], writes=[t_.r()])
            P.op("dve", TS(hT[:, kc, tile * 512:(tile + 1) * 512], t_[:], dc(gcol0 + kc), dc(shcol0 + kc),
                           ALU.mult, ALU.add),
                 reads=[t_.r(), dcol.r()], writes=[hr])

    def residual(tile, ytmp, ggcol0, base, den_bank, sq_from_ytmp_done_den=None):
        rs = sb(base, (512,))
        rstd_from(sq_from_ytmp_done_den, rs)
        for m in range(8):
            yr = ytmp.r(m * 512, 512)
            xr = xT.r(m * S + tile * 512, 512)
            P.op("dve", TT(ytmp[:, m, :], ytmp[:, m, :], rs[:], ALU.mult), reads=[yr, rs.r()], writes=[yr])
            P.op("dve", STT(xT[:, m, tile * 512:(tile + 1) * 512], ytmp[:, m, :], dc(ggcol0 + m),
                            xT[:, m, tile * 512:(tile + 1) * 512], ALU.mult, ALU.add),
                 reads=[yr, xr, dcol.r()], writes=[xr])

    def load_w(eng, key, dst, src, reads=()):
        flat = dst.ap
        if len(dst.ap.shape) == 3:
            flat = dst.ap.rearrange("p a b -> p (a b)")
        n = flat.shape[1]
        if n > 1024:
            flat = flat.rearrange("p (a b) -> p a b", b=1024)
            src = src.rearrange("p (a b) -> p a b", b=1024)
        P.dma_group(eng, key, [DMA(flat, src)], writes=[dst.r()])

    def body():
      chk("setup")
      for tile in range(NT):
        prenorm(tile, DC_G1[0], DC_SH1[0], PH + 45056, 1)
      chk("pre0")
      main()

    ada_next = [16]

    def main():
     for l in range(NL):
        o = PH
        QmT = [sb(o + d * 4096, (S,), BF16) for d in range(2)]; o += 8192
        KmT = [sb(o + d * 4096, (S,), BF16) for d in range(2)]; o += 8192
        Kmt = [sb(o + d * 4096, (16, 128), BF16) for d in range(2)]; o += 8192
        Vtok = sb(o, (16, 128), BF16); o += 4096
        gs = sb(o, (S,), BF16); o += 4096
        qs = sb(o, (S,)); o += 8192
        osum = qs
        TS_ = []
        for d in range(2):
            TS_.append([sb(o + i * 4096, (1024,)) for i in range(3)]); o += 12288
        T1, T2, T3 = TS_[0]
        KdTs = [sb(o + d * 2048, (1024,), BF16) for d in range(2)]; o += 4096
        em = [sb(o + d * 128, (32,)) for d in range(2)]; o += 256
        ea = [sb(o + d * 128, (32,)) for d in range(2)]; o += 256
        ec = [sb(o + d * 128, (32,)) for d in range(2)]; o += 256
        dls = [sb(o + d * 64, (16,)) for d in range(2)]; o += 128
        Sst = [sb(o + d * 512, (128,)) for d in range(2)]; o += 1024
        Ub = [[sb(o + (d * 2 + b) * 256, (128,), BF16) for b in range(2)] for d in range(2)]; o += 1024
        ATb = [[sb(o + (d * 2 + b) * 128, (64,), BF16) for b in range(2)] for d in range(2)]; o += 512
        KAb = [[sb(o + (d * 2 + b) * 128, (64,), BF16) for b in range(2)] for d in range(2)]; o += 512
        KBb = [[sb(o + (d * 2 + b) * 128, (64,), BF16) for b in range(2)] for d in range(2)]; o += 512
        o = (o + 1023) // 1024 * 1024
        wH = [[sb(o + c * 2048, (8, 128), BF16) for c in range(5)]] * 2; o += 10240
        assert o <= ARENA_BYTES, o
        pbuf = [ps(0, 1024), ps(1024, 1024)]
        pch = [dict(sc=[ps(0, 64), ps(512, 64)], o=ps(4 * 512, 64), dS=ps(6 * 512, 128)),
               dict(sc=[ps(2 * 512, 64), ps(3 * 512, 64)], o=ps(5 * 512, 64), dS=ps(7 * 512, 128))]
        ptr = [ps(6 * 512, 512, BF16), ps(7 * 512, 512, BF16)]
        pv = [ps((4 + i) * 512, 128) for i in range(4)]
        pden = ps(0, 512)

        def load_head_w(h):
            for ci, chunk in enumerate((h, 4 + h, 8 + h, 12 + h, 16 + h)):
                load_w("pool", ("wH", h % 2, ci), wH[h % 2][ci], win_d[l, chunk])

        def proj_fm(wt, half, pb):
            for t2 in range(2):
                tile = half * 2 + t2
                for kc in range(8):
                    P.op("pe", MM(pb[:, t2 * 512:(t2 + 1) * 512], wt[:, kc, :], hT[:, kc, tile * 512:(tile + 1) * 512],
                                  start=(kc == 0), stop=(kc == 7)),
                         reads=[wt.r(), hT.r(kc * S + tile * 512, 512)], writes=[pb.r(t2 * 512, 512)])

        load_head_w(0)
        for d in range(2):
            for b in range(2):
                P.op("dve", MS(KAb[d][b][:], 0.0), writes=[KAb[d][b].r()])
                P.op("dve", MS(KBb[d][b][:], 0.0), writes=[KBb[d][b].r()])
        pbi = 0
        for h in range(4):
            wq, wff, wfb, wi, wg = wH[h % 2]
            lbc = dc(DC_LB[l] + h)
            omlc = dc(DC_OML[l] + h)
            nomlc = dc(DC_NOML[l] + h)
            for half in range(2):
                pb = pbuf[pbi % 2]; pbi += 1
                proj_fm(wq, half, pb)
                P.op("act", ACT(qs[:, half * 1024:(half + 1) * 1024], pb[:], AF.Silu),
                     reads=[pb.r()], writes=[qs.r(half * 1024, 1024)])
            for half in range(2):
                pb = pbuf[pbi % 2]; pbi += 1
                proj_fm(wg, half, pb)
                P.op("act", ACT(gs[:, half * 1024:(half + 1) * 1024], pb[:], AF.Silu),
                     reads=[pb.r()], writes=[gs.r(half * 1024, 1024)])
            chk("H%d.%dq" % (l, h))
            for tt in range(16):
                pvt = pv[tt % 4]
                tile = tt // 4
                for kc in range(8):
                    P.op("pe", MM(pvt[:], hT[:, kc, tt * 128:(tt + 1) * 128], wi[:, kc, :],
                                  start=(kc == 0), stop=(kc == 7)),
                         reads=[wi.r(), hT.r(kc * S + tile * 512, 512)], writes=[pvt.r()])
                eng = "act" if tt % 2 == 0 else "dve"
                fn = ACT(Vtok[:, tt, :], pvt[:], AF.Copy) if eng == "act" else CP(Vtok[:, tt, :], pvt[:])
                P.op(eng, fn, reads=[pvt.r()], writes=[Vtok.r(tt * 128, 128)])
            chk("H%d.%dv" % (l, h))
            def gate_stages(d, half):
                wf = wff if d == 0 else wfb
                mi = 31 if d == 0 else 32
                li = 63 if d == 0 else 0
                T1, T2, T3 = TS_[d]
                KdT = KdTs[d]
                dl = dls[d]
                pb = pbuf[d]
                ptrd = ptr[d]
                hs = slice(half * 1024, (half + 1) * 1024)
                b3 = T3[:].rearrange("p (c j) -> p c j", j=64)
                a3 = T1[:].rearrange("p (c j) -> p c j", j=64)
                cs = slice(half * 16, (half + 1) * 16)
                k3 = KmT[d][:, hs].rearrange("p (c j) -> p c j", j=64)
                kd3 = KdT[:].rearrange("p (c j) -> p c j", j=64)
                L = []
                L.append(lambda: proj_fm(wf, half, pb))
                L.append(lambda: P.op("act", ACT(T1[:], pb[:], AF.Sigmoid), reads=[pb.r()], writes=[T1.r()]))

                def s_k():
                    P.op("dve", TS(T2[:], T1[:], nomlc, omlc, ALU.mult, ALU.add),
                         reads=[T1.r(), dcol.r()], writes=[T2.r()])
                    P.op("act", ACT(T1[:], T1[:], AF.Ln, scale=omlc, bias=lbc),
                         reads=[T1.r(), dcol.r()], writes=[T1.r()])
                L.append(s_k)

                def s_scan():
                    if d == 0:
                        P.op("dve", lambda hh: hh.tensor_tensor_scan(out=T3[:], data0=scanm, data1=T1[:], initial=0.0,
                                                                     op0=ALU.mult, op1=ALU.add),
                             reads=[T1.r(), cbf.r()], writes=[T3.r()])
                    else:
                        P.op("dve", lambda hh: hh.tensor_tensor_scan(out=T3[:, ::-1], data0=scanm, data1=T1[:, ::-1],
                                                                     initial=0.0, op0=ALU.mult, op1=ALU.add),
                             reads=[T1.r(), cbf.r()], writes=[T3.r()])
                L.append(s_scan)

                def s_small():
                    P.op("act", ACT(em[d][:, cs], b3[:, :, mi], AF.Exp), reads=[T3.r()], writes=[em[d].r()])
                    P.op("act", ACT(ea[d][:, cs], b3[:, :, li], AF.Exp), reads=[T3.r()], writes=[ea[d].r()])
                    P.op("dve", TT(dl[:], b3[:, :, li], b3[:, :, mi], ALU.subtract), reads=[T3.r()], writes=[dl.r()])
                    P.op("dve", TT(a3, b3, b3[:, :, mi:mi + 1].to_broadcast([128, 16, 64]), ALU.subtract),
                         reads=[T3.r()], writes=[T1.r()])
                    P.op("dve", TS(T1[:], T1[:], 80.0, -80.0, ALU.min, ALU.max), reads=[T1.r()], writes=[T1.r()])
                    P.op("act", ACT(ec[d][:, cs], dl[:], AF.Exp), reads=[dl.r()], writes=[ec[d].r()])
                L.append(s_small)

                def s_e1():
                    P.op("act", ACT(T3[:], T1[:], AF.Exp), reads=[T1.r()], writes=[T3.r()])
                L.append(s_e1)

                def s_q():
                    P.op("dve", TT(QmT[d][:, hs], qs[:, hs], T3[:], ALU.mult),
                         reads=[qs.r(half * 1024, 1024), T3.r()], writes=[QmT[d].r(half * 1024, 1024)])
                    P.op("act", ACT(T1[:], T1[:], AF.Exp, scale=-1.0), reads=[T1.r()], writes=[T1.r()])
                L.append(s_q)

                def s_km():
                    P.op("dve", TT(KmT[d][:, hs], T2[:], T1[:], ALU.mult),
                         reads=[T2.r(), T1.r()], writes=[KmT[d].r(half * 1024, 1024)])
                    P.op("dve", TT(kd3, k3, ec[d][:, cs].unsqueeze(2).to_broadcast([128, 16, 64]), ALU.mult),
                         reads=[KmT[d].r(half * 1024, 1024), ec[d].r()], writes=[KdT.r()])
                L.append(s_km)

                def s_tr(q4):
                    def f():
                        for t4 in range(4):
                            tt = q4 * 4 + t4
                            P.op("pe", TR(ptrd[:, t4 * 128:(t4 + 1) * 128], KdT[:, tt * 128:(tt + 1) * 128], ident),
                                 reads=[KdT.r(), cbf.r()], writes=[ptrd.r(t4 * 128, 128)])
                        g0 = half * 8 + q4 * 4
                        P.op("act", ACT(Kmt[d][:, g0:g0 + 4, :].rearrange("p a b -> p (a b)"),
                                        ptrd[:, 0:512], AF.Copy),
                             reads=[ptrd.r(0, 512)], writes=[Kmt[d].r(g0 * 128, 512)])
                    return f
                L.append(s_tr(0))
                L.append(s_tr(1))
                return L

            for half in range(2):
                A_ = gate_stages(0, half)
                B_ = gate_stages(1, half)
                for k in range(len(A_) + 1):
                    if k < len(A_):
                        A_[k]()
                    if k >= 1:
                        B_[k - 1]()
                    if l == 0 and h == 0:
                        for _ in range(2 if ada_next[0] < 16 + 20 else 1):
                            if ada_next[0] < 48:
                                ada_chunk(ada_next[0])
                                ada_next[0] += 1
            if l == 0 and h == 0:
                while ada_next[0] < 48:
                    ada_chunk(ada_next[0])
                    ada_next[0] += 1
                ada_late()
            if h + 1 < 4:
                load_head_w(h + 1)
            chk("H%d.%dg1" % (l, h))
            for d in range(2):
                P.op("dve", MS(Sst[d][:], 0.0), writes=[Sst[d].r()])
                P.op("dve", MS(Ub[d][0][:], 0.0), writes=[Ub[d][0].r()])
            def chain_step(s_, d):
                c = s_ if d == 0 else 31 - s_
                par = (c % 2) * 64
                tl = c // 2
                pr = slice(par, par + 64)
                ck = slice(c * 64, (c + 1) * 64)
                psc = pch[d]["sc"][s_ % 2]
                po = pch[d]["o"]
                pdS = pch[d]["dS"]
                at = ATb[d][s_ % 2]
                ucur = Ub[d][s_ % 2]
                unxt = Ub[d][(s_ + 1) % 2]
                msk = maskF if d == 0 else maskB
                ka = KAb[d][s_ % 2]
                kb = KBb[d][s_ % 2]
                cA = slice(c * 64, c * 64 + 32)
                cB = slice(c * 64 + 32, c * 64 + 64)

                def kakb():
                    P.op("pool", CP(ka[:, 0:32], KmT[d][:, cA]), reads=[KmT[d].r(c * 64, 64)], writes=[ka.r()])
                    P.op("pool", CP(kb[:, 32:64], KmT[d][:, cB]), reads=[KmT[d].r(c * 64, 64)], writes=[kb.r()])

                def sc():
                    if d == 0:
                        P.op("pe", MM(psc[pr, :], ka[:], QmT[d][:, ck], start=True, stop=False),
                             reads=[ka.r(), QmT[d].r(c * 64, 64)], writes=[psc.r()])
                        P.op("pe", MM(psc[pr, 32:64], kb[:], QmT[d][:, cB], start=False, stop=True),
                             reads=[kb.r(), QmT[d].r(c * 64, 64)], writes=[psc.r()])
                    else:
                        P.op("pe", MM(psc[pr, :], kb[:], QmT[d][:, ck], start=True, stop=False),
                             reads=[kb.r(), QmT[d].r(c * 64, 64)], writes=[psc.r()])
                        P.op("pe", MM(psc[pr, 0:32], ka[:], QmT[d][:, cA], start=False, stop=True),
                             reads=[ka.r(), QmT[d].r(c * 64, 64)], writes=[psc.r()])

                def mask():
                    P.op("dve", TT(at[pr, :], psc[pr, :], msk[pr, :], ALU.mult),
                         reads=[psc.r(), cbf.r()], writes=[at.r()])

                def om():
                    P.op("pe", MM(po[:], Vtok[pr, tl, :], at[pr, :], start=True, stop=False),
                         reads=[Vtok.r(tl * 128, 128), at.r()], writes=[po.r()])
                    P.op("pe", MM(po[:], ucur[:], QmT[d][:, ck], start=False, stop=True),
                         reads=[ucur.r(), QmT[d].r(c * 64, 64)], writes=[po.r()])
                    first = (d == 0 and c <= 15) or (d == 1 and c >= 16)
                    if first:
                        P.op("act", ACT(osum[:, ck], po[:], AF.Copy), reads=[po.r()], writes=[osum.r(c * 64, 64)])
                    else:
                        P.op("dve", TT(osum[:, ck], po[:], osum[:, ck], ALU.add),
                             reads=[po.r(), osum.r(c * 64, 64)], writes=[osum.r(c * 64, 64)])

                def dS():
                    if s_ < 31:
                        P.op("pe", MM(pdS[:], Kmt[d][pr, tl, :], Vtok[pr, tl, :]),
                             reads=[Kmt[d].r(tl * 128, 128), Vtok.r(tl * 128, 128)], writes=[pdS.r()])

                def upd():
                    if s_ < 31:
                        cn = c + 1 if d == 0 else c - 1
                        P.op("dve", STT(Sst[d][:], Sst[d][:], ea[d][:, c:c + 1], pdS[:], ALU.mult, ALU.add),
                             reads=[Sst[d].r(), ea[d].r(), pdS.r()], writes=[Sst[d].r()])
                        P.op("act", ACT(unxt[:], Sst[d][:], AF.Identity, scale=em[d][:, cn:cn + 1]),
                             reads=[Sst[d].r(), em[d].r()], writes=[unxt.r()])

                return dict(kakb=kakb, sc=sc, mask=mask, om=om, dS=dS, upd=upd)

            steps = [[chain_step(s_, d) for d in range(2)] for s_ in range(32)]
            for it in range(-2, 32):
                for d in range(2):
                    if 0 <= it + 2 < 32:
                        steps[it + 2][d]["kakb"]()
                for d in range(2):
                    if 0 <= it + 1 < 32:
                        steps[it + 1][d]["sc"]()
                for d in range(2):
                    if 0 <= it < 32:
                        steps[it][d]["dS"]()
                for d in range(2):
                    if 0 <= it + 1 < 32:
                        steps[it + 1][d]["mask"]()
                for d in range(2):
                    if 0 <= it < 32:
                        steps[it][d]["upd"]()
                for d in range(2):
                    if 0 <= it < 32:
                        steps[it][d]["om"]()
            chk("H%d.%dc" % (l, h))
            base_ = TS_[0][0].lo
            sqbs = [sb(base_ + t_ * 1024, (512,), BF16) for t_ in range(4)]
            rss = [sb(base_ + 4096 + t_ * 2048, (512,)) for t_ in range(4)]
            tms = [sb(base_ + 12288 + t_ * 2048, (512,)) for t_ in range(4)]
            pdens = [ps(t_ * 512, 512) for t_ in range(4)]
            for tile in range(NT):
                ts_ = slice(tile * 512, (tile + 1) * 512)
                P.op("act", ACT(sqbs[tile][:], osum[:, ts_], AF.Square), reads=[osum.r(tile * 512, 512)],
                     writes=[sqbs[tile].r()])
            for tile in range(NT):
                P.op("pe", MM(pdens[tile][:], ones128, sqbs[tile][:]), reads=[sqbs[tile].r(), mats.r()],
                     writes=[pdens[tile].r()])
            for tile in range(NT):
                P.op("act", ACT(rss[tile][:], pdens[tile][:], AF.Ln, bias=eps_c), reads=[pdens[tile].r(), dcol.r()],
                     writes=[rss[tile].r()])
            for tile in range(NT):
                P.op("act", ACT(rss[tile][:], rss[tile][:], AF.Exp, scale=-0.5), reads=[rss[tile].r()],
                     writes=[rss[tile].r()])
            for tile in range(NT):
                ts_ = slice(tile * 512, (tile + 1) * 512)
                P.op("dve", TT(tms[tile][:], osum[:, ts_], rss[tile][:], ALU.mult),
                     reads=[osum.r(tile * 512, 512), rss[tile].r()], writes=[tms[tile].r()])
                P.op("dve", STT(ycat_h[:, h, ts_], tms[tile][:], cl(l, C_HGG), gs[:, ts_], ALU.mult, ALU.mult),
                     reads=[tms[tile].r(), colsT.r(), gs.r(tile * 512, 512)],
                     writes=[ycat_h.r(h * S + tile * 512, 512)])

            chk("H%d.%d" % (l, h))
        chk("H%d" % l)
        o = PH
        kTz = [[sb(o + (g * 2 + e) * 4096, (S,), BF16) for e in range(2)] for g in range(2)]; o += 16384
        Vaug = [[sb(o + (g * 2 + e) * 4096, (16, 128), BF16) for e in range(2)] for g in range(2)]; o += 16384
        cosT = sb(o, (S,), BF16); o += 4096
        sinT = sb(o, (S,), BF16); o += 4096
        wA = [sb(o + i * 2048, (8, 128), BF16) for i in range(4)]; o += 8192
        wO = [sb(o + i * 2048, (8, 128), BF16) for i in range(2)]; o += 4096
        qT = [sb(o + i * 1024, (512,), BF16) for i in range(2)]; o += 2048
        PT = [sb(o + i * 2048, (1024,), BF16) for i in range(2)]; o += 4096
        ov = o
        zsb = sb(o, (512,)); o += 2048
        knb = sb(o, (512,), BF16); o += 1024
        t1a = sb(o, (512,)); o += 2048
        t2a = sb(o, (512,)); o += 2048
        sqq = sb(o, (512,), BF16); o += 1024
        rsq = sb(o, (512,)); o += 2048
        assert o - ov <= 16384
        ytmp = sb(ov, (8, 512))
        pnb = ov
        o = ov + 16384
        sqa = [sb(o + i * 1024, (512,), BF16) for i in range(2)]; o += 2048
        rsa = sb(o, (512,)); o += 2048
        ycat_a = sb(o, (4, 512), BF16); o += 4096
        assert o <= ARENA_BYTES, o
        pz = [ps(0, 512), ps(512, 512)]
        pdn = pz[0]
        pSp = [ps(2 * 512, 1024), ps(4 * 512, 1024)]
        pPV = [ps((6 + i) * 512, 512) for i in range(2)]

        P.dma_group("pool", "cos", [DMA(cosT[:].rearrange("p (a b) -> p a b", b=1024),
                                        cos_d.rearrange("p (a b) -> p a b", b=1024))], writes=[cosT.r()])
        P.dma_group("pool", "sin", [DMA(sinT[:].rearrange("p (a b) -> p a b", b=1024),
                                        sin_d.rearrange("p (a b) -> p a b", b=1024))], writes=[sinT.r()])
        for g in range(2):
            P.op("dve", MS(kTz[g][0][64:128, :], 0.0), writes=[kTz[g][0].r()])
            P.op("dve", MS(kTz[g][1][0:64, :], 0.0), writes=[kTz[g][1].r()])
        load_w("pool", ("wA", 0), wA[0], win_d[l, 24])
        load_w("pool", ("wA", 1), wA[1], win_d[l, 25])
        load_w("pool", ("wA", 2), wA[2], win_d[l, 26])
        for g in range(2):
            P.op("dve", MS(Vaug[g][0][:, :, 64:128], 1.0), writes=[Vaug[g][0].r()])
            P.op("dve", MS(Vaug[g][1][:, :, 0:64], 1.0), writes=[Vaug[g][1].r()])

        scrA = dict(zsb=zsb, knb=knb, t1a=t1a, t2a=t2a, sqq=sqq, rsq=rsq)
        scrB = dict(zsb=sb(qT[0].lo, (512,)), t1a=sb(qT[0].lo + 2048, (512,)), t2a=sb(qT[0].lo + 4096, (512,)),
                    knb=sb(sqa[0].lo, (512,), BF16), sqq=sb(sqa[0].lo + 1024, (512,), BF16),
                    rsq=sb(sqa[0].lo + 2048, (512,)))
        assert qT[0].lo + 6144 <= ov and sqa[0].lo + 4096 <= ycat_a.lo + 4096

        def qk_stages(wt, tile, gaincol, dst, dst_res, pzi, scr=None, banks=None):
            scr = scr or scrA
            zsb, knb, t1a, t2a, sqq, rsq = (scr[k_] for k_ in ("zsb", "knb", "t1a", "t2a", "sqq", "rsq"))
            if banks is None:
                pzt = pz[pzi % 2]
                prt = pz[(pzi + 1) % 2]
            else:
                pzt, prt = banks
            ts_ = slice(tile * 512, (tile + 1) * 512)

            def s0():
                for kc in range(8):
                    P.op("pe", MM(pzt[:], wt[:, kc, :], hT[:, kc, ts_], start=(kc == 0), stop=(kc == 7)),
                         reads=[wt.r(), hT.r(kc * S + tile * 512, 512)], writes=[pzt.r()])

            def s1():
                P.op("act", ACT(sqq[:], pzt[:], AF.Square), reads=[pzt.r()], writes=[sqq.r()])
                P.op("dve", CP(zsb[:], pzt[:]), reads=[pzt.r()], writes=[zsb.r()])

            def s2():
                P.op("pe", MM(prt[:], bd64, sqq[:]), reads=[sqq.r(), mats.r()], writes=[prt.r()])

            def s3():
                rstd_from(prt, rsq)

            def s4():
                P.op("dve", STT(knb[:], zsb[:], gaincol, rsq[:], ALU.mult, ALU.mult),
                     reads=[zsb.r(), rsq.r(), dcol.r(), colsT.r()], writes=[knb.r()])

            def s5():
                P.op("pe", MM(prt[:], prot, knb[:]), reads=[knb.r(), cbf.r()], writes=[prt.r()])

            def s6():
                P.op("dve", TT(t1a[:], knb[:], cosT[:, ts_], ALU.mult), reads=[knb.r(), cosT.r()], writes=[t1a.r()])
                P.op("dve", TT(t2a[:], prt[:], sinT[:, ts_], ALU.mult), reads=[prt.r(), sinT.r()], writes=[t2a.r()])
                for (d_ap, d_res, psl) in dst:
                    P.op("dve", TT(d_ap, t1a[psl, :], t2a[psl, :], ALU.add), reads=[t1a.r(), t2a.r()], writes=[d_res])

            return [s0, s1, s2, s3, s4, s5, s6]

        pzi = 0

        def k_path(g, tile, scr, banks):
            tsl = slice(tile * 512, (tile + 1) * 512)
            dsts = [(kTz[g][0][0:64, tsl], kTz[g][0].r(tile * 512, 512), slice(0, 64)),
                    (kTz[g][1][64:128, tsl], kTz[g][1].r(tile * 512, 512), slice(64, 128))]
            return qk_stages(wA[g], tile, cl(l, C_KG), dsts, None, 0, scr=scr, banks=banks)

        for g in range(2):
            for tp_ in range(0, NT, 2):
                A_ = k_path(g, tp_, scrA, (pz[0], pz[1]))
                B_ = k_path(g, tp_ + 1, scrB, (ps(2 * 512, 512), ps(3 * 512, 512)))
                for k in range(len(A_) + 1):
                    if k < len(A_):
                        A_[k]()
                    if k >= 1:
                        B_[k - 1]()
        for tt in range(16):
            pvt = ps((6 + tt % 2) * 512, 128)
            tile = tt // 4
            for kc in range(8):
                P.op("pe", MM(pvt[:], hT[:, kc, tt * 128:(tt + 1) * 128], wA[2][:, kc, :],
                              start=(kc == 0), stop=(kc == 7)),
                     reads=[wA[2].r(), hT.r(kc * S + tile * 512, 512)], writes=[pvt.r()])
            for g in range(2):
                P.op("act", ACT(Vaug[g][0][:, tt, 0:64], pvt[:, g * 64:(g + 1) * 64], AF.Copy),
                     reads=[pvt.r()], writes=[Vaug[g][0].r(tt * 128, 128)])
                P.op("dve", CP(Vaug[g][1][:, tt, 64:128], pvt[:, g * 64:(g + 1) * 64]),
                     reads=[pvt.r()], writes=[Vaug[g][1].r(tt * 128, 128)])

        chk("AK%d" % l)
        wq = [wA[(pc + 3) % 4] for pc in range(4)]
        for pc in range(4):
            load_w("pool", ("wA", (pc + 3) % 4), wq[pc], win_d[l, 20 + pc])

        def make_q(tile, pc):
            nonlocal pzi
            q_ = qT[pc % 2]
            stg = qk_stages(wq[pc], tile, dc(DC_QG8[l]), [(q_[:], q_.r(), slice(0, 128))], None, pzi)
            pzi += 1
            return stg

        for st_ in make_q(0, 0):
            st_()
        woi = 0
        LA = 2
        for tile in range(NT):
            ts_ = slice(tile * 512, (tile + 1) * 512)
            inj = {}

            def add_inj(i, fn):
                inj.setdefault(i, []).append(fn)

            for pc in range(4):
                if pc < 3:
                    nt_, npc = tile, pc + 1
                elif tile + 1 < NT:
                    nt_, npc = tile + 1, 0
                else:
                    continue
                for fn, off in zip(make_q(nt_, npc), (0, 1, 3, 4, 5, 6, 8)):
                    add_inj(pc * 16 + off, fn)

            wslots = [wO[0], wO[1]] + [sb(ov + (5 + k) * 2048, (8, 128), BF16) for k in range(3)]
            wmap = [0, 1, 2, 3, 4, 0, 1, 4]

            def wout_load(m):
                wt = wslots[wmap[m]]
                load_w("pool", ("wO", wmap[m]), wt, wout_d[l, m])
            for m_, at_ in zip(range(5), (40, 44, 48, 52, 56)):
                add_inj(at_, lambda m_=m_: wout_load(m_))
            if tile < 2:
                for q in range(4):
                    add_inj(2 + q * 11, lambda q=q, which=tile: convert_ffn(l, which, q))

            its = [(pc, par, kp) for pc in range(4) for par in range(2) for kp in range(8)]
            N = len(its)

            def head_post(pc, par):
                pr = slice(par * 64, (par + 1) * 64)
                ppv = pPV[par]
                sq_ = sqa[par]

                def p0():
                    P.op("act", ACT(sq_[:], ppv[:], AF.Square), reads=[ppv.r()], writes=[sq_.r()])

                def p1():
                    P.op("pe", MM(pdn[:], wvE if par == 0 else wvO, sq_[:]), reads=[sq_.r(), mats.r()],
                         writes=[pdn.r()])
                    rstd_from(pdn, rsa, use_eps=False)
                    P.op("dve", STT(ycat_a[pr, pc, :], ppv[pr, :], cl(l, C_AOG)[pr, :], rsa[pr, :],
                                    ALU.mult, ALU.mult),
                         reads=[ppv.r(), rsa.r(), colsT.r()], writes=[ycat_a.r(pc * 512, 512)])
                return p0, p1

            for i in range(N + 1):
                for fn in inj.pop(i, ()):
                    fn()
                if i < N:
                    pc, par, kp = its[i]
                    g = pc // 2
                    q_ = qT[pc % 2]
                    pst = pSp[i % 2]
                    pt_ = PT[i % 2]
                    for e in (0, 0, 1):
                        kc16 = kp * 2 + e
                        P.op("pe", MM(pst[:, e * 512:(e + 1) * 512], kTz[g][par][:, kc16 * 128:(kc16 + 1) * 128], q_[:]),
                             reads=[kTz[g][par].r(kc16 * 128, 128), q_.r()], writes=[pst.r(e * 512, 512)])
                    P.op("act", ACT(pt_[:], pst[:], AF.Exp), reads=[pst.r()], writes=[pt_.r()])
                j = i - 1
                if j >= 0:
                    pc, par, kp = its[j]
                    g = pc // 2
                    pt_ = PT[j % 2]
                    ppv = pPV[par]
                    for e in range(2):
                        kc16 = kp * 2 + e
                        P.op("pe", MM(ppv[:], Vaug[g][par][:, kc16, :], pt_[:, e * 512:(e + 1) * 512],
                                      start=(kc16 == 0), stop=(kc16 == 15)),
                             reads=[Vaug[g][par].r(kc16 * 128, 128), pt_.r(e * 512, 512)], writes=[ppv.r()])
                    if kp == 7:
                        p0, p1 = head_post(pc, par)
                        p0()
                        add_inj(i + 2, p1)
            for i in sorted(inj):
                for fn in inj[i]:
                    fn()
            pend = None
            for m in range(8):
                wt = wslots[wmap[m]]
                py = ps((2 + m % 4) * 512, 512)
                for kc in range(8):
                    if kc < 4:
                        rhs = ycat_h[:, kc, ts_]; rr = ycat_h.r(kc * S + tile * 512, 512)
                    else:
                        rhs = ycat_a[:, kc - 4, :]; rr = ycat_a.r((kc - 4) * 512, 512)
                    P.op("pe", MM(py[:], wt[:, kc, :], rhs, start=(kc == 0), stop=(kc == 7)),
                         reads=[wt.r(), rr], writes=[py.r()])
                if m in (0, 1):
                    wout_load(m + 5)
                elif m == 4:
                    wout_load(7)
                sq_ = sqa[m % 2]
                P.op("dve", CP(ytmp[:, m, :], py[:]), reads=[py.r()], writes=[ytmp.r(m * 512, 512)])
                P.op("act", ACT(sq_[:], py[:], AF.Square), reads=[py.r()], writes=[sq_.r()])
                if pend is not None:
                    pend()

                def pend(m=m, sq_=sq_):
                    P.op("pe", MM(pdn[:], ones1024, sq_[:], start=(m == 0), stop=(m == 7)),
                         reads=[sq_.r(), mats.r()], writes=[pdn.r()])
            pend()
            residual(tile, ytmp, DC_GG1[l], rsa.lo, None, pdn)
            prenorm(tile, DC_G2[l], DC_SH2[l], pnb, 2)

            chk("A%d.%d" % (l, tile))
        chk("A%d" % l)
        o = YH
        h1T = sb(o, (32, 512), BF16); o += 32768
        ytmp = sb(o, (8, 512)); o += 16384
        w1b = [sb(o + i * 8192, (8, 512), BF16) for i in range(2)]; o += 16384
        w2b = [sb(o + i * 8192, (4, 1024), BF16) for i in range(2)]; o += 16384
        rt = [sb(o + i * 2048, (512,)) for i in range(2)]; o += 4096
        sqf = [sb(o + i * 1024, (512,), BF16) for i in range(2)]; o += 2048
        rsf = sb(o, (512,)); o += 2048
        pnb = o; o += 8192
        assert o <= ARENA_BYTES, o
        pbk = [ps(i * 512, 512) for i in range(8)]
        w1i = 0
        w2i = 0
        bki = 0

        def ffn_tail(tile):
            ts_ = slice(tile * 512, (tile + 1) * 512)
            for m in range(8):
                sq_ = sqf[m % 2]
                P.op("act", ACT(sq_[:], ytmp[:, m, :], AF.Square), reads=[ytmp.r(m * 512, 512)], writes=[sq_.r()])
                P.op("pe", MM(pbk[7][:], ones1024, sq_[:], start=(m == 0), stop=(m == 7)),
                     reads=[sq_.r(), mats.r()], writes=[pbk[7].r()])
            residual(tile, ytmp, DC_GG2[l], rsf.lo, None, pbk[7])
            if l + 1 < NL:
                prenorm(tile, DC_G1[l + 1], DC_SH1[l + 1], pnb, 7)
            elif stop is None:
                P.dma_group("sp", ("out", tile),
                            [DMA(out_d[:, m, ts_], xT[:, m, ts_]) for m in range(8)],
                            reads=[xT.r(m * S + tile * 512, 512) for m in range(8)])

        pending = None
        for tile in range(NT):
            ts_ = slice(tile * 512, (tile + 1) * 512)
            for jg in range(8):
                wt = w1b[w1i % 2]
                P.dma_group("sp", ("w1", w1i % 2),
                            [DMA(wt[:].rearrange("p a b -> p (a b)"), w1s[l, jg])],
                            reads=[dres(0, l, jg)], writes=[wt.r()])
                w1i += 1
                for jj in range(4):
                    j = jg * 4 + jj
                    pb = pbk[bki % 7]; bki += 1
                    for kc in range(8):
                        P.op("pe", MM(pb[:], wt[:, kc, jj * 128:(jj + 1) * 128], hT[:, kc, ts_],
                                      start=(kc == 0), stop=(kc == 7)),
                             reads=[wt.r(), hT.r(kc * S + tile * 512, 512)], writes=[pb.r()])
                    r_ = rt[j % 2]
                    P.op("act", ACT(r_[:], pb[:], AF.Relu), reads=[pb.r()], writes=[r_.r()])
                    P.op("dve", TT(h1T[:, j, :], r_[:], r_[:], ALU.mult), reads=[r_.r()], writes=[h1T.r(j * 512, 512)])
                if jg == 1 and pending is not None:
                    pending()
                    pending = None
            for jg in range(8):
                wt = w2b[w2i % 2]
                P.dma_group("sp", ("w2", w2i % 2),
                            [DMA(wt[:].rearrange("p a b -> p (a b)"), w2s[l, jg])],
                            reads=[dres(1, l, jg)], writes=[wt.r()])
                w2i += 1
                for jj in range(4):
                    j = jg * 4 + jj
                    for m in range(8):
                        P.op("pe", MM(pbk[m][:], wt[:, jj, m * 128:(m + 1) * 128], h1T[:, j, :],
                                      start=(j == 0), stop=(j == 31)),
                             reads=[wt.r(), h1T.r(j * 512, 512)], writes=[pbk[m].r()])
            for m in range(8):
                if m % 2 == 0:
                    P.op("act", ACT(ytmp[:, m, :], pbk[m][:], AF.Copy), reads=[pbk[m].r()],
                         writes=[ytmp.r(m * 512, 512)])
                else:
                    P.op("dve", CP(ytmp[:, m, :], pbk[m][:]), reads=[pbk[m].r()], writes=[ytmp.r(m * 512, 512)])
            pending = (lambda t_=tile: ffn_tail(t_))
        pending()
        chk("F%d" % l)

    try:
        body()
    except _Stop:
        pass
    fkeys = [("out", t_) for t_ in range(NT)]
    if stop is not None:
        for tile in range(NT):
            ts_ = slice(tile * 512, (tile + 1) * 512)
            P.dma_group("sp", ("out", tile),
                        [DMA(out_d[:, m, ts_], xT[:, m, ts_]) for m in range(8)],
                        reads=[xT.r(m * S + tile * 512, 512) for m in range(8)])
    if dbg:
        P.dma_group("pool", "dbgh", [DMA(dbg_h[:, m, :], hT[:, m, :]) for m in range(8)], reads=[hT.r()])
        P.dma_group("pool", "dbgy", [DMA(dbg_y[:, m, :], ycat_h[:, m, :]) for m in range(4)], reads=[ycat_h.r()])
        P.dma_group("sp", "dbgc", [DMA(dbg_c, dcol[:])], reads=[dcol.r()])
        fkeys += ["dbgh", "dbgy", "dbgc"]
    counts = P.emit(nc, final_wait_keys=fkeys)
    st.close()
    return counts


def _chunk_cols(w):
    n = w.shape[1] // 128
    return np.ascontiguousarray(w.reshape(8, 128, n, 128).transpose(2, 1, 0, 3).reshape(n, 128, 1024))


def _host_layout(x, c, w_ada, ada_layer_bias, hg_lb_logits, pre_mix, post_mix, w_in, w_out,
                 hg_out_gain, q_gain, k_gain, att_out_gain, pre_ff, post_ff, w_ff1, w_ff2):
    f = np.float32
    col8 = lambda v: np.asarray(v, f).reshape(-1, 128).T
    rep2 = lambda v: np.concatenate([np.asarray(v, f), np.asarray(v, f)]).reshape(128, 1)
    cols = []
    for l in range(NL):
        cols += [col8(ada_layer_bias[l]), col8(pre_mix[l]), col8(post_mix[l]), col8(pre_ff[l]), col8(post_ff[l]),
                 col8(hg_lb_logits[l]), col8(hg_out_gain[l]), rep2(q_gain[l]), rep2(k_gain[l]), rep2(att_out_gain[l])]
    cols = np.ascontiguousarray(np.concatenate(cols, axis=1), dtype=f)
    assert cols.shape == (128, NCOLS)
    cbf = np.zeros((128, NCB), f)
    cbf[:, CB_IDENT:CB_IDENT + 128] = np.eye(128, dtype=f)
    prot = np.zeros((128, 128), f)
    for m in range(128):
        if (m % 32) < 16:
            prot[m + 16, m] = -1.0
        else:
            prot[m - 16, m] = 1.0
    cbf[:, CB_PROT:CB_PROT + 128] = prot
    pp = (np.arange(128) % 64)[:, None]
    tt = np.arange(64)[None, :]
    cbf[:, CB_MASKF:CB_MASKF + 64] = (pp <= tt)
    cbf[:, CB_MASKB:CB_MASKB + 64] = (pp >= tt)
    sm = np.ones(1024, f)
    sm[::64] = 0
    cbf[:, CB_SCAN:CB_SCAN + 1024] = sm[None, :]
    half = 32
    inv_freq = (1.0 / (np.float32(10000.0) ** (np.arange(0, half, 2, dtype=f) / f(half)))).astype(f)
    tpos = np.arange(S)
    row = (tpos // 64).astype(f)
    colp = (tpos % 64).astype(f)
    dd = np.arange(128) % 64
    ang = np.where((dd < 32)[:, None], row[None, :], colp[None, :]).astype(f) * inv_freq[dd % 16][:, None]
    ang = ang.astype(f)
    cosT = np.cos(ang).astype(f)
    sinT = np.sin(ang).astype(f)
    wada = np.ascontiguousarray(
        np.asarray(w_ada, f).reshape(8, 128, 48, 128).transpose(2, 1, 0, 3).reshape(48, 128, 1024))
    win = np.zeros((NL, 27, 128, 1024), f)
    for l in range(NL):
        w = np.asarray(w_in[l], f)
        win[l, 0:24] = _chunk_cols(w[:, 0:3072])
        k0 = w[:, 3072:3136]
        k1 = w[:, 3136:3200]
        win[l, 24] = _chunk_cols(np.concatenate([k0, k0], axis=1))[0]
        win[l, 25] = _chunk_cols(np.concatenate([k1, k1], axis=1))[0]
        win[l, 26] = _chunk_cols(w[:, 3200:3328])[0]
    wout = np.stack([_chunk_cols(np.asarray(w_out[l], f)) for l in range(NL)])
    w1 = np.stack([np.asarray(w_ff1[l], f).reshape(8, 128, 8, 512).transpose(2, 1, 0, 3).reshape(8, 128, 4096)
                   for l in range(NL)])
    w2 = np.stack([np.asarray(w_ff2[l], f).reshape(8, 4, 128, 1024).transpose(0, 2, 1, 3).reshape(8, 128, 4096)
                   for l in range(NL)])
    shared = dict(cols=cols, cbf=cbf, cosT=cosT, sinT=sinT, wada=wada, win=win,
                  wout=np.ascontiguousarray(wout), w1=np.ascontiguousarray(w1), w2=np.ascontiguousarray(w2))
    in_maps = []
    xx = np.asarray(x, f)
    cc = np.asarray(c, f)
    for b in range(NCORES):
        xT = np.ascontiguousarray(xx[b].T.reshape(8, 128, S).transpose(1, 0, 2))
        cT = np.ascontiguousarray(cc[b].reshape(8, 128).T)
        m = dict(shared)
        m["xT"] = xT
        m["cT"] = cT
        in_maps.append(m)
    return in_maps


_CACHE = {}


def kernel(**inputs):
    in_maps = _host_layout(**inputs)
    if "nc" not in _CACHE:
        nc = bass.Bass("TRN2", target_bir_lowering=False)
        build(nc)
        _CACHE["nc"] = nc
    nc = _CACHE["nc"]
    res = run_bass_kernel_spmd(nc, in_maps, core_ids=list(range(NCORES)))
    out = np.empty((NCORES, S, D), np.float32)
    for b in range(NCORES):
        oT = res.results[b]["outT"]
        out[b] = oT.transpose(1, 0, 2).reshape(D, S).T
    return out
```

```python
import contextlib
import numpy as np
import concourse.bass as bass
import concourse.mybir as mybir
from concourse.bass_utils import run_bass_kernel_spmd

F32 = mybir.dt.float32
BF16 = mybir.dt.bfloat16
AF = mybir.ActivationFunctionType
ALU = mybir.AluOpType

D = 1024
S = 2048
NL = 2
NT = 4
EPS = 1e-6
NCORES = 8
ENGS = ("pe", "act", "dve", "pool", "sp")


class Op:
    __slots__ = ("eng", "fn", "deps", "signal", "sigval", "dma_sem", "dma_val", "idx")

    def __init__(self, eng, fn):
        self.eng = eng
        self.fn = fn
        self.deps = []
        self.signal = False
        self.sigval = 0
        self.dma_sem = None
        self.dma_val = 0


class Prog:
    def __init__(self):
        self.ops = []
        self.recs = {"sb": [], "dr": [], "ps": [[b * 2048, (b + 1) * 2048, None, []] for b in range(8)]}
        self.dma_counts = {}

    @staticmethod
    def _bank(res):
        sp, lo, hi = res
        if sp != "ps":
            return res
        return (sp, lo // 2048 * 2048, (hi + 2047) // 2048 * 2048)

    def _deps_for(self, op, reads, writes):
        deps = set()
        for res in reads:
            sp, lo, hi = self._bank(res)
            for rec in self.recs[sp]:
                if rec[0] < hi and lo < rec[1]:
                    if rec[2] is not None:
                        deps.add(rec[2])
                    if sp == "ps":
                        for rd in rec[3]:
                            if rd.eng != op.eng:
                                deps.add(rd)
                    rec[3].append(op)
        for res in writes:
            sp, lo, hi = self._bank(res)
            new = []
            for rec in self.recs[sp]:
                if rec[0] < hi and lo < rec[1]:
                    if rec[2] is not None:
                        deps.add(rec[2])
                    deps.update(rec[3])
                    if rec[0] < lo:
                        new.append([rec[0], lo, rec[2], list(rec[3])])
                    if hi < rec[1]:
                        new.append([hi, rec[1], rec[2], list(rec[3])])
                else:
                    new.append(rec)
            new.append([lo, hi, op, []])
            self.recs[sp] = new
        deps.discard(op)
        return deps

    def _add(self, op, reads, writes):
        latest = {}
        for d in self._deps_for(op, reads, writes):
            if d.dma_sem is None:
                if d.eng == op.eng and op.dma_sem is None and d.eng == "pe":
                    continue
                cur = latest.get(d.eng)
                if cur is None or d.idx > cur.idx:
                    latest[d.eng] = d
            else:
                op.deps.append(d)
        for d in latest.values():
            d.signal = True
            op.deps.append(d)
        op.idx = len(self.ops)
        self.ops.append(op)
        return op

    def op(self, eng, fn, reads=(), writes=()):
        return self._add(Op(eng, fn), reads, writes)

    def dma_group(self, eng, key, fns, reads=(), writes=()):
        ops = []
        for i, fn in enumerate(fns):
            o = Op(eng, fn)
            o.dma_sem = key
            self.dma_counts[key] = self.dma_counts.get(key, 0) + 1
            self._add(o, reads, writes if i == 0 else ())
            ops.append(o)
        final = 16 * self.dma_counts[key]
        for o in ops:
            o.dma_val = final
        if len(ops) > 1:
            pass
        return ops

    def emit(self, nc, final_wait_keys=()):
        counts = {e: 0 for e in ENGS}
        for o in self.ops:
            if o.dma_sem is None and o.signal:
                counts[o.eng] += 1
                o.sigval = counts[o.eng]
        with contextlib.ExitStack() as st:
            esem = {e: st.enter_context(nc.semaphore("s_" + e)) for e in ENGS}
            dsem = {k: st.enter_context(nc.semaphore("d_%d" % i))
                    for i, k in enumerate(self.dma_counts)}
            block = st.enter_context(nc.Block())
            per = {e: [o for o in self.ops if o.eng == e] for e in ENGS}

            def run(e, h, final=False):
                seen = {}
                for o in per[e]:
                    for d in o.deps:
                        if d.dma_sem is not None:
                            s, v = dsem[d.dma_sem], d.dma_val
                        else:
                            s, v = esem[d.eng], d.sigval
                        key = id(s)
                        if seen.get(key, 0) < v:
                            h.wait_ge(s, v)
                            seen[key] = v
                    ins = o.fn(h)
                    if o.dma_sem is not None:
                        ins.then_inc(dsem[o.dma_sem], 16)
                    elif o.signal:
                        ins.then_inc(esem[e], 1)
                if final:
                    for k in final_wait_keys:
                        h.wait_ge(dsem[k], 16 * self.dma_counts[k])

            @block.tensor
            def _(h):
                run("pe", h)

            @block.scalar
            def _(h):
                run("act", h)

            @block.vector
            def _(h):
                run("dve", h)

            @block.gpsimd
            def _(h):
                run("pool", h)

            @block.sync
            def _(h):
                run("sp", h, final=True)
        return counts


class T:
    def __init__(self, ap, space, lo, esz, nelem):
        self.ap = ap
        self.space = space
        self.lo = lo
        self.esz = esz
        self.n = nelem

    def r(self, off=0, n=None):
        if n is None:
            n = self.n - off
        return (self.space, self.lo + off * self.esz, self.lo + (off + n) * self.esz)

    def __getitem__(self, idx):
        return self.ap[idx]


def MM(out, lhsT, rhs, start=True, stop=True):
    return lambda h: h.matmul(out, lhsT=lhsT, rhs=rhs, start=start, stop=stop)


def TR(out, in_, ident):
    return lambda h: h.transpose(out=out, in_=in_, identity=ident)


def ACT(out, in_, func, **kw):
    return lambda h: h.activation(out=out, in_=in_, func=func, **kw)


def TT(out, a, b, op):
    return lambda h: h.tensor_tensor(out=out, in0=a, in1=b, op=op)


def TS(out, a, s1, s2, op0, op1=None):
    if op1 is None:
        return lambda h: h.tensor_scalar(out=out, in0=a, scalar1=s1, scalar2=None, op0=op0)
    return lambda h: h.tensor_scalar(out=out, in0=a, scalar1=s1, scalar2=s2, op0=op0, op1=op1)


def STT(out, in0, scalar, in1, op0, op1):
    return lambda h: h.scalar_tensor_tensor(out=out, in0=in0, scalar=scalar, in1=in1, op0=op0, op1=op1)


def CP(out, in_):
    return lambda h: h.tensor_copy(out=out, in_=in_)


def MS(out, val):
    return lambda h: h.memset(out, val)


def DMA(out, in_, **kw):
    return lambda h: h.dma_start(out=out, in_=in_, **kw)


C_ADAB = 0
C_PREMIX = 48
C_POSTMIX = 56
C_PREFF = 64
C_POSTFF = 72
C_LBLOG = 80
C_HGG = 84
C_QG = 85
C_KG = 86
C_AOG = 87
C_PER_LAYER = 88
NCOLS = NL * C_PER_LAYER
CB_IDENT = 0
CB_PROT = 128
CB_MASKF = 256
CB_MASKB = 320
CB_SCAN = 384
NCB = 384 + 1024

ARENA_BYTES = 212480


class _Stop(Exception):
    pass


def build(nc, stop=None, dbg=False):
    P = Prog()
    st = contextlib.ExitStack()

    def chk(name):
        if stop == name:
            raise _Stop()

    def dram(name, shape, kind="ExternalInput"):
        return nc.dram_tensor(name, list(shape), F32, kind=kind).ap()

    xT_d = dram("xT", [128, 8, S])
    cT_d = dram("cT", [128, 8])
    cols_d = dram("cols", [128, NCOLS])
    cbf_d = dram("cbf", [128, NCB])
    cos_d = dram("cosT", [128, S])
    sin_d = dram("sinT", [128, S])
    wada_d = dram("wada", [48, 128, 1024])
    win_d = dram("win", [NL, 27, 128, 1024])
    wout_d = dram("wout", [NL, 8, 128, 1024])
    w1_d = dram("w1", [NL, 8, 128, 4096])
    w2_d = dram("w2", [NL, 8, 128, 4096])
    out_d = dram("outT", [128, 8, S], kind="ExternalOutput")
    w1s = nc.dram_tensor("w1s", [NL, 8, 128, 4096], BF16).ap()
    w2s = nc.dram_tensor("w2s", [NL, 8, 128, 4096], BF16).ap()

    def dres(which, l, jg, n=1):
        base = ((which * NL + l) * 8 + jg) * 4096
        return ("dr", base, base + n * 4096)

    def convert_ffn(l, which, q):
        src, dst = (w1_d, w1s) if which == 0 else (w2_d, w2s)
        fns = [DMA(dst[l, jg].rearrange("p (a b) -> p a b", b=1024),
                   src[l, jg].rearrange("p (a b) -> p a b", b=1024)) for jg in range(q * 2, q * 2 + 2)]
        P.dma_group("pool", ("cv", l, which, q), fns, writes=[dres(which, l, q * 2, 2)])
    if dbg:
        dbg_h = dram("dbg_hT", [128, 8, S], kind="ExternalOutput")
        dbg_y = dram("dbg_ycat", [128, 4, S], kind="ExternalOutput")
        dbg_c = dram("dbg_dcol", [128, 512], kind="ExternalOutput")

    arena = st.enter_context(nc.sbuf_tensor("arena", [128, ARENA_BYTES // 4], F32))
    psum = st.enter_context(nc.psum_tensor("psum", [128, 8, 512], F32))
    psflat = psum[:].rearrange("p a b -> p (a b)")

    def sb(lo, shape, dt=F32):
        esz = 4 if dt == F32 else 2
        n = int(np.prod(shape))
        assert lo % 4 == 0 and lo + n * esz <= ARENA_BYTES, (lo, shape)
        ap = arena[:, lo // 4:(lo + n * esz + 3) // 4]
        if dt != F32:
            ap = ap.bitcast(dt)[:, 0:n]
        if len(shape) == 2:
            ap = ap.rearrange("p (a b) -> p a b", b=shape[1])
        elif len(shape) == 3:
            ap = ap.rearrange("p (a b c) -> p a b c", b=shape[1], c=shape[2])
        return T(ap, "sb", lo, esz, n)

    def ps(off, n, dt=F32):
        ap = psflat[:, off:off + n]
        if dt != F32:
            ap = ap.bitcast(dt)
            return T(ap, "ps", off * 4, 2, n * 2)
        return T(ap, "ps", off * 4, 4, n)

    xT = sb(0, (8, S))
    hT = sb(65536, (8, S), BF16)
    CB = 98304
    colsT = sb(CB, (NCOLS,))
    dcol = sb(CB + 1024, (512,))
    cbf = sb(CB + 3072, (NCB,), BF16)
    mats = sb(CB + 6144, (5, 128), BF16)
    condc = sb(CB + 7680, (48,))
    sc_c = sb(CB + 7936, (8,))
    YH = CB + 12288
    ycat_h = sb(YH, (4, S), BF16)
    PH = YH + 16384

    ident = cbf[:, CB_IDENT:CB_IDENT + 128]
    prot = cbf[:, CB_PROT:CB_PROT + 128]
    maskF = cbf[:, CB_MASKF:CB_MASKF + 64]
    maskB = cbf[:, CB_MASKB:CB_MASKB + 64]
    scanm = cbf[:, CB_SCAN:CB_SCAN + 1024]
    bd64 = mats[:, 0, :]
    ones128 = mats[:, 1, :]
    ones1024 = mats[:, 2, :]
    wvE = mats[:, 3, :]
    wvO = mats[:, 4, :]

    DC_EPS = 0
    DC_G1 = [8 + l * 64 for l in range(NL)]
    DC_SH1 = [16 + l * 64 for l in range(NL)]
    DC_GG1 = [24 + l * 64 for l in range(NL)]
    DC_G2 = [32 + l * 64 for l in range(NL)]
    DC_SH2 = [40 + l * 64 for l in range(NL)]
    DC_GG2 = [48 + l * 64 for l in range(NL)]
    DC_LB = [56 + l * 64 for l in range(NL)]
    DC_OML = [60 + l * 64 for l in range(NL)]
    DC_NOML = [64 + l * 64 for l in range(NL)]
    DC_QG8 = [68 + l * 64 for l in range(NL)]
    DC_TMP = 200

    def dc(i, n=1):
        return dcol[:, i:i + n]

    def cl(l, i, n=1):
        return colsT[:, l * C_PER_LAYER + i:l * C_PER_LAYER + i + n]

    P.dma_group("sp", "cols", [DMA(colsT[:], cols_d)], writes=[colsT.r()])
    P.dma_group("pool", "cbf", [DMA(cbf[:], cbf_d)], writes=[cbf.r()])
    P.op("dve", MS(dcol[:], 0.0), writes=[dcol.r()])
    P.op("dve", MS(dc(DC_EPS), EPS), writes=[dcol.r()])
    P.op("dve", MS(mats[:, 0, :], 0.0), writes=[mats.r()])
    P.op("dve", MS(mats[0:64, 0, 0:64], 1.0 / 64), writes=[mats.r()])
    P.op("dve", MS(mats[64:128, 0, 64:128], 1.0 / 64), writes=[mats.r()])
    P.op("dve", MS(mats[:, 1, :], 1.0 / 128), writes=[mats.r()])
    P.op("dve", MS(mats[:, 2, :], 1.0 / 1024), writes=[mats.r()])
    P.op("dve", MS(mats[0:64, 3, :], 1.0 / 64), writes=[mats.r()])
    P.op("dve", MS(mats[64:128, 3, :], EPS / 64), writes=[mats.r()])
    P.op("dve", MS(mats[0:64, 4, :], EPS / 64), writes=[mats.r()])
    P.op("dve", MS(mats[64:128, 4, :], 1.0 / 64), writes=[mats.r()])

    for kc in range(8):
        P.dma_group("sp", ("x", kc), [DMA(xT[:, kc, :], xT_d[:, kc, :])], writes=[xT.r(kc * S, S)])

    cst = sb(PH, (8,))
    P.dma_group("sp", "cT", [DMA(cst[:], cT_d)], writes=[cst.r()])
    P.op("act", ACT(sc_c[:], cst[:], AF.Silu), reads=[cst.r()], writes=[sc_c.r()])
    wst = [sb(PH + 1024 + i * 16384, (4, 1024), BF16) for i in range(2)]
    sc_b = sb(PH + 512, (8,), BF16)
    P.op("dve", CP(sc_b[:], sc_c[:]), reads=[sc_c.r()], writes=[sc_b.r()])
    pcond = ps(0, 48)
    for g in range(4):
        w = wst[g % 2]
        P.dma_group("pool", ("wada", g % 2), [DMA(w[:], wada_d[g * 4:(g + 1) * 4].rearrange("c p n -> p c n"))],
                    writes=[w.r()])
        for cc in range(4):
            j = g * 4 + cc
            for kc in range(8):
                P.op("pe", MM(pcond[:, j:j + 1], w[:, cc, kc * 128:(kc + 1) * 128], sc_b[:, kc:kc + 1],
                              start=(kc == 0), stop=(kc == 7)),
                     reads=[w.r(), sc_b.r()], writes=[pcond.r()])
    P.op("dve", CP(condc[:, 0:16], pcond[:, 0:16]), reads=[pcond.r()], writes=[condc.r()])
    DC_MODE = 256
    DC_MOD = [288 + l * 48 for l in range(NL)]
    mode = dcol[:, DC_MODE:DC_MODE + 16]
    P.op("dve", TT(mode, condc[:, 0:16], cl(0, C_ADAB, 16), ALU.add),
         reads=[condc.r(), colsT.r()], writes=[dcol.r()])
    P.op("dve", STT(dc(DC_G1[0], 8), mode[:, 8:16], 1.0, cl(0, C_PREMIX, 8), ALU.add, ALU.mult),
         reads=[dcol.r(), colsT.r()], writes=[dcol.r()])
    P.op("dve", CP(dc(DC_SH1[0], 8), mode[:, 0:8]), reads=[dcol.r()], writes=[dcol.r()])
    for l in range(NL):
        P.op("dve", TS(dc(DC_QG8[l]), cl(l, C_QG), 0.125, None, ALU.mult), reads=[colsT.r()], writes=[dcol.r()])

    st2 = [sb(YH + 8192 + i * 2048, (1024,), BF16) for i in range(4)]
    sc_b2 = sb(CB + 7936 + 64, (8,), BF16)
    P.op("dve", CP(sc_b2[:], sc_c[:]), reads=[sc_c.r()], writes=[sc_b2.r()])
    pcond2 = ps(4 * 512, 32)

    def ada_chunk(j):
        w = st2[j % 4]
        P.dma_group("pool", ("wada2", j % 4), [DMA(w[:], wada_d[j])], writes=[w.r()])
        for kc in range(8):
            P.op("pe", MM(pcond2[:, j - 16:j - 15], w[:, kc * 128:(kc + 1) * 128], sc_b2[:, kc:kc + 1],
                          start=(kc == 0), stop=(kc == 7)),
                 reads=[w.r(), sc_b2.r()], writes=[pcond2.r()])

    def ada_late():
        P.op("dve", CP(condc[:, 16:48], pcond2[:]), reads=[pcond2.r()], writes=[condc.r()])
        for l in range(NL):
            modc = dcol[:, DC_MOD[l]:DC_MOD[l] + 48]
            rw = dict(reads=[condc.r(), colsT.r(), dcol.r()], writes=[dcol.r()])
            P.op("dve", TT(modc, condc[:], cl(l, C_ADAB, 48), ALU.add), **rw)
            for (dst, sci, gi) in ((DC_G1[l], 8, C_PREMIX), (DC_G2[l], 32, C_PREFF)):
                if l == 0 and sci == 8:
                    continue
                P.op("dve", STT(dc(dst, 8), modc[:, sci:sci + 8], 1.0, cl(l, gi, 8), ALU.add, ALU.mult), **rw)
            for (dst, si) in ((DC_SH1[l], 0), (DC_SH2[l], 24)):
                if l == 0 and si == 0:
                    continue
                P.op("dve", CP(dc(dst, 8), modc[:, si:si + 8]), **rw)
            for (dst, gi, pi) in ((DC_GG1[l], 16, C_POSTMIX), (DC_GG2[l], 40, C_POSTFF)):
                P.op("dve", TT(dc(dst, 8), modc[:, gi:gi + 8], cl(l, pi, 8), ALU.mult), **rw)

    l0 = cl(0, C_LBLOG, 4)
    l1 = cl(1, C_LBLOG, 4)
    t = lambda i: dcol[:, DC_TMP + 4 * i:DC_TMP + 4 * i + 4]
    rd = [colsT.r(), dcol.r()]
    P.op("dve", TT(t(0), l0, l1, ALU.max), reads=rd, writes=[dcol.r()])
    P.op("dve", TT(t(1), l0, t(0), ALU.subtract), reads=rd, writes=[dcol.r()])
    P.op("dve", TT(t(2), l1, t(0), ALU.subtract), reads=rd, writes=[dcol.r()])
    P.op("act", ACT(t(1), t(1), AF.Exp), reads=rd, writes=[dcol.r()])
    P.op("act", ACT(t(2), t(2), AF.Exp), reads=rd, writes=[dcol.r()])
    P.op("dve", TT(t(3), t(1), t(2), ALU.add), reads=rd, writes=[dcol.r()])
    P.op("dve", lambda h: h.reciprocal(out=t(3), in_=t(3)), reads=rd, writes=[dcol.r()])
    P.op("dve", TT(t(1), t(1), t(3), ALU.mult), reads=rd, writes=[dcol.r()])
    P.op("dve", TT(t(2), t(2), t(3), ALU.mult), reads=rd, writes=[dcol.r()])
    P.op("dve", TT(dc(DC_LB[0], 4), t(1), t(1), ALU.subtract), reads=rd, writes=[dcol.r()])
    P.op("dve", TT(t(0), t(1), t(2), ALU.add), reads=rd, writes=[dcol.r()])
    P.op("dve", TT(dc(DC_LB[1], 4), t(0), t(1), ALU.subtract), reads=rd, writes=[dcol.r()])
    for l in range(NL):
        P.op("dve", TS(dc(DC_OML[l], 4), dc(DC_LB[l], 4), -1.0, 1.0, ALU.mult, ALU.add), reads=rd, writes=[dcol.r()])
        P.op("dve", TS(dc(DC_NOML[l], 4), dc(DC_LB[l], 4), -1.0, None, ALU.add), reads=rd, writes=[dcol.r()])

    eps_c = dc(DC_EPS)

    def rstd_from(den, out_t, use_eps=True):
        if use_eps:
            P.op("act", ACT(out_t[:], den[:], AF.Ln, bias=eps_c), reads=[den.r(), dcol.r()], writes=[out_t.r()])
        else:
            P.op("act", ACT(out_t[:], den[:], AF.Ln), reads=[den.r()], writes=[out_t.r()])
        P.op("act", ACT(out_t[:], out_t[:], AF.Exp, scale=-0.5), reads=[out_t.r()], writes=[out_t.r()])

    def prenorm(tile, gcol0, shcol0, base, den_bank):
        sq = [sb(base + i * 1024, (512,), BF16) for i in range(2)]
        rs = sb(base + 2048, (512,))
        tm = [sb(base + 4096 + i * 2048, (512,)) for i in range(2)]
        den = ps(den_bank * 512, 512)
        for kc in range(8):
            s_ = sq[kc % 2]
            xr = xT.r(kc * S + tile * 512, 512)
            P.op("act", ACT(s_[:], xT[:, kc, tile * 512:(tile + 1) * 512], AF.Square), reads=[xr], writes=[s_.r()])
            P.op("pe", MM(den[:], ones1024, s_[:], start=(kc == 0), stop=(kc == 7)),
                 reads=[s_.r(), mats.r()], writes=[den.r()])
        rstd_from(den, rs)
        for kc in range(8):
            t_ = tm[kc % 2]
            xr = xT.r(kc * S + tile * 512, 512)
            hr = hT.r(kc * S + tile * 512, 512)
            P.op("dve", TT(t_[:], xT[:, kc, tile * 512:(tile + 1) * 512], rs[:], ALU.mult),
                 reads=[xr, rs.r()], writes=[t_.r()])
            P.op("dve", TS(hT[:, kc, tile * 512:(tile + 1) * 512], t_[:], dc(gcol0 + kc), dc(shcol0 + kc),
                           ALU.mult, ALU.add),
                 reads=[t_.r(), dcol.r()], writes=[hr])

    def residual(tile, ytmp, ggcol0, base, den_bank, sq_from_ytmp_done_den=None):
        rs = sb(base, (512,))
        rstd_from(sq_from_ytmp_done_den, rs)
        for m in range(8):
            yr = ytmp.r(m * 512, 512)
            xr = xT.r(m * S + tile * 512, 512)
            P.op("dve", TT(ytmp[:, m, :], ytmp[:, m, :], rs[:], ALU.mult), reads=[yr, rs.r()], writes=[yr])
            P.op("dve", STT(xT[:, m, tile * 512:(tile + 1) * 512], ytmp[:, m, :], dc(ggcol0 + m),
                            xT[:, m, tile * 512:(tile + 1) * 512], ALU.mult, ALU.add),
                 reads=[yr, xr, dcol.r()], writes=[xr])

    def load_w(eng, key, dst, src, reads=()):
        flat = dst.ap
        if len(dst.ap.shape) == 3:
            flat = dst.ap.rearrange("p a b -> p (a b)")
        n = flat.shape[1]
        if n > 1024:
            flat = flat.rearrange("p (a b) -> p a b", b=1024)
            src = src.rearrange("p (a b) -> p a b", b=1024)
        P.dma_group(eng, key, [DMA(flat, src)], writes=[dst.r()])

    def body():
      chk("setup")
      for tile in range(NT):
        prenorm(tile, DC_G1[0], DC_SH1[0], PH + 45056, 1)
      chk("pre0")
      main()

    ada_next = [16]

    def main():
     for l in range(NL):
        o = PH
        QmT = [sb(o + d * 4096, (S,), BF16) for d in range(2)]; o += 8192
        KmT = [sb(o + d * 4096, (S,), BF16) for d in range(2)]; o += 8192
        Kmt = [sb(o + d * 4096, (16, 128), BF16) for d in range(2)]; o += 8192
        Vtok = sb(o, (16, 128), BF16); o += 4096
        gs = sb(o, (S,), BF16); o += 4096
        qs = sb(o, (S,)); o += 8192
        osum = qs
        TS_ = []
        for d in range(2):
            TS_.append([sb(o + i * 4096, (1024,)) for i in range(3)]); o += 12288
        T1, T2, T3 = TS_[0]
        KdTs = [sb(o + d * 2048, (1024,), BF16) for d in range(2)]; o += 4096
        em = [sb(o + d * 128, (32,)) for d in range(2)]; o += 256
        ea = [sb(o + d * 128, (32,)) for d in range(2)]; o += 256
        ec = [sb(o + d * 128, (32,)) for d in range(2)]; o += 256
        dls = [sb(o + d * 64, (16,)) for d in range(2)]; o += 128
        Sst = [sb(o + d * 512, (128,)) for d in range(2)]; o += 1024
        Ub = [[sb(o + (d * 2 + b) * 256, (128,), BF16) for b in range(2)] for d in range(2)]; o += 1024
        ATb = [[sb(o + (d * 2 + b) * 128, (64,), BF16) for b in range(2)] for d in range(2)]; o += 512
        KAb = [[sb(o + (d * 2 + b) * 128, (64,), BF16) for b in range(2)] for d in range(2)]; o += 512
        KBb = [[sb(o + (d * 2 + b) * 128, (64,), BF16) for b in range(2)] for d in range(2)]; o += 512
        o = (o + 1023) // 1024 * 1024
        wH = [[sb(o + c * 2048, (8, 128), BF16) for c in range(5)]] * 2; o += 10240
        assert o <= ARENA_BYTES, o
        pbuf = [ps(0, 1024), ps(1024, 1024)]
        pch = [dict(sc=[ps(0, 64), ps(512, 64)], o=ps(4 * 512, 64), dS=ps(6 * 512, 128)),
               dict(sc=[ps(2 * 512, 64), ps(3 * 512, 64)], o=ps(5 * 512, 64), dS=ps(7 * 512, 128))]
        ptr = [ps(6 * 512, 512, BF16), ps(7 * 512, 512, BF16)]
        pv = [ps((4 + i) * 512, 128) for i in range(4)]
        pden = ps(0, 512)

        def load_head_w(h):
            for ci, chunk in enumerate((h, 4 + h, 8 + h, 12 + h, 16 + h)):
                load_w("pool", ("wH", h % 2, ci), wH[h % 2][ci], win_d[l, chunk])

        def proj_fm(wt, half, pb):
            for t2 in range(2):
                tile = half * 2 + t2
                for kc in range(8):
                    P.op("pe", MM(pb[:, t2 * 512:(t2 + 1) * 512], wt[:, kc, :], hT[:, kc, tile * 512:(tile + 1) * 512],
                                  start=(kc == 0), stop=(kc == 7)),
                         reads=[wt.r(), hT.r(kc * S + tile * 512, 512)], writes=[pb.r(t2 * 512, 512)])

        load_head_w(0)
        for d in range(2):
            for b in range(2):
                P.op("dve", MS(KAb[d][b][:], 0.0), writes=[KAb[d][b].r()])
                P.op("dve", MS(KBb[d][b][:], 0.0), writes=[KBb[d][b].r()])
        pbi = 0
        for h in range(4):
            wq, wff, wfb, wi, wg = wH[h % 2]
            lbc = dc(DC_LB[l] + h)
            omlc = dc(DC_OML[l] + h)
            nomlc = dc(DC_NOML[l] + h)
            for half in range(2):
                pb = pbuf[pbi % 2]; pbi += 1
                proj_fm(wq, half, pb)
                P.op("act", ACT(qs[:, half * 1024:(half + 1) * 1024], pb[:], AF.Silu),
                     reads=[pb.r()], writes=[qs.r(half * 1024, 1024)])
            for half in range(2):
                pb = pbuf[pbi % 2]; pbi += 1
                proj_fm(wg, half, pb)
                P.op("act", ACT(gs[:, half * 1024:(half + 1) * 1024], pb[:], AF.Silu),
                     reads=[pb.r()], writes=[gs.r(half * 1024, 1024)])
            chk("H%d.%dq" % (l, h))
            for tt in range(16):
                pvt = pv[tt % 4]
                tile = tt // 4
                for kc in range(8):
                    P.op("pe", MM(pvt[:], hT[:, kc, tt * 128:(tt + 1) * 128], wi[:, kc, :],
                                  start=(kc == 0), stop=(kc == 7)),
                         reads=[wi.r(), hT.r(kc * S + tile * 512, 512)], writes=[pvt.r()])
                eng = "act" if tt % 2 == 0 else "dve"
                fn = ACT(Vtok[:, tt, :], pvt[:], AF.Copy) if eng == "act" else CP(Vtok[:, tt, :], pvt[:])
                P.op(eng, fn, reads=[pvt.r()], writes=[Vtok.r(tt * 128, 128)])
            chk("H%d.%dv" % (l, h))
            def gate_stages(d, half):
                wf = wff if d == 0 else wfb
                mi = 31 if d == 0 else 32
                li = 63 if d == 0 else 0
                T1, T2, T3 = TS_[d]
                KdT = KdTs[d]
                dl = dls[d]
                pb = pbuf[d]
                ptrd = ptr[d]
                hs = slice(half * 1024, (half + 1) * 1024)
                b3 = T3[:].rearrange("p (c j) -> p c j", j=64)
                a3 = T1[:].rearrange("p (c j) -> p c j", j=64)
                cs = slice(half * 16, (half + 1) * 16)
                k3 = KmT[d][:, hs].rearrange("p (c j) -> p c j", j=64)
                kd3 = KdT[:].rearrange("p (c j) -> p c j", j=64)
                L = []
                L.append(lambda: proj_fm(wf, half, pb))
                L.append(lambda: P.op("act", ACT(T1[:], pb[:], AF.Sigmoid), reads=[pb.r()], writes=[T1.r()]))

                def s_k():
                    P.op("dve", TS(T2[:], T1[:], nomlc, omlc, ALU.mult, ALU.add),
                         reads=[T1.r(), dcol.r()], writes=[T2.r()])
                    P.op("act", ACT(T1[:], T1[:], AF.Ln, scale=omlc, bias=lbc),
                         reads=[T1.r(), dcol.r()], writes=[T1.r()])
                L.append(s_k)

                def s_scan():
                    if d == 0:
                        P.op("dve", lambda hh: hh.tensor_tensor_scan(out=T3[:], data0=scanm, data1=T1[:], initial=0.0,
                                                                     op0=ALU.mult, op1=ALU.add),
                             reads=[T1.r(), cbf.r()], writes=[T3.r()])
                    else:
                        P.op("dve", lambda hh: hh.tensor_tensor_scan(out=T3[:, ::-1], data0=scanm, data1=T1[:, ::-1],
                                                                     initial=0.0, op0=ALU.mult, op1=ALU.add),
                             reads=[T1.r(), cbf.r()], writes=[T3.r()])
                L.append(s_scan)

                def s_small():
                    P.op("act", ACT(em[d][:, cs], b3[:, :, mi], AF.Exp), reads=[T3.r()], writes=[em[d].r()])
                    P.op("act", ACT(ea[d][:, cs], b3[:, :, li], AF.Exp), reads=[T3.r()], writes=[ea[d].r()])
                    P.op("dve", TT(dl[:], b3[:, :, li], b3[:, :, mi], ALU.subtract), reads=[T3.r()], writes=[dl.r()])
                    P.op("dve", TT(a3, b3, b3[:, :, mi:mi + 1].to_broadcast([128, 16, 64]), ALU.subtract),
                         reads=[T3.r()], writes=[T1.r()])
                    P.op("dve", TS(T1[:], T1[:], 80.0, -80.0, ALU.min, ALU.max), reads=[T1.r()], writes=[T1.r()])
                    P.op("act", ACT(ec[d][:, cs], dl[:], AF.Exp), reads=[dl.r()], writes=[ec[d].r()])
                L.append(s_small)

                def s_e1():
                    P.op("act", ACT(T3[:], T1[:], AF.Exp), reads=[T1.r()], writes=[T3.r()])
                L.append(s_e1)

                def s_q():
                    P.op("dve", TT(QmT[d][:, hs], qs[:, hs], T3[:], ALU.mult),
                         reads=[qs.r(half * 1024, 1024), T3.r()], writes=[QmT[d].r(half * 1024, 1024)])
                    P.op("act", ACT(T1[:], T1[:], AF.Exp, scale=-1.0), reads=[T1.r()], writes=[T1.r()])
                L.append(s_q)

                def s_km():
                    P.op("dve", TT(KmT[d][:, hs], T2[:], T1[:], ALU.mult),
                         reads=[T2.r(), T1.r()], writes=[KmT[d].r(half * 1024, 1024)])
                    P.op("dve", TT(kd3, k3, ec[d][:, cs].unsqueeze(2).to_broadcast([128, 16, 64]), ALU.mult),
                         reads=[KmT[d].r(half * 1024, 1024), ec[d].r()], writes=[KdT.r()])
                L.append(s_km)

                def s_tr(q4):
                    def f():
                        for t4 in range(4):
                            tt = q4 * 4 + t4
                            P.op("pe", TR(ptrd[:, t4 * 128:(t4 + 1) * 128], KdT[:, tt * 128:(tt + 1) * 128], ident),
                                 reads=[KdT.r(), cbf.r()], writes=[ptrd.r(t4 * 128, 128)])
                        g0 = half * 8 + q4 * 4
                        P.op("act", ACT(Kmt[d][:, g0:g0 + 4, :].rearrange("p a b -> p (a b)"),
                                        ptrd[:, 0:512], AF.Copy),
                             reads=[ptrd.r(0, 512)], writes=[Kmt[d].r(g0 * 128, 512)])
                    return f
                L.append(s_tr(0))
                L.append(s_tr(1))
                return L

            for half in range(2):
                A_ = gate_stages(0, half)
                B_ = gate_stages(1, half)
                for k in range(len(A_) + 1):
                    if k < len(A_):
                        A_[k]()
                    if k >= 1:
                        B_[k - 1]()
                    if l == 0 and h == 0:
                        for _ in range(2 if ada_next[0] < 16 + 20 else 1):
                            if ada_next[0] < 48:
                                ada_chunk(ada_next[0])
                                ada_next[0] += 1
            if l == 0 and h == 0:
                while ada_next[0] < 48:
                    ada_chunk(ada_next[0])
                    ada_next[0] += 1
                ada_late()
            if h + 1 < 4:
                load_head_w(h + 1)
            chk("H%d.%dg1" % (l, h))
            for d in range(2):
                P.op("dve", MS(Sst[d][:], 0.0), writes=[Sst[d].r()])
                P.op("dve", MS(Ub[d][0][:], 0.0), writes=[Ub[d][0].r()])
            def chain_step(s_, d):
                c = s_ if d == 0 else 31 - s_
                par = (c % 2) * 64
                tl = c // 2
                pr = slice(par, par + 64)
                ck = slice(c * 64, (c + 1) * 64)
                psc = pch[d]["sc"][s_ % 2]
                po = pch[d]["o"]
                pdS = pch[d]["dS"]
                at = ATb[d][s_ % 2]
                ucur = Ub[d][s_ % 2]
                unxt = Ub[d][(s_ + 1) % 2]
                msk = maskF if d == 0 else maskB
                ka = KAb[d][s_ % 2]
                kb = KBb[d][s_ % 2]
                cA = slice(c * 64, c * 64 + 32)
                cB = slice(c * 64 + 32, c * 64 + 64)

                def kakb():
                    P.op("pool", CP(ka[:, 0:32], KmT[d][:, cA]), reads=[KmT[d].r(c * 64, 64)], writes=[ka.r()])
                    P.op("pool", CP(kb[:, 32:64], KmT[d][:, cB]), reads=[KmT[d].r(c * 64, 64)], writes=[kb.r()])

                def sc():
                    if d == 0:
                        P.op("pe", MM(psc[pr, :], ka[:], QmT[d][:, ck], start=True, stop=False),
                             reads=[ka.r(), QmT[d].r(c * 64, 64)], writes=[psc.r()])
                        P.op("pe", MM(psc[pr, 32:64], kb[:], QmT[d][:, cB], start=False, stop=True),
                             reads=[kb.r(), QmT[d].r(c * 64, 64)], writes=[psc.r()])
                    else:
                        P.op("pe", MM(psc[pr, :], kb[:], QmT[d][:, ck], start=True, stop=False),
                             reads=[kb.r(), QmT[d].r(c * 64, 64)], writes=[psc.r()])
                        P.op("pe", MM(psc[pr, 0:32], ka[:], QmT[d][:, cA], start=False, stop=True),
                             reads=[ka.r(), QmT[d].r(c * 64, 64)], writes=[psc.r()])

                def mask():
                    P.op("dve", TT(at[pr, :], psc[pr, :], msk[pr, :], ALU.mult),
                         reads=[psc.r(), cbf.r()], writes=[at.r()])

                def om():
                    P.op("pe", MM(po[:], Vtok[pr, tl, :], at[pr, :], start=True, stop=False),
                         reads=[Vtok.r(tl * 128, 128), at.r()], writes=[po.r()])
                    P.op("pe", MM(po[:], ucur[:], QmT[d][:, ck], start=False, stop=True),
                         reads=[ucur.r(), QmT[d].r(c * 64, 64)], writes=[po.r()])
                    first = (d == 0 and c <= 15) or (d == 1 and c >= 16)
                    if first:
                        P.op("act", ACT(osum[:, ck], po[:], AF.Copy), reads=[po.r()], writes=[osum.r(c * 64, 64)])
                    else:
                        P.op("dve", TT(osum[:, ck], po[:], osum[:, ck], ALU.add),
                             reads=[po.r(), osum.r(c * 64, 64)], writes=[osum.r(c * 64, 64)])

                def dS():
                    if s_ < 31:
                        P.op("pe", MM(pdS[:], Kmt[d][pr, tl, :], Vtok[pr, tl, :]),
                             reads=[Kmt[d].r(tl * 128, 128), Vtok.r(tl * 128, 128)], writes=[pdS.r()])

                def upd():
                    if s_ < 31:
                        cn = c + 1 if d == 0 else c - 1
                        P.op("dve", STT(Sst[d][:], Sst[d][:], ea[d][:, c:c + 1], pdS[:], ALU.mult, ALU.add),
                             reads=[Sst[d].r(), ea[d].r(), pdS.r()], writes=[Sst[d].r()])
                        P.op("act", ACT(unxt[:], Sst[d][:], AF.Identity, scale=em[d][:, cn:cn + 1]),
                             reads=[Sst[d].r(), em[d].r()], writes=[unxt.r()])

                return dict(kakb=kakb, sc=sc, mask=mask, om=om, dS=dS, upd=upd)

            steps = [[chain_step(s_, d) for d in range(2)] for s_ in range(32)]
            for it in range(-2, 32):
                for d in range(2):
                    if 0 <= it + 2 < 32:
                        steps[it + 2][d]["kakb"]()
                for d in range(2):
                    if 0 <= it + 1 < 32:
                        steps[it + 1][d]["sc"]()
                for d in range(2):
                    if 0 <= it < 32:
                        steps[it][d]["dS"]()
                for d in range(2):
                    if 0 <= it + 1 < 32:
                        steps[it + 1][d]["mask"]()
                for d in range(2):
                    if 0 <= it < 32:
                        steps[it][d]["upd"]()
                for d in range(2):
                    if 0 <= it < 32:
                        steps[it][d]["om"]()
            chk("H%d.%dc" % (l, h))
            base_ = TS_[0][0].lo
            sqbs = [sb(base_ + t_ * 1024, (512,), BF16) for t_ in range(4)]
            rss = [sb(base_ + 4096 + t_ * 2048, (512,)) for t_ in range(4)]
            tms = [sb(base_ + 12288 + t_ * 2048, (512,)) for t_ in range(4)]
            pdens = [ps(t_ * 512, 512) for t_ in range(4)]
            for tile in range(NT):
                ts_ = slice(tile * 512, (tile + 1) * 512)
                P.op("act", ACT(sqbs[tile][:], osum[:, ts_], AF.Square), reads=[osum.r(tile * 512, 512)],
                     writes=[sqbs[tile].r()])
            for tile in range(NT):
                P.op("pe", MM(pdens[tile][:], ones128, sqbs[tile][:]), reads=[sqbs[tile].r(), mats.r()],
                     writes=[pdens[tile].r()])
            for tile in range(NT):
                P.op("act", ACT(rss[tile][:], pdens[tile][:], AF.Ln, bias=eps_c), reads=[pdens[tile].r(), dcol.r()],
                     writes=[rss[tile].r()])
            for tile in range(NT):
                P.op("act", ACT(rss[tile][:], rss[tile][:], AF.Exp, scale=-0.5), reads=[rss[tile].r()],
                     writes=[rss[tile].r()])
            for tile in range(NT):
                ts_ = slice(tile * 512, (tile + 1) * 512)
                P.op("dve", TT(tms[tile][:], osum[:, ts_], rss[tile][:], ALU.mult),
                     reads=[osum.r(tile * 512, 512), rss[tile].r()], writes=[tms[tile].r()])
                P.op("dve", STT(ycat_h[:, h, ts_], tms[tile][:], cl(l, C_HGG), gs[:, ts_], ALU.mult, ALU.mult),
                     reads=[tms[tile].r(), colsT.r(), gs.r(tile * 512, 512)],
                     writes=[ycat_h.r(h * S + tile * 512, 512)])

            chk("H%d.%d" % (l, h))
        chk("H%d" % l)
        o = PH
        kTz = [[sb(o + (g * 2 + e) * 4096, (S,), BF16) for e in range(2)] for g in range(2)]; o += 16384
        Vaug = [[sb(o + (g * 2 + e) * 4096, (16, 128), BF16) for e in range(2)] for g in range(2)]; o += 16384
        cosT = sb(o, (S,), BF16); o += 4096
        sinT = sb(o, (S,), BF16); o += 4096
        wA = [sb(o + i * 2048, (8, 128), BF16) for i in range(4)]; o += 8192
        wO = [sb(o + i * 2048, (8, 128), BF16) for i in range(2)]; o += 4096
        qT = [sb(o + i * 1024, (512,), BF16) for i in range(2)]; o += 2048
        PT = [sb(o + i * 2048, (1024,), BF16) for i in range(2)]; o += 4096
        ov = o
        zsb = sb(o, (512,)); o += 2048
        knb = sb(o, (512,), BF16); o += 1024
        t1a = sb(o, (512,)); o += 2048
        t2a = sb(o, (512,)); o += 2048
        sqq = sb(o, (512,), BF16); o += 1024
        rsq = sb(o, (512,)); o += 2048
        assert o - ov <= 16384
        ytmp = sb(ov, (8, 512))
        pnb = ov
        o = ov + 16384
        sqa = [sb(o + i * 1024, (512,), BF16) for i in range(2)]; o += 2048
        rsa = sb(o, (512,)); o += 2048
        ycat_a = sb(o, (4, 512), BF16); o += 4096
        assert o <= ARENA_BYTES, o
        pz = [ps(0, 512), ps(512, 512)]
        pdn = pz[0]
        pSp = [ps(2 * 512, 1024), ps(4 * 512, 1024)]
        pPV = [ps((6 + i) * 512, 512) for i in range(2)]

        P.dma_group("pool", "cos", [DMA(cosT[:].rearrange("p (a b) -> p a b", b=1024),
                                        cos_d.rearrange("p (a b) -> p a b", b=1024))], writes=[cosT.r()])
        P.dma_group("pool", "sin", [DMA(sinT[:].rearrange("p (a b) -> p a b", b=1024),
                                        sin_d.rearrange("p (a b) -> p a b", b=1024))], writes=[sinT.r()])
        for g in range(2):
            P.op("dve", MS(kTz[g][0][64:128, :], 0.0), writes=[kTz[g][0].r()])
            P.op("dve", MS(kTz[g][1][0:64, :], 0.0), writes=[kTz[g][1].r()])
        load_w("pool", ("wA", 0), wA[0], win_d[l, 24])
        load_w("pool", ("wA", 1), wA[1], win_d[l, 25])
        load_w("pool", ("wA", 2), wA[2], win_d[l, 26])
        for g in range(2):
            P.op("dve", MS(Vaug[g][0][:, :, 64:128], 1.0), writes=[Vaug[g][0].r()])
            P.op("dve", MS(Vaug[g][1][:, :, 0:64], 1.0), writes=[Vaug[g][1].r()])

        scrA = dict(zsb=zsb, knb=knb, t1a=t1a, t2a=t2a, sqq=sqq, rsq=rsq)
        scrB = dict(zsb=sb(qT[0].lo, (512,)), t1a=sb(qT[0].lo + 2048, (512,)), t2a=sb(qT[0].lo + 4096, (512,)),
                    knb=sb(sqa[0].lo, (512,), BF16), sqq=sb(sqa[0].lo + 1024, (512,), BF16),
                    rsq=sb(sqa[0].lo + 2048, (512,)))
        assert qT[0].lo + 6144 <= ov and sqa[0].lo + 4096 <= ycat_a.lo + 4096

        def qk_stages(wt, tile, gaincol, dst, dst_res, pzi, scr=None, banks=None):
            scr = scr or scrA
            zsb, knb, t1a, t2a, sqq, rsq = (scr[k_] for k_ in ("zsb", "knb", "t1a", "t2a", "sqq", "rsq"))
            if banks is None:
                pzt = pz[pzi % 2]
                prt = pz[(pzi + 1) % 2]
            else:
                pzt, prt = banks
            ts_ = slice(tile * 512, (tile + 1) * 512)

            def s0():
                for kc in range(8):
                    P.op("pe", MM(pzt[:], wt[:, kc, :], hT[:, kc, ts_], start=(kc == 0), stop=(kc == 7)),
                         reads=[wt.r(), hT.r(kc * S + tile * 512, 512)], writes=[pzt.r()])

            def s1():
                P.op("act", ACT(sqq[:], pzt[:], AF.Square), reads=[pzt.r()], writes=[sqq.r()])
                P.op("dve", CP(zsb[:], pzt[:]), reads=[pzt.r()], writes=[zsb.r()])

            def s2():
                P.op("pe", MM(prt[:], bd64, sqq[:]), reads=[sqq.r(), mats.r()], writes=[prt.r()])

            def s3():
                rstd_from(prt, rsq)

            def s4():
                P.op("dve", STT(knb[:], zsb[:], gaincol, rsq[:], ALU.mult, ALU.mult),
                     reads=[zsb.r(), rsq.r(), dcol.r(), colsT.r()], writes=[knb.r()])

            def s5():
                P.op("pe", MM(prt[:], prot, knb[:]), reads=[knb.r(), cbf.r()], writes=[prt.r()])

            def s6():
                P.op("dve", TT(t1a[:], knb[:], cosT[:, ts_], ALU.mult), reads=[knb.r(), cosT.r()], writes=[t1a.r()])
                P.op("dve", TT(t2a[:], prt[:], sinT[:, ts_], ALU.mult), reads=[prt.r(), sinT.r()], writes=[t2a.r()])
                for (d_ap, d_res, psl) in dst:
                    P.op("dve", TT(d_ap, t1a[psl, :], t2a[psl, :], ALU.add), reads=[t1a.r(), t2a.r()], writes=[d_res])

            return [s0, s1, s2, s3, s4, s5, s6]

        pzi = 0

        def k_path(g, tile, scr, banks):
            tsl = slice(tile * 512, (tile + 1) * 512)
            dsts = [(kTz[g][0][0:64, tsl], kTz[g][0].r(tile * 512, 512), slice(0, 64)),
                    (kTz[g][1][64:128, tsl], kTz[g][1].r(tile * 512, 512), slice(64, 128))]
            return qk_stages(wA[g], tile, cl(l, C_KG), dsts, None, 0, scr=scr, banks=banks)

        for g in range(2):
            for tp_ in range(0, NT, 2):
                A_ = k_path(g, tp_, scrA, (pz[0], pz[1]))
                B_ = k_path(g, tp_ + 1, scrB, (ps(2 * 512, 512), ps(3 * 512, 512)))
                for k in range(len(A_) + 1):
                    if k < len(A_):
                        A_[k]()
                    if k >= 1:
                        B_[k - 1]()
        for tt in range(16):
            pvt = ps((6 + tt % 2) * 512, 128)
            tile = tt // 4
            for kc in range(8):
                P.op("pe", MM(pvt[:], hT[:, kc, tt * 128:(tt + 1) * 128], wA[2][:, kc, :],
                              start=(kc == 0), stop=(kc == 7)),
                     reads=[wA[2].r(), hT.r(kc * S + tile * 512, 512)], writes=[pvt.r()])
            for g in range(2):
                P.op("act", ACT(Vaug[g][0][:, tt, 0:64], pvt[:, g * 64:(g + 1) * 64], AF.Copy),
                     reads=[pvt.r()], writes=[Vaug[g][0].r(tt * 128, 128)])
                P.op("dve", CP(Vaug[g][1][:, tt, 64:128], pvt[:, g * 64:(g + 1) * 64]),
                     reads=[pvt.r()], writes=[Vaug[g][1].r(tt * 128, 128)])

        chk("AK%d" % l)
        wq = [wA[(pc + 3) % 4] for pc in range(4)]
        for pc in range(4):
            load_w("pool", ("wA", (pc + 3) % 4), wq[pc], win_d[l, 20 + pc])

        def make_q(tile, pc):
            nonlocal pzi
            q_ = qT[pc % 2]
            stg = qk_stages(wq[pc], tile, dc(DC_QG8[l]), [(q_[:], q_.r(), slice(0, 128))], None, pzi)
            pzi += 1
            return stg

        for st_ in make_q(0, 0):
            st_()
        woi = 0
        LA = 2
        for tile in range(NT):
            ts_ = slice(tile * 512, (tile + 1) * 512)
            inj = {}

            def add_inj(i, fn):
                inj.setdefault(i, []).append(fn)

            for pc in range(4):
                if pc < 3:
                    nt_, npc = tile, pc + 1
                elif tile + 1 < NT:
                    nt_, npc = tile + 1, 0
                else:
                    continue
                for fn, off in zip(make_q(nt_, npc), (0, 1, 3, 4, 5, 6, 8)):
                    add_inj(pc * 16 + off, fn)

            wslots = [wO[0], wO[1]] + [sb(ov + (5 + k) * 2048, (8, 128), BF16) for k in range(3)]
            wmap = [0, 1, 2, 3, 4, 0, 1, 4]

            def wout_load(m):
                wt = wslots[wmap[m]]
                load_w("pool", ("wO", wmap[m]), wt, wout_d[l, m])
            for m_, at_ in zip(range(5), (40, 44, 48, 52, 56)):
                add_inj(at_, lambda m_=m_: wout_load(m_))
            if tile < 2:
                for q in range(4):
                    add_inj(2 + q * 11, lambda q=q, which=tile: convert_ffn(l, which, q))

            its = [(pc, par, kp) for pc in range(4) for par in range(2) for kp in range(8)]
            N = len(its)

            def head_post(pc, par):
                pr = slice(par * 64, (par + 1) * 64)
                ppv = pPV[par]
                sq_ = sqa[par]

                def p0():
                    P.op("act", ACT(sq_[:], ppv[:], AF.Square), reads=[ppv.r()], writes=[sq_.r()])

                def p1():
                    P.op("pe", MM(pdn[:], wvE if par == 0 else wvO, sq_[:]), reads=[sq_.r(), mats.r()],
                         writes=[pdn.r()])
                    rstd_from(pdn, rsa, use_eps=False)
                    P.op("dve", STT(ycat_a[pr, pc, :], ppv[pr, :], cl(l, C_AOG)[pr, :], rsa[pr, :],
                                    ALU.mult, ALU.mult),
                         reads=[ppv.r(), rsa.r(), colsT.r()], writes=[ycat_a.r(pc * 512, 512)])
                return p0, p1

            for i in range(N + 1):
                for fn in inj.pop(i, ()):
                    fn()
                if i < N:
                    pc, par, kp = its[i]
                    g = pc // 2
                    q_ = qT[pc % 2]
                    pst = pSp[i % 2]
                    pt_ = PT[i % 2]
                    for e in range(2):
                        kc16 = kp * 2 + e
                        P.op("pe", MM(pst[:, e * 512:(e + 1) * 512], kTz[g][par][:, kc16 * 128:(kc16 + 1) * 128], q_[:]),
                             reads=[kTz[g][par].r(kc16 * 128, 128), q_.r()], writes=[pst.r(e * 512, 512)])
                    P.op("act", ACT(pt_[:], pst[:], AF.Exp), reads=[pst.r()], writes=[pt_.r()])
                j = i - 1
                if j >= 0:
                    pc, par, kp = its[j]
                    g = pc // 2
                    pt_ = PT[j % 2]
                    ppv = pPV[par]
                    for e in range(2):
                        kc16 = kp * 2 + e
                        P.op("pe", MM(ppv[:], Vaug[g][par][:, kc16, :], pt_[:, e * 512:(e + 1) * 512],
                                      start=(kc16 == 0), stop=(kc16 == 15)),
                             reads=[Vaug[g][par].r(kc16 * 128, 128), pt_.r(e * 512, 512)], writes=[ppv.r()])
                    if kp == 7:
                        p0, p1 = head_post(pc, par)
                        p0()
                        add_inj(i + 2, p1)
            for i in sorted(inj):
                for fn in inj[i]:
                    fn()
            pend = None
            for m in range(8):
                wt = wslots[wmap[m]]
                py = ps((2 + m % 4) * 512, 512)
                for kc in range(8):
                    if kc < 4:
                        rhs = ycat_h[:, kc, ts_]; rr = ycat_h.r(kc * S + tile * 512, 512)
                    else:
                        rhs = ycat_a[:, kc - 4, :]; rr = ycat_a.r((kc - 4) * 512, 512)
                    P.op("pe", MM(py[:], wt[:, kc, :], rhs, start=(kc == 0), stop=(kc == 7)),
                         reads=[wt.r(), rr], writes=[py.r()])
                if m in (0, 1):
                    wout_load(m + 5)
                elif m == 4:
                    wout_load(7)
                sq_ = sqa[m % 2]
                P.op("dve", CP(ytmp[:, m, :], py[:]), reads=[py.r()], writes=[ytmp.r(m * 512, 512)])
                P.op("act", ACT(sq_[:], py[:], AF.Square), reads=[py.r()], writes=[sq_.r()])
                if pend is not None:
                    pend()

                def pend(m=m, sq_=sq_):
                    P.op("pe", MM(pdn[:], ones1024, sq_[:], start=(m == 0), stop=(m == 7)),
                         reads=[sq_.r(), mats.r()], writes=[pdn.r()])
            pend()
            residual(tile, ytmp, DC_GG1[l], rsa.lo, None, pdn)
            prenorm(tile, DC_G2[l], DC_SH2[l], pnb, 2)

            chk("A%d.%d" % (l, tile))
        chk("A%d" % l)
        o = YH
        h1T = sb(o, (32, 512), BF16); o += 32768
        ytmp = sb(o, (8, 512)); o += 16384
        w1b = [sb(o + i * 4096, (8, 256), BF16) for i in range(4)]; o += 16384
        w2b = [sb(o + i * 4096, (2, 1024), BF16) for i in range(4)]; o += 16384
        rt = [sb(o + i * 2048, (512,)) for i in range(2)]; o += 4096
        sqf = [sb(o + i * 1024, (512,), BF16) for i in range(2)]; o += 2048
        rsf = sb(o, (512,)); o += 2048
        pnb = o; o += 8192
        assert o <= ARENA_BYTES, o
        pbk = [ps(i * 512, 512) for i in range(8)]
        w1i = 0
        w2i = 0
        bki = 0

        def ffn_tail(tile):
            ts_ = slice(tile * 512, (tile + 1) * 512)
            for m in range(8):
                sq_ = sqf[m % 2]
                P.op("act", ACT(sq_[:], ytmp[:, m, :], AF.Square), reads=[ytmp.r(m * 512, 512)], writes=[sq_.r()])
                P.op("pe", MM(pbk[7][:], ones1024, sq_[:], start=(m == 0), stop=(m == 7)),
                     reads=[sq_.r(), mats.r()], writes=[pbk[7].r()])
            residual(tile, ytmp, DC_GG2[l], rsf.lo, None, pbk[7])
            if l + 1 < NL:
                prenorm(tile, DC_G1[l + 1], DC_SH1[l + 1], pnb, 7)
            elif stop is None:
                P.dma_group("sp", ("out", tile),
                            [DMA(out_d[:, m, ts_], xT[:, m, ts_]) for m in range(8)],
                            reads=[xT.r(m * S + tile * 512, 512) for m in range(8)])

        pending = None
        for tile in range(NT):
            ts_ = slice(tile * 512, (tile + 1) * 512)
            for jg in range(8):
              for hh in range(2):
                wt = w1b[w1i % 4]
                src = w1s[l, jg].rearrange("p (k c) -> p k c", c=512)[:, :, hh * 256:(hh + 1) * 256]
                P.dma_group("sp", ("w1", w1i % 4), [DMA(wt[:], src)],
                            reads=[dres(0, l, jg)], writes=[wt.r()])
                w1i += 1
                for j2 in range(2):
                    jj = hh * 2 + j2
                    j = jg * 4 + jj
                    pb = pbk[bki % 7]; bki += 1
                    for kc in range(8):
                        P.op("pe", MM(pb[:], wt[:, kc, j2 * 128:(j2 + 1) * 128], hT[:, kc, ts_],
                                      start=(kc == 0), stop=(kc == 7)),
                             reads=[wt.r(), hT.r(kc * S + tile * 512, 512)], writes=[pb.r()])
                    r_ = rt[j % 2]
                    P.op("act", ACT(r_[:], pb[:], AF.Relu), reads=[pb.r()], writes=[r_.r()])
                    P.op("dve", TT(h1T[:, j, :], r_[:], r_[:], ALU.mult), reads=[r_.r()], writes=[h1T.r(j * 512, 512)])
              if jg == 1 and pending is not None:
                    pending()
                    pending = None
            for jg in range(8):
              for hh in range(2):
                wt = w2b[w2i % 4]
                P.dma_group("sp", ("w2", w2i % 4),
                            [DMA(wt[:].rearrange("p a b -> p (a b)"), w2s[l, jg][:, hh * 2048:(hh + 1) * 2048])],
                            reads=[dres(1, l, jg)], writes=[wt.r()])
                w2i += 1
                for j2 in range(2):
                    j = jg * 4 + hh * 2 + j2
                    for m in range(8):
                        P.op("pe", MM(pbk[m][:], wt[:, j2, m * 128:(m + 1) * 128], h1T[:, j, :],
                                      start=(j == 0), stop=(j == 31)),
                             reads=[wt.r(), h1T.r(j * 512, 512)], writes=[pbk[m].r()])
            for m in range(8):
                if m % 2 == 0:
                    P.op("act", ACT(ytmp[:, m, :], pbk[m][:], AF.Copy), reads=[pbk[m].r()],
                         writes=[ytmp.r(m * 512, 512)])
                else:
                    P.op("dve", CP(ytmp[:, m, :], pbk[m][:]), reads=[pbk[m].r()], writes=[ytmp.r(m * 512, 512)])
            pending = (lambda t_=tile: ffn_tail(t_))
        pending()
        chk("F%d" % l)

    try:
        body()
    except _Stop:
        pass
    fkeys = [("out", t_) for t_ in range(NT)]
    if stop is not None:
        for tile in range(NT):
            ts_ = slice(tile * 512, (tile + 1) * 512)
            P.dma_group("sp", ("out", tile),
                        [DMA(out_d[:, m, ts_], xT[:, m, ts_]) for m in range(8)],
                        reads=[xT.r(m * S + tile * 512, 512) for m in range(8)])
    if dbg:
        P.dma_group("pool", "dbgh", [DMA(dbg_h[:, m, :], hT[:, m, :]) for m in range(8)], reads=[hT.r()])
        P.dma_group("pool", "dbgy", [DMA(dbg_y[:, m, :], ycat_h[:, m, :]) for m in range(4)], reads=[ycat_h.r()])
        P.dma_group("sp", "dbgc", [DMA(dbg_c, dcol[:])], reads=[dcol.r()])
        fkeys += ["dbgh", "dbgy", "dbgc"]
    counts = P.emit(nc, final_wait_keys=fkeys)
    st.close()
    return counts


def _chunk_cols(w):
    n = w.shape[1] // 128
    return np.ascontiguousarray(w.reshape(8, 128, n, 128).transpose(2, 1, 0, 3).reshape(n, 128, 1024))


def _host_layout(x, c, w_ada, ada_layer_bias, hg_lb_logits, pre_mix, post_mix, w_in, w_out,
                 hg_out_gain, q_gain, k_gain, att_out_gain, pre_ff, post_ff, w_ff1, w_ff2):
    f = np.float32
    col8 = lambda v: np.asarray(v, f).reshape(-1, 128).T
    rep2 = lambda v: np.concatenate([np.asarray(v, f), np.asarray(v, f)]).reshape(128, 1)
    cols = []
    for l in range(NL):
        cols += [col8(ada_layer_bias[l]), col8(pre_mix[l]), col8(post_mix[l]), col8(pre_ff[l]), col8(post_ff[l]),
                 col8(hg_lb_logits[l]), col8(hg_out_gain[l]), rep2(q_gain[l]), rep2(k_gain[l]), rep2(att_out_gain[l])]
    cols = np.ascontiguousarray(np.concatenate(cols, axis=1), dtype=f)
    assert cols.shape == (128, NCOLS)
    cbf = np.zeros((128, NCB), f)
    cbf[:, CB_IDENT:CB_IDENT + 128] = np.eye(128, dtype=f)
    prot = np.zeros((128, 128), f)
    for m in range(128):
        if (m % 32) < 16:
            prot[m + 16, m] = -1.0
        else:
            prot[m - 16, m] = 1.0
    cbf[:, CB_PROT:CB_PROT + 128] = prot
    pp = (np.arange(128) % 64)[:, None]
    tt = np.arange(64)[None, :]
    cbf[:, CB_MASKF:CB_MASKF + 64] = (pp <= tt)
    cbf[:, CB_MASKB:CB_MASKB + 64] = (pp >= tt)
    sm = np.ones(1024, f)
    sm[::64] = 0
    cbf[:, CB_SCAN:CB_SCAN + 1024] = sm[None, :]
    half = 32
    inv_freq = (1.0 / (np.float32(10000.0) ** (np.arange(0, half, 2, dtype=f) / f(half)))).astype(f)
    tpos = np.arange(S)
    row = (tpos // 64).astype(f)
    colp = (tpos % 64).astype(f)
    dd = np.arange(128) % 64
    ang = np.where((dd < 32)[:, None], row[None, :], colp[None, :]).astype(f) * inv_freq[dd % 16][:, None]
    ang = ang.astype(f)
    cosT = np.cos(ang).astype(f)
    sinT = np.sin(ang).astype(f)
    wada = np.ascontiguousarray(
        np.asarray(w_ada, f).reshape(8, 128, 48, 128).transpose(2, 1, 0, 3).reshape(48, 128, 1024))
    win = np.zeros((NL, 27, 128, 1024), f)
    for l in range(NL):
        w = np.asarray(w_in[l], f)
        win[l, 0:24] = _chunk_cols(w[:, 0:3072])
        k0 = w[:, 3072:3136]
        k1 = w[:, 3136:3200]
        win[l, 24] = _chunk_cols(np.concatenate([k0, k0], axis=1))[0]
        win[l, 25] = _chunk_cols(np.concatenate([k1, k1], axis=1))[0]
        win[l, 26] = _chunk_cols(w[:, 3200:3328])[0]
    wout = np.stack([_chunk_cols(np.asarray(w_out[l], f)) for l in range(NL)])
    w1 = np.stack([np.asarray(w_ff1[l], f).reshape(8, 128, 8, 512).transpose(2, 1, 0, 3).reshape(8, 128, 4096)
                   for l in range(NL)])
    w2 = np.stack([np.asarray(w_ff2[l], f).reshape(8, 4, 128, 1024).transpose(0, 2, 1, 3).reshape(8, 128, 4096)
                   for l in range(NL)])
    shared = dict(cols=cols, cbf=cbf, cosT=cosT, sinT=sinT, wada=wada, win=win,
                  wout=np.ascontiguousarray(wout), w1=np.ascontiguousarray(w1), w2=np.ascontiguousarray(w2))
    in_maps = []
    xx = np.asarray(x, f)
    cc = np.asarray(c, f)
    for b in range(NCORES):
        xT = np.ascontiguousarray(xx[b].T.reshape(8, 128, S).transpose(1, 0, 2))
        cT = np.ascontiguousarray(cc[b].reshape(8, 128).T)
        m = dict(shared)
        m["xT"] = xT
        m["cT"] = cT
        in_maps.append(m)
    return in_maps


_CACHE = {}


def kernel(**inputs):
    in_maps = _host_layout(**inputs)
    if "nc" not in _CACHE:
        nc = bass.Bass("TRN2", target_bir_lowering=False)
        build(nc)
        _CACHE["nc"] = nc
    nc = _CACHE["nc"]
    res = run_bass_kernel_spmd(nc, in_maps, core_ids=list(range(NCORES)))
    out = np.empty((NCORES, S, D), np.float32)
    for b in range(NCORES):
        oT = res.results[b]["outT"]
        out[b] = oT.transpose(1, 0, 2).reshape(D, S).T
    return out
```

```python
import contextlib
import numpy as np
import concourse.bass as bass
import concourse.mybir as mybir
from concourse.bass_utils import run_bass_kernel_spmd

F32 = mybir.dt.float32
BF16 = mybir.dt.bfloat16
AF = mybir.ActivationFunctionType
ALU = mybir.AluOpType

D = 1024
S = 2048
NL = 2
NT = 4
EPS = 1e-6
NCORES = 8
ENGS = ("pe", "act", "dve", "pool", "sp")


class Op:
    __slots__ = ("eng", "fn", "deps", "signal", "sigval", "dma_sem", "dma_val", "idx")

    def __init__(self, eng, fn):
        self.eng = eng
        self.fn = fn
        self.deps = []
        self.signal = False
        self.sigval = 0
        self.dma_sem = None
        self.dma_val = 0


class Prog:
    def __init__(self):
        self.ops = []
        self.recs = {"sb": [], "dr": [], "ps": [[b * 2048, (b + 1) * 2048, None, []] for b in range(8)]}
        self.dma_counts = {}

    @staticmethod
    def _bank(res):
        sp, lo, hi = res
        if sp != "ps":
            return res
        return (sp, lo // 2048 * 2048, (hi + 2047) // 2048 * 2048)

    def _deps_for(self, op, reads, writes):
        deps = set()
        for res in reads:
            sp, lo, hi = self._bank(res)
            for rec in self.recs[sp]:
                if rec[0] < hi and lo < rec[1]:
                    if rec[2] is not None:
                        deps.add(rec[2])
                    if sp == "ps":
                        for rd in rec[3]:
                            if rd.eng != op.eng:
                                deps.add(rd)
                    rec[3].append(op)
        for res in writes:
            sp, lo, hi = self._bank(res)
            new = []
            for rec in self.recs[sp]:
                if rec[0] < hi and lo < rec[1]:
                    if rec[2] is not None:
                        deps.add(rec[2])
                    deps.update(rec[3])
                    if rec[0] < lo:
                        new.append([rec[0], lo, rec[2], list(rec[3])])
                    if hi < rec[1]:
                        new.append([hi, rec[1], rec[2], list(rec[3])])
                else:
                    new.append(rec)
            new.append([lo, hi, op, []])
            self.recs[sp] = new
        deps.discard(op)
        return deps

    def _add(self, op, reads, writes):
        latest = {}
        for d in self._deps_for(op, reads, writes):
            if d.dma_sem is None:
                if d.eng == op.eng and op.dma_sem is None and d.eng == "pe":
                    continue
                cur = latest.get(d.eng)
                if cur is None or d.idx > cur.idx:
                    latest[d.eng] = d
            else:
                op.deps.append(d)
        for d in latest.values():
            d.signal = True
            op.deps.append(d)
        op.idx = len(self.ops)
        self.ops.append(op)
        return op

    def op(self, eng, fn, reads=(), writes=()):
        return self._add(Op(eng, fn), reads, writes)

    def dma_group(self, eng, key, fns, reads=(), writes=()):
        ops = []
        for i, fn in enumerate(fns):
            o = Op(eng, fn)
            o.dma_sem = key
            self.dma_counts[key] = self.dma_counts.get(key, 0) + 1
            self._add(o, reads, writes if i == 0 else ())
            ops.append(o)
        final = 16 * self.dma_counts[key]
        for o in ops:
            o.dma_val = final
        if len(ops) > 1:
            pass
        return ops

    def emit(self, nc, final_wait_keys=()):
        counts = {e: 0 for e in ENGS}
        for o in self.ops:
            if o.dma_sem is None and o.signal:
                counts[o.eng] += 1
                o.sigval = counts[o.eng]
        with contextlib.ExitStack() as st:
            esem = {e: st.enter_context(nc.semaphore("s_" + e)) for e in ENGS}
            dsem = {k: st.enter_context(nc.semaphore("d_%d" % i))
                    for i, k in enumerate(self.dma_counts)}
            block = st.enter_context(nc.Block())
            per = {e: [o for o in self.ops if o.eng == e] for e in ENGS}

            def run(e, h, final=False):
                seen = {}
                for o in per[e]:
                    for d in o.deps:
                        if d.dma_sem is not None:
                            s, v = dsem[d.dma_sem], d.dma_val
                        else:
                            s, v = esem[d.eng], d.sigval
                        key = id(s)
                        if seen.get(key, 0) < v:
                            h.wait_ge(s, v)
                            seen[key] = v
                    ins = o.fn(h)
                    if o.dma_sem is not None:
                        ins.then_inc(dsem[o.dma_sem], 16)
                    elif o.signal:
                        ins.then_inc(esem[e], 1)
                if final:
                    for k in final_wait_keys:
                        h.wait_ge(dsem[k], 16 * self.dma_counts[k])

            @block.tensor
            def _(h):
                run("pe", h)

            @block.scalar
            def _(h):
                run("act", h)

            @block.vector
            def _(h):
                run("dve", h)

            @block.gpsimd
            def _(h):
                run("pool", h)

            @block.sync
            def _(h):
                run("sp", h, final=True)
        return counts


class T:
    def __init__(self, ap, space, lo, esz, nelem):
        self.ap = ap
        self.space = space
        self.lo = lo
        self.esz = esz
        self.n = nelem

    def r(self, off=0, n=None):
        if n is None:
            n = self.n - off
        return (self.space, self.lo + off * self.esz, self.lo + (off + n) * self.esz)

    def __getitem__(self, idx):
        return self.ap[idx]


def MM(out, lhsT, rhs, start=True, stop=True):
    return lambda h: h.matmul(out, lhsT=lhsT, rhs=rhs, start=start, stop=stop)


def TR(out, in_, ident):
    return lambda h: h.transpose(out=out, in_=in_, identity=ident)


def ACT(out, in_, func, **kw):
    return lambda h: h.activation(out=out, in_=in_, func=func, **kw)


def TT(out, a, b, op):
    return lambda h: h.tensor_tensor(out=out, in0=a, in1=b, op=op)


def TS(out, a, s1, s2, op0, op1=None):
    if op1 is None:
        return lambda h: h.tensor_scalar(out=out, in0=a, scalar1=s1, scalar2=None, op0=op0)
    return lambda h: h.tensor_scalar(out=out, in0=a, scalar1=s1, scalar2=s2, op0=op0, op1=op1)


def STT(out, in0, scalar, in1, op0, op1):
    return lambda h: h.scalar_tensor_tensor(out=out, in0=in0, scalar=scalar, in1=in1, op0=op0, op1=op1)


def CP(out, in_):
    return lambda h: h.tensor_copy(out=out, in_=in_)


def MS(out, val):
    return lambda h: h.memset(out, val)


def DMA(out, in_, **kw):
    return lambda h: h.dma_start(out=out, in_=in_, **kw)


C_ADAB = 0
C_PREMIX = 48
C_POSTMIX = 56
C_PREFF = 64
C_POSTFF = 72
C_LBLOG = 80
C_HGG = 84
C_QG = 85
C_KG = 86
C_AOG = 87
C_PER_LAYER = 88
NCOLS = NL * C_PER_LAYER
CB_IDENT = 0
CB_PROT = 128
CB_MASKF = 256
CB_MASKB = 320
CB_SCAN = 384
NCB = 384 + 1024

ARENA_BYTES = 212480


class _Stop(Exception):
    pass


def build(nc, stop=None, dbg=False):
    P = Prog()
    st = contextlib.ExitStack()

    def chk(name):
        if stop == name:
            raise _Stop()

    def dram(name, shape, kind="ExternalInput"):
        return nc.dram_tensor(name, list(shape), F32, kind=kind).ap()

    xT_d = dram("xT", [128, 8, S])
    cT_d = dram("cT", [128, 8])
    cols_d = dram("cols", [128, NCOLS])
    cbf_d = dram("cbf", [128, NCB])
    cos_d = dram("cosT", [128, S])
    sin_d = dram("sinT", [128, S])
    wada_d = dram("wada", [48, 128, 1024])
    win_d = dram("win", [NL, 27, 128, 1024])
    wout_d = dram("wout", [NL, 8, 128, 1024])
    w1_d = dram("w1", [NL, 8, 128, 4096])
    w2_d = dram("w2", [NL, 8, 128, 4096])
    out_d = dram("outT", [128, 8, S], kind="ExternalOutput")
    w1s = nc.dram_tensor("w1s", [NL, 8, 128, 4096], BF16).ap()
    w2s = nc.dram_tensor("w2s", [NL, 8, 128, 4096], BF16).ap()

    def dres(which, l, jg, n=1):
        base = ((which * NL + l) * 8 + jg) * 4096
        return ("dr", base, base + n * 4096)

    def convert_ffn(l, which, q):
        src, dst = (w1_d, w1s) if which == 0 else (w2_d, w2s)
        fns = [DMA(dst[l, jg].rearrange("p (a b) -> p a b", b=1024),
                   src[l, jg].rearrange("p (a b) -> p a b", b=1024)) for jg in range(q * 2, q * 2 + 2)]
        P.dma_group("pool", ("cv", l, which, q), fns, writes=[dres(which, l, q * 2, 2)])
    if dbg:
        dbg_h = dram("dbg_hT", [128, 8, S], kind="ExternalOutput")
        dbg_y = dram("dbg_ycat", [128, 4, S], kind="ExternalOutput")
        dbg_c = dram("dbg_dcol", [128, 512], kind="ExternalOutput")

    arena = st.enter_context(nc.sbuf_tensor("arena", [128, ARENA_BYTES // 4], F32))
    psum = st.enter_context(nc.psum_tensor("psum", [128, 8, 512], F32))
    psflat = psum[:].rearrange("p a b -> p (a b)")

    def sb(lo, shape, dt=F32):
        esz = 4 if dt == F32 else 2
        n = int(np.prod(shape))
        assert lo % 4 == 0 and lo + n * esz <= ARENA_BYTES, (lo, shape)
        ap = arena[:, lo // 4:(lo + n * esz + 3) // 4]
        if dt != F32:
            ap = ap.bitcast(dt)[:, 0:n]
        if len(shape) == 2:
            ap = ap.rearrange("p (a b) -> p a b", b=shape[1])
        elif len(shape) == 3:
            ap = ap.rearrange("p (a b c) -> p a b c", b=shape[1], c=shape[2])
        return T(ap, "sb", lo, esz, n)

    def ps(off, n, dt=F32):
        ap = psflat[:, off:off + n]
        if dt != F32:
            ap = ap.bitcast(dt)
            return T(ap, "ps", off * 4, 2, n * 2)
        return T(ap, "ps", off * 4, 4, n)

    xT = sb(0, (8, S))
    hT = sb(65536, (8, S), BF16)
    CB = 98304
    colsT = sb(CB, (NCOLS,))
    dcol = sb(CB + 1024, (512,))
    cbf = sb(CB + 3072, (NCB,), BF16)
    mats = sb(CB + 6144, (5, 128), BF16)
    condc = sb(CB + 7680, (48,))
    sc_c = sb(CB + 7936, (8,))
    YH = CB + 12288
    ycat_h = sb(YH, (4, S), BF16)
    PH = YH + 16384

    ident = cbf[:, CB_IDENT:CB_IDENT + 128]
    prot = cbf[:, CB_PROT:CB_PROT + 128]
    maskF = cbf[:, CB_MASKF:CB_MASKF + 64]
    maskB = cbf[:, CB_MASKB:CB_MASKB + 64]
    scanm = cbf[:, CB_SCAN:CB_SCAN + 1024]
    bd64 = mats[:, 0, :]
    ones128 = mats[:, 1, :]
    ones1024 = mats[:, 2, :]
    wvE = mats[:, 3, :]
    wvO = mats[:, 4, :]

    DC_EPS = 0
    DC_G1 = [8 + l * 64 for l in range(NL)]
    DC_SH1 = [16 + l * 64 for l in range(NL)]
    DC_GG1 = [24 + l * 64 for l in range(NL)]
    DC_G2 = [32 + l * 64 for l in range(NL)]
    DC_SH2 = [40 + l * 64 for l in range(NL)]
    DC_GG2 = [48 + l * 64 for l in range(NL)]
    DC_LB = [56 + l * 64 for l in range(NL)]
    DC_OML = [60 + l * 64 for l in range(NL)]
    DC_NOML = [64 + l * 64 for l in range(NL)]
    DC_QG8 = [68 + l * 64 for l in range(NL)]
    DC_TMP = 200

    def dc(i, n=1):
        return dcol[:, i:i + n]

    def cl(l, i, n=1):
        return colsT[:, l * C_PER_LAYER + i:l * C_PER_LAYER + i + n]

    P.dma_group("sp", "cols", [DMA(colsT[:], cols_d)], writes=[colsT.r()])
    P.dma_group("pool", "cbf", [DMA(cbf[:], cbf_d)], writes=[cbf.r()])
    P.op("dve", MS(dcol[:], 0.0), writes=[dcol.r()])
    P.op("dve", MS(dc(DC_EPS), EPS), writes=[dcol.r()])
    P.op("dve", MS(mats[:, 0, :], 0.0), writes=[mats.r()])
    P.op("dve", MS(mats[0:64, 0, 0:64], 1.0 / 64), writes=[mats.r()])
    P.op("dve", MS(mats[64:128, 0, 64:128], 1.0 / 64), writes=[mats.r()])
    P.op("dve", MS(mats[:, 1, :], 1.0 / 128), writes=[mats.r()])
    P.op("dve", MS(mats[:, 2, :], 1.0 / 1024), writes=[mats.r()])
    P.op("dve", MS(mats[0:64, 3, :], 1.0 / 64), writes=[mats.r()])
    P.op("dve", MS(mats[64:128, 3, :], EPS / 64), writes=[mats.r()])
    P.op("dve", MS(mats[0:64, 4, :], EPS / 64), writes=[mats.r()])
    P.op("dve", MS(mats[64:128, 4, :], 1.0 / 64), writes=[mats.r()])

    for kc in range(8):
        P.dma_group("sp", ("x", kc), [DMA(xT[:, kc, :], xT_d[:, kc, :])], writes=[xT.r(kc * S, S)])

    cst = sb(PH, (8,))
    P.dma_group("sp", "cT", [DMA(cst[:], cT_d)], writes=[cst.r()])
    P.op("act", ACT(sc_c[:], cst[:], AF.Silu), reads=[cst.r()], writes=[sc_c.r()])
    wst = [sb(PH + 1024 + i * 16384, (4, 1024), BF16) for i in range(2)]
    sc_b = sb(PH + 512, (8,), BF16)
    P.op("dve", CP(sc_b[:], sc_c[:]), reads=[sc_c.r()], writes=[sc_b.r()])
    pcond = ps(0, 48)
    for g in range(4):
        w = wst[g % 2]
        P.dma_group("pool", ("wada", g % 2), [DMA(w[:], wada_d[g * 4:(g + 1) * 4].rearrange("c p n -> p c n"))],
                    writes=[w.r()])
        for cc in range(4):
            j = g * 4 + cc
            for kc in range(8):
                P.op("pe", MM(pcond[:, j:j + 1], w[:, cc, kc * 128:(kc + 1) * 128], sc_b[:, kc:kc + 1],
                              start=(kc == 0), stop=(kc == 7)),
                     reads=[w.r(), sc_b.r()], writes=[pcond.r()])
    P.op("dve", CP(condc[:, 0:16], pcond[:, 0:16]), reads=[pcond.r()], writes=[condc.r()])
    DC_MODE = 256
    DC_MOD = [288 + l * 48 for l in range(NL)]
    mode = dcol[:, DC_MODE:DC_MODE + 16]
    P.op("dve", TT(mode, condc[:, 0:16], cl(0, C_ADAB, 16), ALU.add),
         reads=[condc.r(), colsT.r()], writes=[dcol.r()])
    P.op("dve", STT(dc(DC_G1[0], 8), mode[:, 8:16], 1.0, cl(0, C_PREMIX, 8), ALU.add, ALU.mult),
         reads=[dcol.r(), colsT.r()], writes=[dcol.r()])
    P.op("dve", CP(dc(DC_SH1[0], 8), mode[:, 0:8]), reads=[dcol.r()], writes=[dcol.r()])
    for l in range(NL):
        P.op("dve", TS(dc(DC_QG8[l]), cl(l, C_QG), 0.125, None, ALU.mult), reads=[colsT.r()], writes=[dcol.r()])

    st2 = [sb(YH + 8192 + i * 2048, (1024,), BF16) for i in range(4)]
    sc_b2 = sb(CB + 7936 + 64, (8,), BF16)
    P.op("dve", CP(sc_b2[:], sc_c[:]), reads=[sc_c.r()], writes=[sc_b2.r()])
    pcond2 = ps(4 * 512, 32)

    def ada_chunk(j):
        w = st2[j % 4]
        P.dma_group("pool", ("wada2", j % 4), [DMA(w[:], wada_d[j])], writes=[w.r()])
        for kc in range(8):
            P.op("pe", MM(pcond2[:, j - 16:j - 15], w[:, kc * 128:(kc + 1) * 128], sc_b2[:, kc:kc + 1],
                          start=(kc == 0), stop=(kc == 7)),
                 reads=[w.r(), sc_b2.r()], writes=[pcond2.r()])

    def ada_late():
        P.op("dve", CP(condc[:, 16:48], pcond2[:]), reads=[pcond2.r()], writes=[condc.r()])
        for l in range(NL):
            modc = dcol[:, DC_MOD[l]:DC_MOD[l] + 48]
            rw = dict(reads=[condc.r(), colsT.r(), dcol.r()], writes=[dcol.r()])
            P.op("dve", TT(modc, condc[:], cl(l, C_ADAB, 48), ALU.add), **rw)
            for (dst, sci, gi) in ((DC_G1[l], 8, C_PREMIX), (DC_G2[l], 32, C_PREFF)):
                if l == 0 and sci == 8:
                    continue
                P.op("dve", STT(dc(dst, 8), modc[:, sci:sci + 8], 1.0, cl(l, gi, 8), ALU.add, ALU.mult), **rw)
            for (dst, si) in ((DC_SH1[l], 0), (DC_SH2[l], 24)):
                if l == 0 and si == 0:
                    continue
                P.op("dve", CP(dc(dst, 8), modc[:, si:si + 8]), **rw)
            for (dst, gi, pi) in ((DC_GG1[l], 16, C_POSTMIX), (DC_GG2[l], 40, C_POSTFF)):
                P.op("dve", TT(dc(dst, 8), modc[:, gi:gi + 8], cl(l, pi, 8), ALU.mult), **rw)

    l0 = cl(0, C_LBLOG, 4)
    l1 = cl(1, C_LBLOG, 4)
    t = lambda i: dcol[:, DC_TMP + 4 * i:DC_TMP + 4 * i + 4]
    rd = [colsT.r(), dcol.r()]
    P.op("dve", TT(t(0), l0, l1, ALU.max), reads=rd, writes=[dcol.r()])
    P.op("dve", TT(t(1), l0, t(0), ALU.subtract), reads=rd, writes=[dcol.r()])
    P.op("dve", TT(t(2), l1, t(0), ALU.subtract), reads=rd, writes=[dcol.r()])
    P.op("act", ACT(t(1), t(1), AF.Exp), reads=rd, writes=[dcol.r()])
    P.op("act", ACT(t(2), t(2), AF.Exp), reads=rd, writes=[dcol.r()])
    P.op("dve", TT(t(3), t(1), t(2), ALU.add), reads=rd, writes=[dcol.r()])
    P.op("dve", lambda h: h.reciprocal(out=t(3), in_=t(3)), reads=rd, writes=[dcol.r()])
    P.op("dve", TT(t(1), t(1), t(3), ALU.mult), reads=rd, writes=[dcol.r()])
    P.op("dve", TT(t(2), t(2), t(3), ALU.mult), reads=rd, writes=[dcol.r()])
    P.op("dve", TT(dc(DC_LB[0], 4), t(1), t(1), ALU.subtract), reads=rd, writes=[dcol.r()])
    P.op("dve", TT(t(0), t(1), t(2), ALU.add), reads=rd, writes=[dcol.r()])
    P.op("dve", TT(dc(DC_LB[1], 4), t(0), t(1), ALU.subtract), reads=rd, writes=[dcol.r()])
    for l in range(NL):
        P.op("dve", TS(dc(DC_OML[l], 4), dc(DC_LB[l], 4), -1.0, 1.0, ALU.mult, ALU.add), reads=rd, writes=[dcol.r()])
        P.op("dve", TS(dc(DC_NOML[l], 4), dc(DC_LB[l], 4), -1.0, None, ALU.add), reads=rd, writes=[dcol.r()])

    eps_c = dc(DC_EPS)

    def rstd_from(den, out_t, use_eps=True):
        if use_eps:
            P.op("act", ACT(out_t[:], den[:], AF.Ln, bias=eps_c), reads=[den.r(), dcol.r()], writes=[out_t.r()])
        else:
            P.op("act", ACT(out_t[:], den[:], AF.Ln), reads=[den.r()], writes=[out_t.r()])
        P.op("act", ACT(out_t[:], out_t[:], AF.Exp, scale=-0.5), reads=[out_t.r()], writes=[out_t.r()])

    def prenorm(tile, gcol0, shcol0, base, den_bank):
        sq = [sb(base + i * 1024, (512,), BF16) for i in range(2)]
        rs = sb(base + 2048, (512,))
        tm = [sb(base + 4096 + i * 2048, (512,)) for i in range(2)]
        den = ps(den_bank * 512, 512)
        for kc in range(8):
            s_ = sq[kc % 2]
            xr = xT.r(kc * S + tile * 512, 512)
            P.op("act", ACT(s_[:], xT[:, kc, tile * 512:(tile + 1) * 512], AF.Square), reads=[xr], writes=[s_.r()])
            P.op("pe", MM(den[:], ones1024, s_[:], start=(kc == 0), stop=(kc == 7)),
                 reads=[s_.r(), mats.r()], writes=[den.r()])
        rstd_from(den, rs)
        for kc in range(8):
            t_ = tm[kc % 2]
            xr = xT.r(kc * S + tile * 512, 512)
            hr = hT.r(kc * S + tile * 512, 512)
            P.op("dve", TT(t_[:], xT[:, kc, tile * 512:(tile + 1) * 512], rs[:], ALU.mult),
                 reads=[xr, rs.r()], writes=[t_.r()])
            P.op("dve", TS(hT[:, kc, tile * 512:(tile + 1) * 512], t_[:], dc(gcol0 + kc), dc(shcol0 + kc),
                           ALU.mult, ALU.add),
                 reads=[t_.r(), dcol.r()], writes=[hr])

    def residual(tile, ytmp, ggcol0, base, den_bank, sq_from_ytmp_done_den=None):
        rs = sb(base, (512,))
        rstd_from(sq_from_ytmp_done_den, rs)
        for m in range(8):
            yr = ytmp.r(m * 512, 512)
            xr = xT.r(m * S + tile * 512, 512)
            P.op("dve", TT(ytmp[:, m, :], ytmp[:, m, :], rs[:], ALU.mult), reads=[yr, rs.r()], writes=[yr])
            P.op("dve", STT(xT[:, m, tile * 512:(tile + 1) * 512], ytmp[:, m, :], dc(ggcol0 + m),
                            xT[:, m, tile * 512:(tile + 1) * 512], ALU.mult, ALU.add),
                 reads=[yr, xr, dcol.r()], writes=[xr])

    def load_w(eng, key, dst, src, reads=()):
        flat = dst.ap
        if len(dst.ap.shape) == 3:
            flat = dst.ap.rearrange("p a b -> p (a b)")
        n = flat.shape[1]
        if n > 1024:
            flat = flat.rearrange("p (a b) -> p a b", b=1024)
            src = src.rearrange("p (a b) -> p a b", b=1024)
        P.dma_group(eng, key, [DMA(flat, src)], writes=[dst.r()])

    def body():
      chk("setup")
      for tile in range(NT):
        prenorm(tile, DC_G1[0], DC_SH1[0], PH + 45056, 1)
      chk("pre0")
      main()

    ada_next = [16]

    def main():
     for l in range(NL):
        o = PH
        QmT = [sb(o + d * 4096, (S,), BF16) for d in range(2)]; o += 8192
        KmT = [sb(o + d * 4096, (S,), BF16) for d in range(2)]; o += 8192
        Kmt = [sb(o + d * 4096, (16, 128), BF16) for d in range(2)]; o += 8192
        Vtok = sb(o, (16, 128), BF16); o += 4096
        gs = sb(o, (S,), BF16); o += 4096
        qs = sb(o, (S,)); o += 8192
        osum = qs
        TS_ = []
        for d in range(2):
            TS_.append([sb(o + i * 4096, (1024,)) for i in range(3)]); o += 12288
        T1, T2, T3 = TS_[0]
        KdTs = [sb(o + d * 2048, (1024,), BF16) for d in range(2)]; o += 4096
        em = [sb(o + d * 128, (32,)) for d in range(2)]; o += 256
        ea = [sb(o + d * 128, (32,)) for d in range(2)]; o += 256
        ec = [sb(o + d * 128, (32,)) for d in range(2)]; o += 256
        dls = [sb(o + d * 64, (16,)) for d in range(2)]; o += 128
        Sst = [sb(o + d * 512, (128,)) for d in range(2)]; o += 1024
        Ub = [[sb(o + (d * 2 + b) * 256, (128,), BF16) for b in range(2)] for d in range(2)]; o += 1024
        ATb = [[sb(o + (d * 2 + b) * 128, (64,), BF16) for b in range(2)] for d in range(2)]; o += 512
        KAb = [[sb(o + (d * 2 + b) * 128, (64,), BF16) for b in range(2)] for d in range(2)]; o += 512
        KBb = [[sb(o + (d * 2 + b) * 128, (64,), BF16) for b in range(2)] for d in range(2)]; o += 512
        o = (o + 1023) // 1024 * 1024
        wH = [[sb(o + c * 2048, (8, 128), BF16) for c in range(5)]] * 2; o += 10240
        assert o <= ARENA_BYTES, o
        pbuf = [ps(0, 1024), ps(1024, 1024)]
        pch = [dict(sc=[ps(0, 64), ps(512, 64)], o=ps(4 * 512, 64), dS=ps(6 * 512, 128)),
               dict(sc=[ps(2 * 512, 64), ps(3 * 512, 64)], o=ps(5 * 512, 64), dS=ps(7 * 512, 128))]
        ptr = [ps(6 * 512, 512, BF16), ps(7 * 512, 512, BF16)]
        pv = [ps((4 + i) * 512, 128) for i in range(4)]
        pden = ps(0, 512)

        def load_head_w(h):
            for ci, chunk in enumerate((h, 4 + h, 8 + h, 12 + h, 16 + h)):
                load_w("pool", ("wH", h % 2, ci), wH[h % 2][ci], win_d[l, chunk])

        def proj_fm(wt, half, pb):
            for t2 in range(2):
                tile = half * 2 + t2
                for kc in range(8):
                    P.op("pe", MM(pb[:, t2 * 512:(t2 + 1) * 512], wt[:, kc, :], hT[:, kc, tile * 512:(tile + 1) * 512],
                                  start=(kc == 0), stop=(kc == 7)),
                         reads=[wt.r(), hT.r(kc * S + tile * 512, 512)], writes=[pb.r(t2 * 512, 512)])

        load_head_w(0)
        for d in range(2):
            for b in range(2):
                P.op("dve", MS(KAb[d][b][:], 0.0), writes=[KAb[d][b].r()])
                P.op("dve", MS(KBb[d][b][:], 0.0), writes=[KBb[d][b].r()])
        pbi = 0
        for h in range(4):
            wq, wff, wfb, wi, wg = wH[h % 2]
            lbc = dc(DC_LB[l] + h)
            omlc = dc(DC_OML[l] + h)
            nomlc = dc(DC_NOML[l] + h)
            for half in range(2):
                pb = pbuf[pbi % 2]; pbi += 1
                proj_fm(wq, half, pb)
                P.op("act", ACT(qs[:, half * 1024:(half + 1) * 1024], pb[:], AF.Silu),
                     reads=[pb.r()], writes=[qs.r(half * 1024, 1024)])
            for half in range(2):
                pb = pbuf[pbi % 2]; pbi += 1
                proj_fm(wg, half, pb)
                P.op("act", ACT(gs[:, half * 1024:(half + 1) * 1024], pb[:], AF.Silu),
                     reads=[pb.r()], writes=[gs.r(half * 1024, 1024)])
            chk("H%d.%dq" % (l, h))
            for tt in range(16):
                pvt = pv[tt % 4]
                tile = tt // 4
                for kc in range(8):
                    P.op("pe", MM(pvt[:], hT[:, kc, tt * 128:(tt + 1) * 128], wi[:, kc, :],
                                  start=(kc == 0), stop=(kc == 7)),
                         reads=[wi.r(), hT.r(kc * S + tile * 512, 512)], writes=[pvt.r()])
                eng = "act" if tt % 2 == 0 else "dve"
                fn = ACT(Vtok[:, tt, :], pvt[:], AF.Copy) if eng == "act" else CP(Vtok[:, tt, :], pvt[:])
                P.op(eng, fn, reads=[pvt.r()], writes=[Vtok.r(tt * 128, 128)])
            chk("H%d.%dv" % (l, h))
            def gate_stages(d, half):
                wf = wff if d == 0 else wfb
                mi = 31 if d == 0 else 32
                li = 63 if d == 0 else 0
                T1, T2, T3 = TS_[d]
                KdT = KdTs[d]
                dl = dls[d]
                pb = pbuf[d]
                ptrd = ptr[d]
                hs = slice(half * 1024, (half + 1) * 1024)
                b3 = T3[:].rearrange("p (c j) -> p c j", j=64)
                a3 = T1[:].rearrange("p (c j) -> p c j", j=64)
                cs = slice(half * 16, (half + 1) * 16)
                k3 = KmT[d][:, hs].rearrange("p (c j) -> p c j", j=64)
                kd3 = KdT[:].rearrange("p (c j) -> p c j", j=64)
                L = []
                L.append(lambda: proj_fm(wf, half, pb))
                L.append(lambda: P.op("act", ACT(T1[:], pb[:], AF.Sigmoid), reads=[pb.r()], writes=[T1.r()]))

                def s_k():
                    P.op("dve", TS(T2[:], T1[:], nomlc, omlc, ALU.mult, ALU.add),
                         reads=[T1.r(), dcol.r()], writes=[T2.r()])
                    P.op("act", ACT(T1[:], T1[:], AF.Ln, scale=omlc, bias=lbc),
                         reads=[T1.r(), dcol.r()], writes=[T1.r()])
                L.append(s_k)

                def s_scan():
                    if d == 0:
                        P.op("dve", lambda hh: hh.tensor_tensor_scan(out=T3[:], data0=scanm, data1=T1[:], initial=0.0,
                                                                     op0=ALU.mult, op1=ALU.add),
                             reads=[T1.r(), cbf.r()], writes=[T3.r()])
                    else:
                        P.op("dve", lambda hh: hh.tensor_tensor_scan(out=T3[:, ::-1], data0=scanm, data1=T1[:, ::-1],
                                                                     initial=0.0, op0=ALU.mult, op1=ALU.add),
                             reads=[T1.r(), cbf.r()], writes=[T3.r()])
                L.append(s_scan)

                def s_small():
                    P.op("act", ACT(em[d][:, cs], b3[:, :, mi], AF.Exp), reads=[T3.r()], writes=[em[d].r()])
                    P.op("act", ACT(ea[d][:, cs], b3[:, :, li], AF.Exp), reads=[T3.r()], writes=[ea[d].r()])
                    P.op("dve", TT(dl[:], b3[:, :, li], b3[:, :, mi], ALU.subtract), reads=[T3.r()], writes=[dl.r()])
                    P.op("dve", TT(a3, b3, b3[:, :, mi:mi + 1].to_broadcast([128, 16, 64]), ALU.subtract),
                         reads=[T3.r()], writes=[T1.r()])
                    P.op("dve", TS(T1[:], T1[:], 80.0, -80.0, ALU.min, ALU.max), reads=[T1.r()], writes=[T1.r()])
                    P.op("act", ACT(ec[d][:, cs], dl[:], AF.Exp), reads=[dl.r()], writes=[ec[d].r()])
                L.append(s_small)

                def s_e1():
                    P.op("act", ACT(T3[:], T1[:], AF.Exp), reads=[T1.r()], writes=[T3.r()])
                L.append(s_e1)

                def s_q():
                    P.op("dve", TT(QmT[d][:, hs], qs[:, hs], T3[:], ALU.mult),
                         reads=[qs.r(half * 1024, 1024), T3.r()], writes=[QmT[d].r(half * 1024, 1024)])
                    P.op("act", ACT(T1[:], T1[:], AF.Exp, scale=-1.0), reads=[T1.r()], writes=[T1.r()])
                L.append(s_q)

                def s_km():
                    P.op("dve", TT(KmT[d][:, hs], T2[:], T1[:], ALU.mult),
                         reads=[T2.r(), T1.r()], writes=[KmT[d].r(half * 1024, 1024)])
                    P.op("dve", TT(kd3, k3, ec[d][:, cs].unsqueeze(2).to_broadcast([128, 16, 64]), ALU.mult),
                         reads=[KmT[d].r(half * 1024, 1024), ec[d].r()], writes=[KdT.r()])
                L.append(s_km)

                def s_tr(q4):
                    def f():
                        for t4 in range(4):
                            tt = q4 * 4 + t4
                            P.op("pe", TR(ptrd[:, t4 * 128:(t4 + 1) * 128], KdT[:, tt * 128:(tt + 1) * 128], ident),
                                 reads=[KdT.r(), cbf.r()], writes=[ptrd.r(t4 * 128, 128)])
                        g0 = half * 8 + q4 * 4
                        P.op("act", ACT(Kmt[d][:, g0:g0 + 4, :].rearrange("p a b -> p (a b)"),
                                        ptrd[:, 0:512], AF.Copy),
                             reads=[ptrd.r(0, 512)], writes=[Kmt[d].r(g0 * 128, 512)])
                    return f
                L.append(s_tr(0))
                L.append(s_tr(1))
                return L

            for half in range(2):
                A_ = gate_stages(0, half)
                B_ = gate_stages(1, half)
                for k in range(len(A_)):
                    A_[k]()
                    B_[k]()
                    if l == 0 and h == 0:
                        for _ in range(2 if ada_next[0] < 16 + 20 else 1):
                            if ada_next[0] < 48:
                                ada_chunk(ada_next[0])
                                ada_next[0] += 1
            if l == 0 and h == 0:
                while ada_next[0] < 48:
                    ada_chunk(ada_next[0])
                    ada_next[0] += 1
                ada_late()
            if h + 1 < 4:
                load_head_w(h + 1)
            chk("H%d.%dg1" % (l, h))
            for d in range(2):
                P.op("dve", MS(Sst[d][:], 0.0), writes=[Sst[d].r()])
                P.op("dve", MS(Ub[d][0][:], 0.0), writes=[Ub[d][0].r()])
            def chain_step(s_, d):
                c = s_ if d == 0 else 31 - s_
                par = (c % 2) * 64
                tl = c // 2
                pr = slice(par, par + 64)
                ck = slice(c * 64, (c + 1) * 64)
                psc = pch[d]["sc"][s_ % 2]
                po = pch[d]["o"]
                pdS = pch[d]["dS"]
                at = ATb[d][s_ % 2]
                ucur = Ub[d][s_ % 2]
                unxt = Ub[d][(s_ + 1) % 2]
                msk = maskF if d == 0 else maskB
                ka = KAb[d][s_ % 2]
                kb = KBb[d][s_ % 2]
                cA = slice(c * 64, c * 64 + 32)
                cB = slice(c * 64 + 32, c * 64 + 64)

                def kakb():
                    P.op("pool", CP(ka[:, 0:32], KmT[d][:, cA]), reads=[KmT[d].r(c * 64, 64)], writes=[ka.r()])
                    P.op("pool", CP(kb[:, 32:64], KmT[d][:, cB]), reads=[KmT[d].r(c * 64, 64)], writes=[kb.r()])

                def sc():
                    if d == 0:
                        P.op("pe", MM(psc[pr, :], ka[:], QmT[d][:, ck], start=True, stop=False),
                             reads=[ka.r(), QmT[d].r(c * 64, 64)], writes=[psc.r()])
                        P.op("pe", MM(psc[pr, 32:64], kb[:], QmT[d][:, cB], start=False, stop=True),
                             reads=[kb.r(), QmT[d].r(c * 64, 64)], writes=[psc.r()])
                    else:
                        P.op("pe", MM(psc[pr, :], kb[:], QmT[d][:, ck], start=True, stop=False),
                             reads=[kb.r(), QmT[d].r(c * 64, 64)], writes=[psc.r()])
                        P.op("pe", MM(psc[pr, 0:32], ka[:], QmT[d][:, cA], start=False, stop=True),
                             reads=[ka.r(), QmT[d].r(c * 64, 64)], writes=[psc.r()])

                def mask():
                    P.op("dve", TT(at[pr, :], psc[pr, :], msk[pr, :], ALU.mult),
                         reads=[psc.r(), cbf.r()], writes=[at.r()])

                def om():
                    P.op("pe", MM(po[:], Vtok[pr, tl, :], at[pr, :], start=True, stop=False),
                         reads=[Vtok.r(tl * 128, 128), at.r()], writes=[po.r()])
                    P.op("pe", MM(po[:], ucur[:], QmT[d][:, ck], start=False, stop=True),
                         reads=[ucur.r(), QmT[d].r(c * 64, 64)], writes=[po.r()])
                    first = (d == 0 and c <= 15) or (d == 1 and c >= 16)
                    if first:
                        P.op("act", ACT(osum[:, ck], po[:], AF.Copy), reads=[po.r()], writes=[osum.r(c * 64, 64)])
                    else:
                        P.op("dve", TT(osum[:, ck], po[:], osum[:, ck], ALU.add),
                             reads=[po.r(), osum.r(c * 64, 64)], writes=[osum.r(c * 64, 64)])

                def dS():
                    if s_ < 31:
                        P.op("pe", MM(pdS[:], Kmt[d][pr, tl, :], Vtok[pr, tl, :]),
                             reads=[Kmt[d].r(tl * 128, 128), Vtok.r(tl * 128, 128)], writes=[pdS.r()])

                def upd():
                    if s_ < 31:
                        cn = c + 1 if d == 0 else c - 1
                        P.op("dve", STT(Sst[d][:], Sst[d][:], ea[d][:, c:c + 1], pdS[:], ALU.mult, ALU.add),
                             reads=[Sst[d].r(), ea[d].r(), pdS.r()], writes=[Sst[d].r()])
                        P.op("act", ACT(unxt[:], Sst[d][:], AF.Identity, scale=em[d][:, cn:cn + 1]),
                             reads=[Sst[d].r(), em[d].r()], writes=[unxt.r()])

                return dict(kakb=kakb, sc=sc, mask=mask, om=om, dS=dS, upd=upd)

            steps = [[chain_step(s_, d) for d in range(2)] for s_ in range(32)]
            for it in range(-2, 32):
                for d in range(2):
                    if 0 <= it + 2 < 32:
                        steps[it + 2][d]["kakb"]()
                for d in range(2):
                    if 0 <= it + 1 < 32:
                        steps[it + 1][d]["sc"]()
                for d in range(2):
                    if 0 <= it < 32:
                        steps[it][d]["dS"]()
                for d in range(2):
                    if 0 <= it + 1 < 32:
                        steps[it + 1][d]["mask"]()
                for d in range(2):
                    if 0 <= it < 32:
                        steps[it][d]["upd"]()
                for d in range(2):
                    if 0 <= it < 32:
                        steps[it][d]["om"]()
            chk("H%d.%dc" % (l, h))
            base_ = TS_[0][0].lo
            sqbs = [sb(base_ + t_ * 1024, (512,), BF16) for t_ in range(4)]
            rss = [sb(base_ + 4096 + t_ * 2048, (512,)) for t_ in range(4)]
            tms = [sb(base_ + 12288 + t_ * 2048, (512,)) for t_ in range(4)]
            pdens = [ps(t_ * 512, 512) for t_ in range(4)]
            for tile in range(NT):
                ts_ = slice(tile * 512, (tile + 1) * 512)
                P.op("act", ACT(sqbs[tile][:], osum[:, ts_], AF.Square), reads=[osum.r(tile * 512, 512)],
                     writes=[sqbs[tile].r()])
            for tile in range(NT):
                P.op("pe", MM(pdens[tile][:], ones128, sqbs[tile][:]), reads=[sqbs[tile].r(), mats.r()],
                     writes=[pdens[tile].r()])
            for tile in range(NT):
                P.op("act", ACT(rss[tile][:], pdens[tile][:], AF.Ln, bias=eps_c), reads=[pdens[tile].r(), dcol.r()],
                     writes=[rss[tile].r()])
            for tile in range(NT):
                P.op("act", ACT(rss[tile][:], rss[tile][:], AF.Exp, scale=-0.5), reads=[rss[tile].r()],
                     writes=[rss[tile].r()])
            for tile in range(NT):
                ts_ = slice(tile * 512, (tile + 1) * 512)
                P.op("dve", TT(tms[tile][:], osum[:, ts_], rss[tile][:], ALU.mult),
                     reads=[osum.r(tile * 512, 512), rss[tile].r()], writes=[tms[tile].r()])
                P.op("dve", STT(ycat_h[:, h, ts_], tms[tile][:], cl(l, C_HGG), gs[:, ts_], ALU.mult, ALU.mult),
                     reads=[tms[tile].r(), colsT.r(), gs.r(tile * 512, 512)],
                     writes=[ycat_h.r(h * S + tile * 512, 512)])

            chk("H%d.%d" % (l, h))
        chk("H%d" % l)
        o = PH
        kTz = [[sb(o + (g * 2 + e) * 4096, (S,), BF16) for e in range(2)] for g in range(2)]; o += 16384
        Vaug = [[sb(o + (g * 2 + e) * 4096, (16, 128), BF16) for e in range(2)] for g in range(2)]; o += 16384
        cosT = sb(o, (S,), BF16); o += 4096
        sinT = sb(o, (S,), BF16); o += 4096
        wA = [sb(o + i * 2048, (8, 128), BF16) for i in range(4)]; o += 8192
        wO = [sb(o + i * 2048, (8, 128), BF16) for i in range(2)]; o += 4096
        qT = [sb(o + i * 1024, (512,), BF16) for i in range(2)]; o += 2048
        PT = [sb(o + i * 2048, (1024,), BF16) for i in range(2)]; o += 4096
        ov = o
        zsb = sb(o, (512,)); o += 2048
        knb = sb(o, (512,), BF16); o += 1024
        t1a = sb(o, (512,)); o += 2048
        t2a = sb(o, (512,)); o += 2048
        sqq = sb(o, (512,), BF16); o += 1024
        rsq = sb(o, (512,)); o += 2048
        assert o - ov <= 16384
        ytmp = sb(ov, (8, 512))
        pnb = ov
        o = ov + 16384
        sqa = [sb(o + i * 1024, (512,), BF16) for i in range(2)]; o += 2048
        rsa = sb(o, (512,)); o += 2048
        ycat_a = sb(o, (4, 512), BF16); o += 4096
        assert o <= ARENA_BYTES, o
        pz = [ps(0, 512), ps(512, 512)]
        pdn = pz[0]
        pSp = [ps(2 * 512, 1024), ps(4 * 512, 1024)]
        pPV = [ps((6 + i) * 512, 512) for i in range(2)]

        P.dma_group("pool", "cos", [DMA(cosT[:].rearrange("p (a b) -> p a b", b=1024),
                                        cos_d.rearrange("p (a b) -> p a b", b=1024))], writes=[cosT.r()])
        P.dma_group("pool", "sin", [DMA(sinT[:].rearrange("p (a b) -> p a b", b=1024),
                                        sin_d.rearrange("p (a b) -> p a b", b=1024))], writes=[sinT.r()])
        for g in range(2):
            P.op("dve", MS(kTz[g][0][64:128, :], 0.0), writes=[kTz[g][0].r()])
            P.op("dve", MS(kTz[g][1][0:64, :], 0.0), writes=[kTz[g][1].r()])
        load_w("pool", ("wA", 0), wA[0], win_d[l, 24])
        load_w("pool", ("wA", 1), wA[1], win_d[l, 25])
        load_w("pool", ("wA", 2), wA[2], win_d[l, 26])
        for g in range(2):
            P.op("dve", MS(Vaug[g][0][:, :, 64:128], 1.0), writes=[Vaug[g][0].r()])
            P.op("dve", MS(Vaug[g][1][:, :, 0:64], 1.0), writes=[Vaug[g][1].r()])

        scrA = dict(zsb=zsb, knb=knb, t1a=t1a, t2a=t2a, sqq=sqq, rsq=rsq)
        scrB = dict(zsb=sb(qT[0].lo, (512,)), t1a=sb(qT[0].lo + 2048, (512,)), t2a=sb(qT[0].lo + 4096, (512,)),
                    knb=sb(sqa[0].lo, (512,), BF16), sqq=sb(sqa[0].lo + 1024, (512,), BF16),
                    rsq=sb(sqa[0].lo + 2048, (512,)))
        assert qT[0].lo + 6144 <= ov and sqa[0].lo + 4096 <= ycat_a.lo + 4096

        def qk_stages(wt, tile, gaincol, dst, dst_res, pzi, scr=None, banks=None):
            scr = scr or scrA
            zsb, knb, t1a, t2a, sqq, rsq = (scr[k_] for k_ in ("zsb", "knb", "t1a", "t2a", "sqq", "rsq"))
            if banks is None:
                pzt = pz[pzi % 2]
                prt = pz[(pzi + 1) % 2]
            else:
                pzt, prt = banks
            ts_ = slice(tile * 512, (tile + 1) * 512)

            def s0():
                for kc in range(8):
                    P.op("pe", MM(pzt[:], wt[:, kc, :], hT[:, kc, ts_], start=(kc == 0), stop=(kc == 7)),
                         reads=[wt.r(), hT.r(kc * S + tile * 512, 512)], writes=[pzt.r()])

            def s1():
                P.op("act", ACT(sqq[:], pzt[:], AF.Square), reads=[pzt.r()], writes=[sqq.r()])
                P.op("dve", CP(zsb[:], pzt[:]), reads=[pzt.r()], writes=[zsb.r()])

            def s2():
                P.op("pe", MM(prt[:], bd64, sqq[:]), reads=[sqq.r(), mats.r()], writes=[prt.r()])

            def s3():
                rstd_from(prt, rsq)

            def s4():
                P.op("dve", STT(knb[:], zsb[:], gaincol, rsq[:], ALU.mult, ALU.mult),
                     reads=[zsb.r(), rsq.r(), dcol.r(), colsT.r()], writes=[knb.r()])

            def s5():
                P.op("pe", MM(prt[:], prot, knb[:]), reads=[knb.r(), cbf.r()], writes=[prt.r()])

            def s6():
                P.op("dve", TT(t1a[:], knb[:], cosT[:, ts_], ALU.mult), reads=[knb.r(), cosT.r()], writes=[t1a.r()])
                P.op("dve", TT(t2a[:], prt[:], sinT[:, ts_], ALU.mult), reads=[prt.r(), sinT.r()], writes=[t2a.r()])
                for (d_ap, d_res, psl) in dst:
                    P.op("dve", TT(d_ap, t1a[psl, :], t2a[psl, :], ALU.add), reads=[t1a.r(), t2a.r()], writes=[d_res])

            return [s0, s1, s2, s3, s4, s5, s6]

        pzi = 0

        def k_path(g, tile, scr, banks):
            tsl = slice(tile * 512, (tile + 1) * 512)
            dsts = [(kTz[g][0][0:64, tsl], kTz[g][0].r(tile * 512, 512), slice(0, 64)),
                    (kTz[g][1][64:128, tsl], kTz[g][1].r(tile * 512, 512), slice(64, 128))]
            return qk_stages(wA[g], tile, cl(l, C_KG), dsts, None, 0, scr=scr, banks=banks)

        for g in range(2):
            for tp_ in range(0, NT, 2):
                A_ = k_path(g, tp_, scrA, (pz[0], pz[1]))
                B_ = k_path(g, tp_ + 1, scrB, (ps(2 * 512, 512), ps(3 * 512, 512)))
                for k in range(len(A_) + 1):
                    if k < len(A_):
                        A_[k]()
                    if k >= 1:
                        B_[k - 1]()
        for tt in range(16):
            pvt = ps((6 + tt % 2) * 512, 128)
            tile = tt // 4
            for kc in range(8):
                P.op("pe", MM(pvt[:], hT[:, kc, tt * 128:(tt + 1) * 128], wA[2][:, kc, :],
                              start=(kc == 0), stop=(kc == 7)),
                     reads=[wA[2].r(), hT.r(kc * S + tile * 512, 512)], writes=[pvt.r()])
            for g in range(2):
                P.op("act", ACT(Vaug[g][0][:, tt, 0:64], pvt[:, g * 64:(g + 1) * 64], AF.Copy),
                     reads=[pvt.r()], writes=[Vaug[g][0].r(tt * 128, 128)])
                P.op("dve", CP(Vaug[g][1][:, tt, 64:128], pvt[:, g * 64:(g + 1) * 64]),
                     reads=[pvt.r()], writes=[Vaug[g][1].r(tt * 128, 128)])

        chk("AK%d" % l)
        wq = [wA[(pc + 3) % 4] for pc in range(4)]
        for pc in range(4):
            load_w("pool", ("wA", (pc + 3) % 4), wq[pc], win_d[l, 20 + pc])

        def make_q(tile, pc):
            nonlocal pzi
            q_ = qT[pc % 2]
            stg = qk_stages(wq[pc], tile, dc(DC_QG8[l]), [(q_[:], q_.r(), slice(0, 128))], None, pzi)
            pzi += 1
            return stg

        for st_ in make_q(0, 0):
            st_()
        woi = 0
        LA = 2
        for tile in range(NT):
            ts_ = slice(tile * 512, (tile + 1) * 512)
            inj = {}

            def add_inj(i, fn):
                inj.setdefault(i, []).append(fn)

            for pc in range(4):
                if pc < 3:
                    nt_, npc = tile, pc + 1
                elif tile + 1 < NT:
                    nt_, npc = tile + 1, 0
                else:
                    continue
                for fn, off in zip(make_q(nt_, npc), (0, 1, 3, 4, 5, 6, 8)):
                    add_inj(pc * 16 + off, fn)

            wslots = [wO[0], wO[1]] + [sb(ov + (5 + k) * 2048, (8, 128), BF16) for k in range(3)]
            wmap = [0, 1, 2, 3, 4, 0, 1, 4]

            def wout_load(m):
                wt = wslots[wmap[m]]
                load_w("pool", ("wO", wmap[m]), wt, wout_d[l, m])
            for m_, at_ in zip(range(5), (40, 44, 48, 52, 56)):
                add_inj(at_, lambda m_=m_: wout_load(m_))
            if tile < 2:
                for q in range(4):
                    add_inj(2 + q * 11, lambda q=q, which=tile: convert_ffn(l, which, q))

            its = [(pc, par, kp) for pc in range(4) for par in range(2) for kp in range(8)]
            N = len(its)

            def head_post(pc, par):
                pr = slice(par * 64, (par + 1) * 64)
                ppv = pPV[par]
                sq_ = sqa[par]

                def p0():
                    P.op("act", ACT(sq_[:], ppv[:], AF.Square), reads=[ppv.r()], writes=[sq_.r()])

                def p1():
                    P.op("pe", MM(pdn[:], wvE if par == 0 else wvO, sq_[:]), reads=[sq_.r(), mats.r()],
                         writes=[pdn.r()])
                    rstd_from(pdn, rsa, use_eps=False)
                    P.op("dve", STT(ycat_a[pr, pc, :], ppv[pr, :], cl(l, C_AOG)[pr, :], rsa[pr, :],
                                    ALU.mult, ALU.mult),
                         reads=[ppv.r(), rsa.r(), colsT.r()], writes=[ycat_a.r(pc * 512, 512)])
                return p0, p1

            for i in range(N + 1):
                for fn in inj.pop(i, ()):
                    fn()
                if i < N:
                    pc, par, kp = its[i]
                    g = pc // 2
                    q_ = qT[pc % 2]
                    pst = pSp[i % 2]
                    pt_ = PT[i % 2]
                    for e in range(2):
                        kc16 = kp * 2 + e
                        P.op("pe", MM(pst[:, e * 512:(e + 1) * 512], kTz[g][par][:, kc16 * 128:(kc16 + 1) * 128], q_[:]),
                             reads=[kTz[g][par].r(kc16 * 128, 128), q_.r()], writes=[pst.r(e * 512, 512)])
                    P.op("act", ACT(pt_[:], pst[:], AF.Exp), reads=[pst.r()], writes=[pt_.r()])
                j = i - 1
                if j >= 0:
                    pc, par, kp = its[j]
                    g = pc // 2
                    pt_ = PT[j % 2]
                    ppv = pPV[par]
                    for e in range(2):
                        kc16 = kp * 2 + e
                        P.op("pe", MM(ppv[:], Vaug[g][par][:, kc16, :], pt_[:, e * 512:(e + 1) * 512],
                                      start=(kc16 == 0), stop=(kc16 == 15)),
                             reads=[Vaug[g][par].r(kc16 * 128, 128), pt_.r(e * 512, 512)], writes=[ppv.r()])
                    if kp == 7:
                        p0, p1 = head_post(pc, par)
                        p0()
                        add_inj(i + 2, p1)
            for i in sorted(inj):
                for fn in inj[i]:
                    fn()
            pend = None
            for m in range(8):
                wt = wslots[wmap[m]]
                py = ps((2 + m % 4) * 512, 512)
                for kc in range(8):
                    if kc < 4:
                        rhs = ycat_h[:, kc, ts_]; rr = ycat_h.r(kc * S + tile * 512, 512)
                    else:
                        rhs = ycat_a[:, kc - 4, :]; rr = ycat_a.r((kc - 4) * 512, 512)
                    P.op("pe", MM(py[:], wt[:, kc, :], rhs, start=(kc == 0), stop=(kc == 7)),
                         reads=[wt.r(), rr], writes=[py.r()])
                if m in (0, 1):
                    wout_load(m + 5)
                elif m == 4:
                    wout_load(7)
                sq_ = sqa[m % 2]
                P.op("dve", CP(ytmp[:, m, :], py[:]), reads=[py.r()], writes=[ytmp.r(m * 512, 512)])
                P.op("act", ACT(sq_[:], py[:], AF.Square), reads=[py.r()], writes=[sq_.r()])
                if pend is not None:
                    pend()

                def pend(m=m, sq_=sq_):
                    P.op("pe", MM(pdn[:], ones1024, sq_[:], start=(m == 0), stop=(m == 7)),
                         reads=[sq_.r(), mats.r()], writes=[pdn.r()])
            pend()
            residual(tile, ytmp, DC_GG1[l], rsa.lo, None, pdn)
            prenorm(tile, DC_G2[l], DC_SH2[l], pnb, 2)

            chk("A%d.%d" % (l, tile))
        chk("A%d" % l)
        o = YH
        h1T = sb(o, (32, 512), BF16); o += 32768
        ytmp = sb(o, (8, 512)); o += 16384
        w1b = [sb(o + i * 8192, (8, 512), BF16) for i in range(2)]; o += 16384
        w2b = [sb(o + i * 8192, (4, 1024), BF16) for i in range(2)]; o += 16384
        rt = [sb(o + i * 2048, (512,)) for i in range(2)]; o += 4096
        sqf = [sb(o + i * 1024, (512,), BF16) for i in range(2)]; o += 2048
        rsf = sb(o, (512,)); o += 2048
        pnb = o; o += 8192
        assert o <= ARENA_BYTES, o
        pbk = [ps(i * 512, 512) for i in range(8)]
        w1i = 0
        w2i = 0
        bki = 0

        def ffn_tail(tile):
            ts_ = slice(tile * 512, (tile + 1) * 512)
            for m in range(8):
                sq_ = sqf[m % 2]
                P.op("act", ACT(sq_[:], ytmp[:, m, :], AF.Square), reads=[ytmp.r(m * 512, 512)], writes=[sq_.r()])
                P.op("pe", MM(pbk[7][:], ones1024, sq_[:], start=(m == 0), stop=(m == 7)),
                     reads=[sq_.r(), mats.r()], writes=[pbk[7].r()])
            residual(tile, ytmp, DC_GG2[l], rsf.lo, None, pbk[7])
            if l + 1 < NL:
                prenorm(tile, DC_G1[l + 1], DC_SH1[l + 1], pnb, 7)
            elif stop is None:
                P.dma_group("sp", ("out", tile),
                            [DMA(out_d[:, m, ts_], xT[:, m, ts_]) for m in range(8)],
                            reads=[xT.r(m * S + tile * 512, 512) for m in range(8)])

        pending = None
        for tile in range(NT):
            ts_ = slice(tile * 512, (tile + 1) * 512)
            for jg in range(8):
                wt = w1b[w1i % 2]
                P.dma_group("sp", ("w1", w1i % 2),
                            [DMA(wt[:].rearrange("p a b -> p (a b)"), w1s[l, jg])],
                            reads=[dres(0, l, jg)], writes=[wt.r()])
                w1i += 1
                for jj in range(4):
                    j = jg * 4 + jj
                    pb = pbk[bki % 7]; bki += 1
                    for kc in range(8):
                        P.op("pe", MM(pb[:], wt[:, kc, jj * 128:(jj + 1) * 128], hT[:, kc, ts_],
                                      start=(kc == 0), stop=(kc == 7)),
                             reads=[wt.r(), hT.r(kc * S + tile * 512, 512)], writes=[pb.r()])
                    r_ = rt[j % 2]
                    P.op("act", ACT(r_[:], pb[:], AF.Relu), reads=[pb.r()], writes=[r_.r()])
                    P.op("dve", TT(h1T[:, j, :], r_[:], r_[:], ALU.mult), reads=[r_.r()], writes=[h1T.r(j * 512, 512)])
                if jg == 1 and pending is not None:
                    pending()
                    pending = None
            for jg in range(8):
                wt = w2b[w2i % 2]
                P.dma_group("sp", ("w2", w2i % 2),
                            [DMA(wt[:].rearrange("p a b -> p (a b)"), w2s[l, jg])],
                            reads=[dres(1, l, jg)], writes=[wt.r()])
                w2i += 1
                for jj in range(4):
                    j = jg * 4 + jj
                    for m in range(8):
                        P.op("pe", MM(pbk[m][:], wt[:, jj, m * 128:(m + 1) * 128], h1T[:, j, :],
                                      start=(j == 0), stop=(j == 31)),
                             reads=[wt.r(), h1T.r(j * 512, 512)], writes=[pbk[m].r()])
            for m in range(8):
                if m % 2 == 0:
                    P.op("act", ACT(ytmp[:, m, :], pbk[m][:], AF.Copy), reads=[pbk[m].r()],
                         writes=[ytmp.r(m * 512, 512)])
                else:
                    P.op("dve", CP(ytmp[:, m, :], pbk[m][:]), reads=[pbk[m].r()], writes=[ytmp.r(m * 512, 512)])
            pending = (lambda t_=tile: ffn_tail(t_))
        pending()
        chk("F%d" % l)

    try:
        body()
    except _Stop:
        pass
    fkeys = [("out", t_) for t_ in range(NT)]
    if stop is not None:
        for tile in range(NT):
            ts_ = slice(tile * 512, (tile + 1) * 512)
            P.dma_group("sp", ("out", tile),
                        [DMA(out_d[:, m, ts_], xT[:, m, ts_]) for m in range(8)],
                        reads=[xT.r(m * S + tile * 512, 512) for m in range(8)])
    if dbg:
        P.dma_group("pool", "dbgh", [DMA(dbg_h[:, m, :], hT[:, m, :]) for m in range(8)], reads=[hT.r()])
        P.dma_group("pool", "dbgy", [DMA(dbg_y[:, m, :], ycat_h[:, m, :]) for m in range(4)], reads=[ycat_h.r()])
        P.dma_group("sp", "dbgc", [DMA(dbg_c, dcol[:])], reads=[dcol.r()])
        fkeys += ["dbgh", "dbgy", "dbgc"]
    counts = P.emit(nc, final_wait_keys=fkeys)
    st.close()
    return counts


def _chunk_cols(w):
    n = w.shape[1] // 128
    return np.ascontiguousarray(w.reshape(8, 128, n, 128).transpose(2, 1, 0, 3).reshape(n, 128, 1024))


def _host_layout(x, c, w_ada, ada_layer_bias, hg_lb_logits, pre_mix, post_mix, w_in, w_out,
                 hg_out_gain, q_gain, k_gain, att_out_gain, pre_ff, post_ff, w_ff1, w_ff2):
    f = np.float32
    col8 = lambda v: np.asarray(v, f).reshape(-1, 128).T
    rep2 = lambda v: np.concatenate([np.asarray(v, f), np.asarray(v, f)]).reshape(128, 1)
    cols = []
    for l in range(NL):
        cols += [col8(ada_layer_bias[l]), col8(pre_mix[l]), col8(post_mix[l]), col8(pre_ff[l]), col8(post_ff[l]),
                 col8(hg_lb_logits[l]), col8(hg_out_gain[l]), rep2(q_gain[l]), rep2(k_gain[l]), rep2(att_out_gain[l])]
    cols = np.ascontiguousarray(np.concatenate(cols, axis=1), dtype=f)
    assert cols.shape == (128, NCOLS)
    cbf = np.zeros((128, NCB), f)
    cbf[:, CB_IDENT:CB_IDENT + 128] = np.eye(128, dtype=f)
    prot = np.zeros((128, 128), f)
    for m in range(128):
        if (m % 32) < 16:
            prot[m + 16, m] = -1.0
        else:
            prot[m - 16, m] = 1.0
    cbf[:, CB_PROT:CB_PROT + 128] = prot
    pp = (np.arange(128) % 64)[:, None]
    tt = np.arange(64)[None, :]
    cbf[:, CB_MASKF:CB_MASKF + 64] = (pp <= tt)
    cbf[:, CB_MASKB:CB_MASKB + 64] = (pp >= tt)
    sm = np.ones(1024, f)
    sm[::64] = 0
    cbf[:, CB_SCAN:CB_SCAN + 1024] = sm[None, :]
    half = 32
    inv_freq = (1.0 / (np.float32(10000.0) ** (np.arange(0, half, 2, dtype=f) / f(half)))).astype(f)
    tpos = np.arange(S)
    row = (tpos // 64).astype(f)
    colp = (tpos % 64).astype(f)
    dd = np.arange(128) % 64
    ang = np.where((dd < 32)[:, None], row[None, :], colp[None, :]).astype(f) * inv_freq[dd % 16][:, None]
    ang = ang.astype(f)
    cosT = np.cos(ang).astype(f)
    sinT = np.sin(ang).astype(f)
    wada = np.ascontiguousarray(
        np.asarray(w_ada, f).reshape(8, 128, 48, 128).transpose(2, 1, 0, 3).reshape(48, 128, 1024))
    win = np.zeros((NL, 27, 128, 1024), f)
    for l in range(NL):
        w = np.asarray(w_in[l], f)
        win[l, 0:24] = _chunk_cols(w[:, 0:3072])
        k0 = w[:, 3072:3136]
        k1 = w[:, 3136:3200]
        win[l, 24] = _chunk_cols(np.concatenate([k0, k0], axis=1))[0]
        win[l, 25] = _chunk_cols(np.concatenate([k1, k1], axis=1))[0]
        win[l, 26] = _chunk_cols(w[:, 3200:3328])[0]
    wout = np.stack([_chunk_cols(np.asarray(w_out[l], f)) for l in range(NL)])
    w1 = np.stack([np.asarray(w_ff1[l], f).reshape(8, 128, 8, 512).transpose(2, 1, 0, 3).reshape(8, 128, 4096)
                   for l in range(NL)])
    w2 = np.stack([np.asarray(w_ff2[l], f).reshape(8, 4, 128, 1024).transpose(0, 2, 1, 3).reshape(8, 128, 4096)
                   for l in range(NL)])
    shared = dict(cols=cols, cbf=cbf, cosT=cosT, sinT=sinT, wada=wada, win=win,
                  wout=np.ascontiguousarray(wout), w1=np.ascontiguousarray(w1), w2=np.ascontiguousarray(w2))
    in_maps = []
    xx = np.asarray(x, f)
    cc = np.asarray(c, f)
    for b in range(NCORES):
        xT = np.ascontiguousarray(xx[b].T.reshape(8, 128, S).transpose(1, 0, 2))
        cT = np.ascontiguousarray(cc[b].reshape(8, 128).T)
        m = dict(shared)
        m["xT"] = xT
        m["cT"] = cT
        in_maps.append(m)
    return in_maps


_CACHE = {}


def kernel(**inputs):
    in_maps = _host_layout(**inputs)
    if "nc" not in _CACHE:
        nc = bass.Bass("TRN2", target_bir_lowering=False)
        build(nc)
        _CACHE["nc"] = nc
    nc = _CACHE["nc"]
    res = run_bass_kernel_spmd(nc, in_maps, core_ids=list(range(NCORES)))
    out = np.empty((NCORES, S, D), np.float32)
    for b in range(NCORES):
        oT = res.results[b]["outT"]
        out[b] = oT.transpose(1, 0, 2).reshape(D, S).T
    return out
```
